# Optimizing a Trainium2 kernel written in Bass

```python
import math
import jax, jax.numpy as jnp
from jax import lax
import numpy as np

D_MODEL = 1024
BATCH = 8
SEQ = 2048
DEPTH = 4
DEC_BATCH = 128
DEC_SEQ = 1
PAST_LEN = 16384
PAGE_SIZE = 128

CONF_CH = D_MODEL // 2
CONF_KW = 31
N_HEADS = 4
HEAD_K = 128
HEAD_V = 128
QK_DIM = N_HEADS * HEAD_K
V_DIM = N_HEADS * HEAD_V
QKV_DIM = 2 * QK_DIM + V_DIM
SHORT_KW = 4
CHUNK = 64
D_FF = -(-(8 * D_MODEL) // (3 * 256)) * 256
RMS_EPS = 1e-6
LN_EPS = 1e-5
O_GLU_A = CONF_CH
O_GLU_B = 2 * CONF_CH
O_QKV = O_GLU_B + QKV_DIM
O_Z = O_QKV + V_DIM
O_BETA = O_Z + N_HEADS
O_ALPHA = O_BETA + N_HEADS
O_MERGE_A = O_ALPHA + D_MODEL
IN_DIM = O_MERGE_A + D_MODEL

kernel_name = "hybrid_conformer_gdn_adaln_step"


def rms_norm(x, g, eps=RMS_EPS):
    xf = x.astype(jnp.float32)
    y = xf * lax.rsqrt(jnp.mean(xf * xf, axis=-1, keepdims=True) + eps)
    return (y * g.astype(jnp.float32)).astype(x.dtype)


def layer_norm(x, g, b, eps=LN_EPS):
    xf = x.astype(jnp.float32)
    mu = jnp.mean(xf, axis=-1, keepdims=True)
    var = jnp.mean(jnp.square(xf - mu), axis=-1, keepdims=True)
    y = (xf - mu) * lax.rsqrt(var + eps)
    return (y * g.astype(jnp.float32) + b.astype(jnp.float32)).astype(x.dtype)


def l2_normalize(x, eps=1e-6):
    return x * lax.rsqrt(jnp.sum(x * x, axis=-1, keepdims=True) + eps)


def causal_depthwise_conv(x, buf, w):
    width = w.shape[0]
    xp = jnp.concatenate([buf.astype(x.dtype), x], axis=1)
    y = lax.conv_general_dilated(xp, w[:, None, :].astype(x.dtype), window_strides=(1,), padding="VALID",
                                 dimension_numbers=("NWC", "WIO", "NWC"), feature_group_count=x.shape[-1])
    return y, xp[:, xp.shape[1] - (width - 1):]


def gated_delta_rule(q, k, v, beta, g, s0):
    bsz, t_len = q.shape[0], q.shape[1]
    c = min(CHUNK, t_len)
    n_chunks = -(-t_len // c)
    pad = n_chunks * c - t_len

    def prep(a):
        a = a.astype(jnp.float32)
        a = jnp.pad(a, [(0, 0), (0, pad)] + [(0, 0)] * (a.ndim - 2))
        a = a.reshape((bsz, n_chunks, c) + a.shape[2:])
        perm = (1, 0, 3, 2) + tuple(range(4, a.ndim))
        return jnp.transpose(a, perm)

    qc, kc, vc, bc, gc = prep(q), prep(k), prep(v), prep(beta), prep(g)
    gc = jnp.cumsum(gc, axis=-1)
    causal = jnp.tril(jnp.ones((c, c), dtype=bool))
    strict = jnp.tril(jnp.ones((c, c), dtype=bool), k=-1)
    eye = jnp.eye(c, dtype=jnp.float32)

    def step(state, inp):
        q_i, k_i, v_i, b_i, g_i = inp
        diff = g_i[..., :, None] - g_i[..., None, :]
        decay = jnp.where(causal, jnp.exp(jnp.where(causal, diff, 0.0)), 0.0)
        kb = k_i * b_i[..., None]
        kk = jnp.einsum("bhik,bhjk->bhij", kb, k_i)
        a_mat = eye + jnp.where(strict, kk * decay, 0.0)
        rhs = jnp.concatenate([v_i * b_i[..., None], kb * jnp.exp(g_i)[..., None]], axis=-1)
        sol = lax.linalg.triangular_solve(a_mat, rhs, left_side=True, lower=True, unit_diagonal=True)
        u, w = sol[..., :HEAD_V], sol[..., HEAD_V:]
        v_new = u - jnp.einsum("bhck,bhkv->bhcv", w, state)
        qk = jnp.einsum("bhik,bhjk->bhij", q_i, k_i) * decay
        o = (jnp.einsum("bhck,bhkv->bhcv", q_i * jnp.exp(g_i)[..., None], state)
             + jnp.einsum("bhij,bhjv->bhiv", qk, v_new))
        g_last = g_i[..., -1]
        k_dec = k_i * jnp.exp(g_last[..., None] - g_i)[..., None]
        state = state * jnp.exp(g_last)[..., None, None] + jnp.einsum("bhck,bhcv->bhkv", k_dec, v_new)
        return state, o

    s_final, o = lax.scan(step, s0.astype(jnp.float32), (qc, kc, vc, bc, gc))
    o = jnp.transpose(o, (1, 0, 3, 2, 4)).reshape(bsz, n_chunks * c, N_HEADS, HEAD_V)[:, :t_len]
    return o, s_final


def hybrid_layer(x, c, conf_buf, short_buf, s0, w_ada, b_ada, norm1_g, w_in, conf_dw_w, conf_dw_b,
                 conf_ln_g, conf_ln_b, w_conf_out, short_conv_w, a_log, dt_bias, delta_norm_g,
                 w_delta_out, w_merge_out, norm2_g, w_ffn_in, w_ffn_out):
    bsz, t_len, _ = x.shape
    mod = (jax.nn.silu(c) @ w_ada + b_ada)[:, None, :]
    sh1, sc1, gt1, sh2, sc2, gt2 = jnp.split(mod, 6, axis=-1)
    h = rms_norm(x, norm1_g) * (1 + sc1) + sh1
    proj = h @ w_in
    glu_a, glu_b, qkv, z, b_raw, a_raw, m_a, m_b = jnp.split(
        proj, [O_GLU_A, O_GLU_B, O_QKV, O_Z, O_BETA, O_ALPHA, O_MERGE_A], axis=-1)
    u = glu_a * jax.nn.sigmoid(glu_b)
    ca, conf_buf_new = causal_depthwise_conv(u, conf_buf, conf_dw_w)
    ca = jax.nn.silu(layer_norm(ca + conf_dw_b, conf_ln_g, conf_ln_b))
    y_a = ca @ w_conf_out
    qkv_c, short_buf_new = causal_depthwise_conv(qkv, short_buf, short_conv_w)
    qkv_c = jax.nn.silu(qkv_c).astype(jnp.float32)
    q, k, v = jnp.split(qkv_c, [QK_DIM, 2 * QK_DIM], axis=-1)
    q = l2_normalize(q.reshape(bsz, t_len, N_HEADS, HEAD_K)) * (HEAD_K ** -0.5)
    k = l2_normalize(k.reshape(bsz, t_len, N_HEADS, HEAD_K))
    v = v.reshape(bsz, t_len, N_HEADS, HEAD_V)
    beta = jax.nn.sigmoid(b_raw.astype(jnp.float32))
    g = -jnp.exp(a_log.astype(jnp.float32)) * jax.nn.softplus(a_raw.astype(jnp.float32) + dt_bias.astype(jnp.float32))
    o, s_new = gated_delta_rule(q, k, v, beta, g, s0)
    zf = z.astype(jnp.float32).reshape(bsz, t_len, N_HEADS, HEAD_V)
    o = (o * lax.rsqrt(jnp.mean(o * o, axis=-1, keepdims=True) + RMS_EPS)
         * delta_norm_g.astype(jnp.float32) * jax.nn.silu(zf))
    y_b = o.reshape(bsz, t_len, V_DIM).astype(x.dtype) @ w_delta_out
    merged = jax.nn.sigmoid(m_a) * y_a + jax.nn.sigmoid(m_b) * y_b
    x = x + gt1 * (merged @ w_merge_out)
    h2 = rms_norm(x, norm2_g) * (1 + sc2) + sh2
    gate, up = jnp.split(h2 @ w_ffn_in, 2, axis=-1)
    x = x + gt2 * ((jax.nn.silu(gate) * up) @ w_ffn_out)
    return x, conf_buf_new, short_buf_new, s_new


def setup_inputs(seed: int = 0) -> dict:
    key = jax.random.key(seed)
    ks = jax.random.split(key, 32)
    D = D_MODEL

    def nrm(k, shape, scale):
        return jax.random.normal(k, shape, jnp.float32) * scale

    dt = jnp.exp(jax.random.uniform(ks[17], (DEPTH, N_HEADS), jnp.float32, math.log(1e-3), math.log(1e-1)))
    return {
        "x_prompt": nrm(ks[0], (BATCH, SEQ, D), 1.0),
        "x_sample": nrm(ks[1], (DEC_BATCH, DEC_SEQ, D), 1.0),
        "c_prompt": nrm(ks[2], (BATCH, D), 1.0),
        "c_sample": nrm(ks[3], (DEC_BATCH, D), 1.0),
        "state_conformer_conv": nrm(ks[4], (DEPTH, DEC_BATCH, CONF_KW - 1, CONF_CH), 0.5),
        "state_short_conv": nrm(ks[5], (DEPTH, DEC_BATCH, SHORT_KW - 1, QKV_DIM), 1.0),
        "state_delta": nrm(ks[6], (DEPTH, DEC_BATCH, N_HEADS, HEAD_K, HEAD_V), 0.05),
        "w_ada": nrm(ks[7], (DEPTH, D, 6 * D), 0.5 * D ** -0.5),
        "b_ada": nrm(ks[8], (DEPTH, 6 * D), 0.01),
        "norm1_g": 1.0 + nrm(ks[9], (DEPTH, D), 0.02),
        "w_in": nrm(ks[10], (DEPTH, D, IN_DIM), D ** -0.5),
        "conf_dw_w": nrm(ks[11], (DEPTH, CONF_KW, CONF_CH), CONF_KW ** -0.5),
        "conf_dw_b": nrm(ks[12], (DEPTH, CONF_CH), 0.01),
        "conf_ln_g": 1.0 + nrm(ks[13], (DEPTH, CONF_CH), 0.02),
        "conf_ln_b": nrm(ks[14], (DEPTH, CONF_CH), 0.01),
        "w_conf_out": nrm(ks[15], (DEPTH, CONF_CH, D), CONF_CH ** -0.5),
        "short_conv_w": nrm(ks[16], (DEPTH, SHORT_KW, QKV_DIM), SHORT_KW ** -0.5),
        "a_log": jnp.log(jax.random.uniform(ks[18], (DEPTH, N_HEADS), jnp.float32, 1.0, 16.0)),
        "dt_bias": dt + jnp.log(-jnp.expm1(-dt)),
        "delta_norm_g": 1.0 + nrm(ks[19], (DEPTH, HEAD_V), 0.02),
        "w_delta_out": nrm(ks[20], (DEPTH, V_DIM, D), V_DIM ** -0.5),
        "w_merge_out": nrm(ks[21], (DEPTH, D, D), D ** -0.5),
        "norm2_g": 1.0 + nrm(ks[22], (DEPTH, D), 0.02),
        "w_ffn_in": nrm(ks[23], (DEPTH, D, 2 * D_FF), D ** -0.5),
        "w_ffn_out": nrm(ks[24], (DEPTH, D_FF, D), D_FF ** -0.5),
        "final_norm_g": 1.0 + nrm(ks[25], (D,), 0.02),
    }


def reference(x_prompt, x_sample, c_prompt, c_sample, state_conformer_conv, state_short_conv, state_delta,
              w_ada, b_ada, norm1_g, w_in, conf_dw_w, conf_dw_b, conf_ln_g, conf_ln_b, w_conf_out,
              short_conv_w, a_log, dt_bias, delta_norm_g, w_delta_out, w_merge_out, norm2_g,
              w_ffn_in, w_ffn_out, final_norm_g):
    bp = x_prompt.shape[0]
    xp, xs = x_prompt, x_sample
    conf_p, conf_s, short_p, short_s, delta_p, delta_s = [], [], [], [], [], []
    for l in range(DEPTH):
        weights = (w_ada[l], b_ada[l], norm1_g[l], w_in[l], conf_dw_w[l], conf_dw_b[l], conf_ln_g[l],
                   conf_ln_b[l], w_conf_out[l], short_conv_w[l], a_log[l], dt_bias[l], delta_norm_g[l],
                   w_delta_out[l], w_merge_out[l], norm2_g[l], w_ffn_in[l], w_ffn_out[l])
        xp, cb, sb, ds = hybrid_layer(
            xp, c_prompt,
            jnp.zeros((bp, CONF_KW - 1, CONF_CH), x_prompt.dtype),
            jnp.zeros((bp, SHORT_KW - 1, QKV_DIM), x_prompt.dtype),
            jnp.zeros((bp, N_HEADS, HEAD_K, HEAD_V), jnp.float32),
            *weights)
        conf_p.append(cb)
        short_p.append(sb)
        delta_p.append(ds)
        xs, cb, sb, ds = hybrid_layer(
            xs, c_sample, state_conformer_conv[l], state_short_conv[l], state_delta[l], *weights)
        conf_s.append(cb)
        short_s.append(sb)
        delta_s.append(ds)
    y_prompt = rms_norm(xp, final_norm_g)
    y_sample = rms_norm(xs, final_norm_g)
    return (y_prompt, y_sample, jnp.stack(conf_p), jnp.stack(conf_s), jnp.stack(short_p),
            jnp.stack(short_s), jnp.stack(delta_p), jnp.stack(delta_s))
```

```python
import numpy as np
from contextlib import ExitStack
import concourse.bass as bass
import concourse.mybir as mybir
from concourse.bass_utils import run_bass_kernel_spmd

F32 = mybir.dt.float32
BF16 = mybir.dt.bfloat16
AF = mybir.ActivationFunctionType
ALU = mybir.AluOpType
AX = mybir.AxisListType

DEPTH = 4
D = 1024
KD = 8
SEQ = 2048
NS = 16
NTOK = SEQ + NS
CONF_CH = 512
CONF_KW = 31
D_FF = 2816
KF = 22
O_GLU_B = 512
O_Q = 1024
O_K = 1536
O_V = 2048
O_Z = 2560
O_BETA = 3072
O_MA = 3080
O_MB = 4104
IN_DIM = 5128
NEG = -1.0e6

SAME_ENGINE_SYNC = True
HL_ENG = "pool"
GCH = 3
STAG = 19
N_LAYERS = DEPTH

PV_G1 = 0
PV_G2 = 8
PV_BADA = 16
PV_CW = 64
PV_CB = 188
PV_LNG = 192
PV_LNB = 196
PV_SW = 200
PV_DNG = 248
PV_ALOG = 249
PV_DTB = 281
PV_FG = 313
PV_N = 321


class Buf:
    __slots__ = ("name", "w", "r", "psum")

    def __init__(self, name="", psum=False):
        self.name = name
        self.w = None
        self.r = {}
        self.psum = psum


class T:
    __slots__ = ("ap", "bufs")

    def __init__(self, ap, bufs=None):
        self.ap = ap
        if bufs is None:
            bufs = [Buf()]
        elif isinstance(bufs, Buf):
            bufs = [bufs]
        self.bufs = bufs

    def __getitem__(self, idx):
        return T(self.ap[idx], self.bufs)

    def v(self, fn):
        return T(fn(self.ap), self.bufs)


def _ap(x):
    return x.ap if isinstance(x, T) else x


def _bufs(*xs):
    out = []
    for x in xs:
        if isinstance(x, T):
            out.extend(x.bufs)
    return out


class Ring:
    def __init__(self, tiles):
        self.tiles = tiles
        self.i = 0

    def next(self):
        t = self.tiles[self.i % len(self.tiles)]
        self.i += 1
        return t


class Prog:
    ENGS = ("pe", "act", "dve", "pool", "sp")

    def __init__(self):
        self.q = {n: [] for n in self.ENGS}
        self.cnt = {}
        self.seen = {n: {} for n in self.ENGS}
        self.nops = 0

    def op(self, e, fn, reads=(), writes=(), sem=None, inc=1):
        deps = {}
        for b in reads:
            if b.w is not None:
                if deps.get(b.w[0], 0) < b.w[1]:
                    deps[b.w[0]] = b.w[1]
            if b.psum:
                for s, v in b.r.items():
                    if s != e and deps.get(s, 0) < v:
                        deps[s] = v
        for b in writes:
            if b.w is not None:
                if deps.get(b.w[0], 0) < b.w[1]:
                    deps[b.w[0]] = b.w[1]
            for s, v in b.r.items():
                if deps.get(s, 0) < v:
                    deps[s] = v
        seen = self.seen[e]
        q = self.q[e]
        for s, v in deps.items():
            if s == e and (e == "pe" or not SAME_ENGINE_SYNC):
                continue
            if seen.get(s, 0) >= v:
                continue
            seen[s] = v
            q.append(("wait", s, v))
        sname = sem or e
        val = self.cnt.get(sname, 0) + inc
        self.cnt[sname] = val
        q.append(("op", fn, sname, inc))
        for b in reads:
            if b.r.get(sname, 0) < val:
                b.r[sname] = val
        for b in writes:
            b.w = (sname, val)
            b.r = {}
        self.nops += 1
        return val

    def MM(self, out, lhsT, rhs, start=True, stop=True):
        o, l, r = _ap(out), _ap(lhsT), _ap(rhs)
        self.op("pe", lambda e: e.matmul(o, lhsT=l, rhs=r, start=start, stop=stop), _bufs(lhsT, rhs), _bufs(out))

    def TR(self, out, in_, ident):
        o, i, d = _ap(out), _ap(in_), _ap(ident)
        self.op("pe", lambda e: e.transpose(o, i, d), _bufs(in_, ident), _bufs(out))

    def ACT(self, out, in_, func, bias=None, scale=None, accum=None):
        o, i = _ap(out), _ap(in_)
        kw = {}
        if bias is None and isinstance(scale, T):
            bias = self.zeroc[0:o.shape[0], :] if o.shape[0] != 128 else self.zeroc
        if bias is not None:
            kw["bias"] = _ap(bias)
        if scale is not None:
            kw["scale"] = _ap(scale)
        if accum is not None:
            kw["accum_out"] = _ap(accum)
        self.op("act", lambda e: e.activation(out=o, in_=i, func=func, **kw), _bufs(in_, bias, scale), _bufs(out, accum))

    def TT(self, out, in0, in1, op, eng="dve"):
        o, a, b = _ap(out), _ap(in0), _ap(in1)
        self.op(eng, lambda e: e.tensor_tensor(out=o, in0=a, in1=b, op=op), _bufs(in0, in1), _bufs(out))

    def TS(self, out, in0, s1, op0, s2=None, op1=None, eng="dve"):
        o, a, x1, x2 = _ap(out), _ap(in0), _ap(s1), _ap(s2)
        if op1 is None:
            self.op(eng, lambda e: e.tensor_scalar(out=o, in0=a, scalar1=x1, scalar2=None, op0=op0), _bufs(in0, s1), _bufs(out))
        else:
            self.op(eng, lambda e: e.tensor_scalar(out=o, in0=a, scalar1=x1, scalar2=x2, op0=op0, op1=op1), _bufs(in0, s1, s2), _bufs(out))

    def STT(self, out, in0, scalar, in1, op0, op1, eng="dve"):
        o, a, s, b = _ap(out), _ap(in0), _ap(scalar), _ap(in1)
        self.op(eng, lambda e: e.scalar_tensor_tensor(out=o, in0=a, scalar=s, in1=b, op0=op0, op1=op1), _bufs(in0, scalar, in1), _bufs(out))

    def CP(self, out, in_, eng="dve"):
        o, i = _ap(out), _ap(in_)
        self.op(eng, lambda e: e.tensor_copy(out=o, in_=i), _bufs(in_), _bufs(out))

    def RED(self, out, in_, op=ALU.add, eng="dve"):
        o, i = _ap(out), _ap(in_)
        self.op(eng, lambda e: e.tensor_reduce(out=o, in_=i, axis=AX.X, op=op), _bufs(in_), _bufs(out))

    def RECIP(self, out, in_):
        o, i = _ap(out), _ap(in_)
        self.op("dve", lambda e: e.reciprocal(out=o, in_=i), _bufs(in_), _bufs(out))

    def MEMSET(self, out, val, eng="dve"):
        o = _ap(out)
        self.op(eng, lambda e: e.memset(o, val), (), _bufs(out))

    def DMA(self, q, out, in_, sem, **kw):
        o, i = _ap(out), _ap(in_)
        self.op(q, lambda e: e.dma_start(out=o, in_=i, **kw), _bufs(in_), _bufs(out), sem=sem, inc=16)


class WStream:
    NSLOT = 6
    LAG = 4
    ELEMS = 2048

    def __init__(self):
        self.specs = []
        self.recording = True

    def begin(self, P, slot_aps):
        self.P = P
        self.i = 0
        self.loaded = 0
        self.slots = [T(a) for a in slot_aps]

    def _view(self, j):
        _, kc, n = self.specs[j]
        t = self.slots[j % self.NSLOT]
        return T(t.ap[:, 0:kc * n].rearrange("p (k n) -> p k n", k=kc), t.bufs)

    def get(self, dram_ap, kc, n):
        assert kc * n <= self.ELEMS
        i = self.i
        self.i += 1
        if self.recording:
            self.specs.append((dram_ap, kc, n))
            return self._view(i)
        while self.loaded < len(self.specs) and (self.loaded < self.NSLOT or self.loaded <= i - self.LAG + self.NSLOT):
            j = self.loaded
            self.P.DMA("pool", self._view(j), self.specs[j][0], sem="w%d" % (j % self.NSLOT))
            self.loaded += 1
        assert self.loaded > i
        return self._view(i)


class StopBuild(Exception):
    pass


STOP_AT = [None]


def ckpt(name):
    if STOP_AT[0] == name:
        raise StopBuild(name)


def alias_barrier(olds, news):
    acc = {}
    for t in olds:
        for b in t.bufs:
            if b.w is not None and acc.get(b.w[0], 0) < b.w[1]:
                acc[b.w[0]] = b.w[1]
            for s, v in b.r.items():
                if acc.get(s, 0) < v:
                    acc[s] = v
    for t in news:
        for b in t.bufs:
            for s, v in acc.items():
                if b.r.get(s, 0) < v:
                    b.r[s] = v


def build_nc(n_layers=N_LAYERS):
    nc = bass.Bass("TRN2", target_bir_lowering=False)

    def din(name, shape):
        return nc.dram_tensor(name, list(shape), F32, kind="ExternalInput").ap()

    def dout(name, shape):
        return nc.dram_tensor(name, list(shape), F32, kind="ExternalOutput").ap()

    x_p = din("x_p", [SEQ, D])
    x_s = din("x_s", [NS, D])
    c_in = din("c_in", [NS + 1, D])
    st_conf = din("st_conf", [DEPTH, NS, 30, CONF_CH])
    st_short = din("st_short", [DEPTH, NS, 3, 1536])
    st_delta = din("st_delta", [DEPTH, NS, 4, 128, 128])
    w_ada = din("w_ada", [DEPTH, D, 6 * D])
    w_in = din("w_in", [DEPTH, D, IN_DIM])
    w_conf_out = din("w_conf_out", [DEPTH, CONF_CH, D])
    w_delta_out = din("w_delta_out", [DEPTH, 512, D])
    w_merge_out = din("w_merge_out", [DEPTH, D, D])
    w_ffn_in = din("w_ffn_in", [DEPTH, D, 2 * D_FF])
    w_ffn_out = din("w_ffn_out", [DEPTH, D_FF, D])
    pvec = din("pvec", [DEPTH, 128, PV_N])
    consts = din("consts", [128, 5, 128])
    bmask = din("bmask", [128, 3, 128])

    y_p = dout("y_p", [SEQ, D])
    y_s = dout("y_s", [NS, D])
    o_conf_p = dout("o_conf_p", [DEPTH, 30, CONF_CH])
    o_conf_s = dout("o_conf_s", [DEPTH, NS, 30, CONF_CH])
    o_short_p = dout("o_short_p", [DEPTH, 3, 1536])
    o_short_s = dout("o_short_s", [DEPTH, NS, 3, 1536])
    o_delta_p = dout("o_delta_p", [DEPTH, 4, 128, 128])
    o_delta_s = dout("o_delta_s", [DEPTH, NS, 4, 128, 128])

    with ExitStack() as es:
        def sb(name, shape, dt=F32):
            return es.enter_context(nc.sbuf_tensor(name, list(shape), dt))

        xT_t = sb("xT", [128, KD, NTOK])
        hT_t = sb("hT", [128, KD, 1040], BF16)
        A_CA = 1070
        A_QKV = 1043
        A_ZS = A_QKV + 3 * 1040
        R_WORDS = 1070 + 4 * 1040
        A_CAT = R_WORDS
        A_OGT = A_CAT + 2080
        ARENA_WORDS = A_OGT + 2080
        assert A_ZS + 1040 <= R_WORDS and 11 * 520 <= A_OGT
        arena = sb("arena", [128, ARENA_WORDS])
        wslots = [sb("wslot%d" % i, [128, WStream.ELEMS], BF16) for i in range(WStream.NSLOT)]
        scr_t = [sb("scr%d" % i, [128, 512]) for i in range(4)]
        st_t = [sb("st%d" % i, [128, 512]) for i in range(3)]
        sm_t = [sb("sm%d" % i, [128, 128]) for i in range(18)]
        LGN = ("qkT", "kdc", "kbg", "vb", "u", "wT", "vn", "qg")
        lg_t = {n: [sb("%s%d" % (n, i), [128, 128]) for i in range(2)] for n in LGN}
        S_t = [sb("S%d" % h, [128, 128]) for h in range(4)]
        Ss_t = sb("Ss", [128, NS * 128])
        cols_t = sb("cols", [128, 12, 32])
        scol_t = sb("scol", [128, 16, 16])
        modT_t2 = [sb("modT%d" % i, [128, 48, NS + 1]) for i in range(2)]
        gs_t2 = [sb("gs%d" % i, [128, 16, NS + 1]) for i in range(2)]
        pvn_t = sb("pvn", [128, 64])
        cT_t = sb("cT", [128, KD, NS + 1], BF16)
        pv_t = sb("pv", [128, PV_N])
        negA_t = sb("negA", [128, 32])
        cst_t = sb("cst", [128, 6, 128])
        upre_t = sb("upre", [128, 4, 30])
        qpre_t = sb("qpre", [128, 12, 3])
        sstT_t = sb("sstT", [128, 12, 48])
        pc_t = [sb("pc%d" % i, [NS, 128]) for i in range(2)]
        bm_t = sb("bmask_sb", [128, 3, 128], BF16)
        qkb_t = sb("qkb", [128, 2, 1040], BF16)
        ps_t = [es.enter_context(nc.psum_tensor("ps%d" % i, [128, 512], F32)) for i in range(8)]
        print("SBUF bytes remaining per partition:", nc.sbuf_bytes_remaining)

        sem_names = ["pe", "act", "dve", "pool"] + ["w%d" % i for i in range(WStream.NSLOT)] + \
            ["ld0", "ld1", "ldc", "ldp", "ldpn", "ldcs", "ldss", "ldSs", "o_cpo", "o_cso", "o_pc0", "o_pc1", "ldbm",
             "o_S0", "o_S1", "o_S2", "o_S3", "o_Ss", "o_y0", "o_y1", "o_d2d"]

        ws = WStream()

        def build(P):
            ws.begin(P, [w[:] for w in wslots])
            MM, TR, ACT, TT, TS, STT, CP, RED, RECIP, MEMSET, DMA = (P.MM, P.TR, P.ACT, P.TT, P.TS, P.STT, P.CP, P.RED,
                                                                     P.RECIP, P.MEMSET, P.DMA)
            GB = [(0, 512), (512, 512), (1024, 512), (1536, 512), (2048, NS)]
            xT = [[T(xT_t[:, k, b0:b0 + bw]) for (b0, bw) in GB] for k in range(KD)]
            LB = [(0, 512), (512, 512), (1024, NS)]
            hT = [[T(hT_t[:, k, b0:b0 + bw]) for (b0, bw) in LB] for k in range(KD)]
            ar = arena
            upad = T(ar[:, 0:1070])
            ca = [T(ar[:, A_CA + j * 1040:A_CA + (j + 1) * 1040]) for j in range(4)]
            pad = T(ar[:, 0:1043])
            qkv = [T(ar[:, A_QKV + c * 1040:A_QKV + (c + 1) * 1040]) for c in range(3)]
            zs = T(ar[:, A_ZS:A_ZS + 1040])
            merged_v = ar[:, 0:4160].bitcast(BF16).rearrange("p (k n) -> p k n", k=8)
            merged = [T(merged_v[:, c, :]) for c in range(8)]
            caT_v = ar[:, A_CAT:A_CAT + 2080].bitcast(BF16).rearrange("p (k n) -> p k n", k=4)
            caT = [T(caT_v[:, j, :]) for j in range(4)]
            ogT_v = ar[:, A_OGT:A_OGT + 2080].bitcast(BF16).rearrange("p (k n) -> p k n", k=4)
            ogT = [T(ogT_v[:, h, :]) for h in range(4)]
            act_v = ar[:, 0:11 * 520].bitcast(BF16).rearrange("p (k n) -> p k n", k=11)
            actT = [T(act_v[:, j, :]) for j in range(11)]
            cs_tok = T(ar[:120, A_OGT:A_OGT + 2048].rearrange("p (r c) -> p r c", r=4))
            PH1 = [upad] + ca + caT + [cs_tok]
            PH2 = [pad] + qkv + [zs] + ogT
            PH3 = merged
            PH4 = actT

            scr = Ring([T(t[:]) for t in scr_t])
            st1, st2, st3 = [T(t[:]) for t in st_t]
            cpo = T(st_t[0][:30, :], st1.bufs)
            cso = T(st_t[2][:NS, :], st3.bufs)
            cstT = T(st_t[1][:, 0:NS * 31].rearrange("p (s j) -> p s j", j=31), st2.bufs)
            sm_base = [T(t[:]) for t in sm_t]
            sm = Ring(sm_base)
            sm_extra = [T(st_t[i][:, k_ * 128:(k_ + 1) * 128]) for i in range(3) for k_ in range(4)]
            smx = Ring(sm_base + sm_extra)
            lgx_t = {n: list(v) + [Ss_t[:, j * 128:(j + 1) * 128]] for j, (n, v) in enumerate(lg_t.items())}
            lg = {n: Ring([T(t[:]) for t in v]) for n, v in lgx_t.items()}
            lgb = {n: Ring([T(t[:, 0:64].bitcast(BF16)) for t in v]) for n, v in lgx_t.items()}
            lgb["Xb"] = Ring([T(t[:, 64:128].bitcast(BF16)) for t in lgx_t["qkT"]])
            lgb["Sb"] = Ring([T(t[:, 64:128].bitcast(BF16)) for t in lgx_t["kdc"]])
            qkb = [T(qkb_t[:, i, :]) for i in range(2)]
            def _pair(a, b):
                bf_ = Buf()
                return (T(a[:, 64:128].bitcast(BF16), bf_), T(b[:, 64:128].bitcast(BF16), bf_))
            NFr = Ring([_pair(lgx_t["kbg"][i], lgx_t["vb"][i]) for i in range(3)])
            MFr = Ring([_pair(lgx_t["wT"][i], lgx_t["vn"][i]) for i in range(3)])
            S = [T(t[:]) for t in S_t]
            Ss = T(Ss_t[:].rearrange("p (s v) -> p s v", v=128))
            ld = [T(Ss_t[:, i * 1024:(i + 1) * 1024], Ss.bufs) for i in range(2)]
            ss_tok = T(Ss_t[:48, 0:1536], Ss.bufs)
            big = Ring([T(ps_t[i][:], Buf("pb%d" % i, True)) for i in range(4)])
            sbank = [Buf("psb%d" % i, True) for i in range(4)]
            small = Ring([T(ps_t[4 + i % 4][:, (i // 4) * 128:(i // 4 + 1) * 128], sbank[i % 4]) for i in range(16)])
            colsv = [T(cols_t[:, i, :]) for i in range(12)]
            beta_c, g_c, gc_c, ngc_c, egc_c, bgc_c, nbeta_c, kdec_c, egl_c, tmp_c, tmp2_c, colr = colsv
            scol = Ring([T(scol_t[:, i, :]) for i in range(16)])
            modTs = [T(t[:]) for t in modT_t2]
            gss = [T(t[:]) for t in gs_t2]
            pvn = T(pvn_t[:])
            cur = {}
            cT = T(cT_t[:])
            pvl = T(pv_t[:])
            negA = T(negA_t[:])
            cst = T(cst_t[:, 0:5, :])
            ident = cst[:, 0, :]
            ones = cst[:, 1, :]
            tri = cst[:, 2, :]
            nm_strict = cst[:, 3, :]
            nm_inclT = cst[:, 4, :]
            eps6 = T(cst_t[:, 5, 0:1])
            eps5 = T(cst_t[:, 5, 1:2])
            onec = T(cst_t[:, 5, 2:3])
            zeroc = T(cst_t[:, 5, 3:4])
            P.zeroc = zeroc
            upre = T(upre_t[:])
            qpre = T(qpre_t[:])
            sstT = T(sstT_t[:])
            pcs = [T(t[:]) for t in pc_t]
            pc_i = [0]

            def piece_out(dram_ap, psrc, rows):
                i = pc_i[0] % 2
                pc_i[0] += 1
                CP(pcs[i][:rows, :], psrc)
                DMA("sp", dram_ap, pcs[i][:rows, :], sem="o_pc%d" % i)

            DMA("sp", cst, consts[:, :, :], sem="ldc")
            bm = T(bm_t[:])
            DMA("pool", bm, bmask[:, :, :], sem="ldbm")
            MEMSET(eps6, 1e-6)
            MEMSET(eps5, 1e-5)
            MEMSET(onec, 1.0)
            MEMSET(zeroc, 0.0)

            for t in range(SEQ // 128):
                l_ = ld[t % 2]
                DMA("sp", l_, x_p[t * 128:(t + 1) * 128, :], sem="ld%d" % (t % 2))
                gb = t // 4
                for k in range(KD):
                    p = small.next()
                    TR(p, l_[:, k * 128:(k + 1) * 128], ident)
                    dst = T(xT_t[:, k, t * 128:(t + 1) * 128], xT[k][gb].bufs)
                    if k % 2 == 0:
                        ACT(dst, p, AF.Copy)
                    else:
                        CP(dst, p)
            l_ = ld[0]
            DMA("sp", l_[:NS, :], x_s[:, :], sem="ld0")
            for k in range(KD):
                p = small.next()
                TR(p[:, :NS], l_[:NS, k * 128:(k + 1) * 128], ident[:NS, :NS])
                CP(xT[k][4], p[:, :NS])
            l_ = ld[1]
            DMA("sp", l_[:NS + 1, :], c_in[:, :], sem="ld1")
            ACT(l_[:NS + 1, :], l_[:NS + 1, :], AF.Silu)
            for k in range(KD):
                p = small.next()
                TR(p[:, :NS + 1], l_[:NS + 1, k * 128:(k + 1) * 128], ident[:NS + 1, :NS + 1])
                CP(cT[:, k, :], p[:, :NS + 1])

            ckpt('init')
            def rmsnorm_mod(blocks, gbs, gsoff, shoff):
                for b, (b0, bw) in enumerate(blocks):
                    gb = gbs[b]
                    ps = big.next()
                    for k in range(KD):
                        sq = scr.next()
                        ACT(sq[:, :bw], xT[k][gb], AF.Square)
                        MM(ps[:, :bw], ones, sq[:, :bw], start=(k == 0), stop=(k == KD - 1))
                    rs = st1
                    ACT(rs[:, :bw], ps[:, :bw], AF.Ln, scale=1.0 / D, bias=eps6)
                    ACT(rs[:, :bw], rs[:, :bw], AF.Exp, scale=-0.5)
                    for k in range(KD):
                        tmp = scr.next()
                        TT(tmp[:, :bw], xT[k][gb], rs[:, :bw], ALU.mult)
                        if bw == NS:
                            TT(tmp[:, :bw], tmp[:, :bw], gs[:, gsoff + k, 1:NS + 1], ALU.mult)
                            TT(hT[k][b], tmp[:, :bw], modT[:, shoff + k, 1:NS + 1], ALU.add)
                        else:
                            ACT(hT[k][b], tmp[:, :bw], AF.Identity, scale=gs[:, gsoff + k, 0:1], bias=modT[:, shoff + k, 0:1])

            def resid_update(c, ps, bw, gb, gtoff):
                if bw == NS:
                    tmp = scr.next()
                    TT(tmp[:, :bw], ps[:, :bw], modT[:, gtoff + c, 1:NS + 1], ALU.mult)
                    TT(xT[c][gb], xT[c][gb], tmp[:, :bw], ALU.add)
                else:
                    STT(xT[c][gb], ps[:, :bw], modT[:, gtoff + c, 0:1], xT[c][gb], ALU.mult, ALU.add)

            def wv(ap2d):
                return ap2d.rearrange("(k p) n -> p k n", p=128)

            def mod_piece(ll, n2):
                mT = modTs[ll % 2]
                w = ws.get(wv(w_ada[ll, :, n2 * 256:(n2 + 1) * 256]), KD, 256)
                for half in range(2):
                    n = n2 * 2 + half
                    p = small.next()
                    for k in range(KD):
                        MM(p[:, :NS + 1], w[:, k, half * 128:(half + 1) * 128], cT[:, k, :], start=(k == 0), stop=(k == KD - 1))
                    ACT(mT[:, n, :], p[:, :NS + 1], AF.Identity, bias=pvn[:, PV_BADA + n:PV_BADA + n + 1])

            def mod_finish(ll):
                mT, g_ = modTs[ll % 2], gss[ll % 2]
                for k in range(KD):
                    TS(g_[:, k, :], mT[:, 8 + k, :], 1.0, ALU.add, pvn[:, PV_G1 + k:PV_G1 + k + 1], ALU.mult)
                    TS(g_[:, 8 + k, :], mT[:, 32 + k, :], 1.0, ALU.add, pvn[:, PV_G2 + k:PV_G2 + k + 1], ALU.mult)

            for l in range(n_layers):
                DMA("sp", pvl, pvec[l], sem="ldp")
                modT, gs = modTs[l % 2], gss[l % 2]
                cur["modT"], cur["gs"] = modT, gs
                if l == 0:
                    DMA("sp", pvn, pvec[0, :, 0:64], sem="ldpn")
                    for n2 in range(24):
                        mod_piece(0, n2)
                    mod_finish(0)
                ACT(negA, pvl[:, PV_ALOG:PV_ALOG + 32], AF.Exp)
                ACT(negA, negA, AF.Identity, scale=-1.0)
                ckpt('mod')
                DMA("sp", o_conf_s[l, :, 0:29, :], st_conf[l, :, 1:30, :], sem="o_d2d")
                DMA("sp", o_short_s[l, :, 0:2, :], st_short[l, :, 1:3, :], sem="o_d2d")
                DMA("sp", ss_tok, st_short[l].rearrange("s j c -> (s j) c"), sem="ldss")
                for gch in range(12):
                    p = small.next()
                    TR(p[:, :48], ss_tok[:48, gch * 128:(gch + 1) * 128], ident[:48, :48])
                    CP(sstT[:, gch, :], p[:, :48])

                ckpt('lstart')
                for pi in range(2):
                    blocks = LB[:2] if pi == 0 else LB
                    gbs = [0, 1] if pi == 0 else [2, 3, 4]
                    has_s = pi == 1
                    W = 1024 + (NS if has_s else 0)
                    NCH = 8

                    rmsnorm_mod(blocks, gbs, 0, 0)

                    ckpt('norm1')
                    alias_barrier(PH2 + PH3 + PH4, PH1)
                    if has_s:
                        DMA("sp", cs_tok, st_conf[l].rearrange("(r s) j c -> (s j) r c", r=4), sem="ldcs")
                    else:
                        MEMSET(upad[:, 0:30], 0.0)
                    for j in range(4):
                        if has_s:
                            CP(upad[:, 0:30], upre[:, j, :])
                        wa = ws.get(wv(w_in[l, :, j * 128:(j + 1) * 128]), KD, 128)
                        wb = ws.get(wv(w_in[l, :, O_GLU_B + j * 128:O_GLU_B + (j + 1) * 128]), KD, 128)
                        for b, (b0, bw) in enumerate(blocks):
                            pa = big.next()
                            pb = big.next()
                            for k in range(KD):
                                MM(pa[:, :bw], wa[:, k, :], hT[k][b], start=(k == 0), stop=(k == KD - 1))
                            for k in range(KD):
                                MM(pb[:, :bw], wb[:, k, :], hT[k][b], start=(k == 0), stop=(k == KD - 1))
                            sg = scr.next()
                            ACT(sg[:, :bw], pb[:, :bw], AF.Sigmoid)
                            TT(upad[:, 30 + b0:30 + b0 + bw], pa[:, :bw], sg[:, :bw], ALU.mult)
                        cw0 = PV_CW + j * 31
                        caj = ca[j][:, 0:1024]
                        TS(caj, upad[:, 0:1024], pvl[:, cw0:cw0 + 1], ALU.mult, pvl[:, PV_CB + j:PV_CB + j + 1], ALU.add)
                        for i in range(1, CONF_KW):
                            STT(caj, upad[:, i:i + 1024], pvl[:, cw0 + i:cw0 + i + 1], caj, ALU.mult, ALU.add)
                        if not has_s:
                            ACT(upre[:, j, :], upad[:, 1024:1054], AF.Copy)
                        else:
                            p = small.next()
                            TR(p[:30, :], upad[:, 1024:1054], ident)
                            CP(cpo[:30, j * 128:(j + 1) * 128], p[:30, :])
                            us_ = upad[:, 30 + 1024:30 + 1024 + NS]
                            for rt in range(4):
                                p = small.next()
                                TR(p[:, :120], cs_tok[:120, rt, j * 128:(j + 1) * 128], ident[:120, :120])
                                CP(cstT[:, 4 * rt:4 * rt + 4, 0:30], p[:, :120].v(lambda a: a.rearrange("p (s j) -> p s j", j=30)))
                            CP(cstT[:, :, 30], us_)
                            prod = scr.next()[:, 0:NS * 31].v(lambda a: a.rearrange("p (s j) -> p s j", j=31))
                            TT(prod, cstT, pvl[:, cw0:cw0 + 31].v(lambda a: a.unsqueeze(1).to_broadcast([128, NS, 31])), ALU.mult)
                            RED(ca[j][:, 1024:1024 + NS], prod)
                            TS(ca[j][:, 1024:1024 + NS], ca[j][:, 1024:1024 + NS], pvl[:, PV_CB + j:PV_CB + j + 1], ALU.add)
                            p = small.next()
                            TR(p[:NS, :], us_, ident)
                            CP(cso[:NS, j * 128:(j + 1) * 128], p[:NS, :])
                    if has_s:
                        DMA("sp", o_conf_p[l], cpo, sem="o_cpo")
                        DMA("sp", o_conf_s[l, :, 29, :], cso, sem="o_cso")
                    ckpt('conf')
                    for b, (b0, bw) in enumerate(blocks):
                        p1 = big.next()
                        p2 = big.next()
                        for j in range(4):
                            MM(p1[:, :bw], ones, ca[j][:, b0:b0 + bw], start=(j == 0), stop=(j == 3))
                        for j in range(4):
                            sq = scr.next()
                            ACT(sq[:, :bw], ca[j][:, b0:b0 + bw], AF.Square)
                            MM(p2[:, :bw], ones, sq[:, :bw], start=(j == 0), stop=(j == 3))
                        mean, msq, var = st1, st2, st3
                        ACT(mean[:, :bw], p1[:, :bw], AF.Identity, scale=1.0 / CONF_CH)
                        TT(msq[:, :bw], mean[:, :bw], mean[:, :bw], ALU.mult)
                        STT(var[:, :bw], p2[:, :bw], 1.0 / CONF_CH, msq[:, :bw], ALU.mult, ALU.subtract)
                        ACT(var[:, :bw], var[:, :bw], AF.Ln, bias=eps5)
                        ACT(var[:, :bw], var[:, :bw], AF.Exp, scale=-0.5)
                        for j in range(4):
                            t = scr.next()
                            TT(t[:, :bw], ca[j][:, b0:b0 + bw], mean[:, :bw], ALU.subtract)
                            TT(t[:, :bw], t[:, :bw], var[:, :bw], ALU.mult)
                            ACT(caT[j][:, b0:b0 + bw], t[:, :bw], AF.Silu, scale=pvl[:, PV_LNG + j:PV_LNG + j + 1],
                                bias=pvl[:, PV_LNB + j:PV_LNB + j + 1])

                    ckpt('confln')
                    alias_barrier(PH1, PH2)
                    wbg = ws.get(wv(w_in[l, :, O_BETA:O_BETA + 8]), KD, 8)
                    pbg = small.next()
                    for c in range(NCH):
                        for k in range(KD):
                            MM(pbg[:, c * 8:(c + 1) * 8], hT[k][c // 4][:, (c % 4) * 128:(c % 4 + 1) * 128], wbg[:, k, :],
                               start=(k == 0), stop=(k == KD - 1))
                    pbg3 = pbg[:, 0:64].v(lambda a: a.rearrange("p (c e) -> p c e", e=8))

                    def v3(t):
                        return t.v(lambda a: a.rearrange("p (c e) -> p c e", e=4))
                    ACT(v3(beta_c), pbg3[:, :, 0:4], AF.Sigmoid)
                    TT(v3(tmp_c), pbg3[:, :, 4:8], v3(pvl[:, PV_DTB:PV_DTB + 32]), ALU.add)
                    ACT(tmp_c, tmp_c, AF.Exp)
                    ACT(tmp_c, tmp_c, AF.Ln, bias=onec)
                    TT(g_c, tmp_c, negA, ALU.mult)
                    pgc = small.next()
                    pgl = small.next()
                    for c in range(NCH):
                        MM(pgc[:, c * 4:(c + 1) * 4], tri, g_c[:, c * 4:(c + 1) * 4])
                    for c in range(NCH):
                        MM(pgl[:, c * 4:(c + 1) * 4], ones, g_c[:, c * 4:(c + 1) * 4])
                    CP(gc_c, pgc[:, 0:32])
                    ACT(ngc_c, pgc[:, 0:32], AF.Identity, scale=-1.0)
                    ACT(egc_c, pgc[:, 0:32], AF.Exp)
                    TT(bgc_c, beta_c, egc_c, ALU.mult)
                    ACT(nbeta_c, beta_c, AF.Identity, scale=-1.0)
                    TT(tmp2_c, pgl[:, 0:32], gc_c, ALU.subtract)
                    ACT(kdec_c, tmp2_c, AF.Exp)
                    ACT(egl_c, pgl[:, 0:32], AF.Exp)
                    if has_s:
                        pbs = small.next()
                        for k in range(KD):
                            MM(pbs[:NS, 0:8], hT[k][2], wbg[:, k, :], start=(k == 0), stop=(k == KD - 1))
                        beta_s = T(cols_t[:NS, 11, 0:4], colr.bufs)
                        eg_s = T(cols_t[:NS, 11, 4:8], colr.bufs)
                        tmp_s = T(cols_t[:NS, 11, 8:12], colr.bufs)
                        ACT(beta_s, pbs[:NS, 0:4], AF.Sigmoid)
                        TT(tmp_s, pbs[:NS, 4:8], pvl[:NS, PV_DTB:PV_DTB + 4], ALU.add)
                        ACT(tmp_s, tmp_s, AF.Exp)
                        ACT(tmp_s, tmp_s, AF.Ln, bias=onec[:NS, :])
                        TT(tmp_s, tmp_s, negA[:NS, 0:4], ALU.mult)
                        ACT(eg_s, tmp_s, AF.Exp)
                    else:
                        MEMSET(pad[:, 0:3], 0.0)

                    ckpt('dprol')
                    for h in range(4):
                        wq = [ws.get(wv(w_in[l, :, off + h * 128:off + (h + 1) * 128]), KD, 128) for off in (O_Q, O_K, O_V, O_Z)]
                        for ci in range(3):
                            gch = ci * 4 + h
                            if has_s:
                                CP(pad[:, 0:3], qpre[:, gch, :])
                            for b, (b0, bw) in enumerate(blocks):
                                ps = big.next()
                                for k in range(KD):
                                    MM(ps[:, :bw], wq[ci][:, k, :], hT[k][b], start=(k == 0), stop=(k == KD - 1))
                                ACT(pad[:, 3 + b0:3 + b0 + bw], ps[:, :bw], AF.Copy)
                            sw0 = PV_SW + gch * 4
                            qc = qkv[ci][:, 0:1024]
                            TS(qc, pad[:, 0:1024], pvl[:, sw0:sw0 + 1], ALU.mult)
                            for i in range(1, 4):
                                STT(qc, pad[:, i:i + 1024], pvl[:, sw0 + i:sw0 + i + 1], qc, ALU.mult, ALU.add)
                            if not has_s:
                                ACT(qpre[:, gch, :], pad[:, 1024:1027], AF.Copy)
                            else:
                                p = small.next()
                                TR(p[:3, :], pad[:, 1024:1027], ident)
                                piece_out(o_short_p[l, :, gch * 128:(gch + 1) * 128], p[:3, :], 3)
                                xs_ = pad[:, 1027:1027 + NS]
                                qs_ = qkv[ci][:, 1024:1024 + NS]
                                st3v = sstT[:, gch, :].v(lambda a: a.rearrange("p (s j) -> p s j", j=3))
                                TS(qs_, xs_, pvl[:, sw0 + 3:sw0 + 4], ALU.mult)
                                for i in range(3):
                                    STT(qs_, st3v[:, :, i], pvl[:, sw0 + i:sw0 + i + 1], qs_, ALU.mult, ALU.add)
                                p = small.next()
                                TR(p[:NS, :], xs_, ident)
                                piece_out(o_short_s[l, :, 2, gch * 128:(gch + 1) * 128], p[:NS, :], NS)
                            ACT(qkv[ci][:, 0:W], qkv[ci][:, 0:W], AF.Silu)
                            if ci < 2:
                                for b, (b0, bw) in enumerate(blocks):
                                    sq = scr.next()
                                    ACT(sq[:, :bw], qkv[ci][:, b0:b0 + bw], AF.Square)
                                    ps = big.next()
                                    MM(ps[:, :bw], ones, sq[:, :bw])
                                    rn = scr.next()
                                    ACT(rn[:, :bw], ps[:, :bw], AF.Ln, bias=eps6)
                                    ACT(rn[:, :bw], rn[:, :bw], AF.Exp, scale=-0.5)
                                    if ci == 0:
                                        STT(qkv[ci][:, b0:b0 + bw], qkv[ci][:, b0:b0 + bw], 128.0 ** -0.5, rn[:, :bw], ALU.mult, ALU.mult)
                                    else:
                                        TT(qkv[ci][:, b0:b0 + bw], qkv[ci][:, b0:b0 + bw], rn[:, :bw], ALU.mult)
                                    if bw != NS:
                                        CP(qkb[ci][:, b0:b0 + bw], qkv[ci][:, b0:b0 + bw], eng="pool")
                        for b, (b0, bw) in enumerate(blocks):
                            ps = big.next()
                            for k in range(KD):
                                MM(ps[:, :bw], wq[3][:, k, :], hT[k][b], start=(k == 0), stop=(k == KD - 1))
                            ACT(zs[:, b0:b0 + bw], ps[:, :bw], AF.Silu)
                        if not has_s:
                            MEMSET(S[h], 0.0)

                        ckpt('dhead')
                        def hl_views(t):
                            return (T(t.ap[:, 0:64].bitcast(BF16), t.bufs), T(t.ap[:, 64:128].bitcast(BF16), t.bufs))

                        def hl_psum(ps):
                            hi, lo = hl_views(smx.next())
                            ACT(hi, ps, AF.Copy)
                            TT(lo, ps, hi, ALU.subtract)
                            return hi, lo

                        def hl_sbuf(x):
                            hi, lo = hl_views(smx.next())
                            ACT(hi, x, AF.Copy)
                            TT(lo, x, hi, ALU.subtract, eng=HL_ENG)
                            return hi, lo

                        def masked(src, mask):
                            hi, lo = hl_views(smx.next())
                            TT(hi, src[0], mask, ALU.mult, eng="pool")
                            TT(lo, src[1], mask, ALU.mult, eng="pool")
                            return hi, lo

                        def mm3(ps, a, b):
                            MM(ps, a[0], b[0], start=True, stop=False)
                            MM(ps, a[0], b[1], start=False, stop=False)
                            MM(ps, a[1], b[0], start=False, stop=True)

                        def chunk_gen(c):
                            cs = slice(c * 128, (c + 1) * 128)
                            ix = c * 4 + h
                            col = slice(ix, ix + 1)
                            qTc, kTc, vTc = qkv[0][:, cs], qkv[1][:, cs], qkv[2][:, cs]
                            qbc, kbc = qkb[0][:, cs], qkb[1][:, cs]
                            grep = smx.next()
                            TS(grep, ones, g_c[:, col], ALU.mult)
                            pg = small.next()
                            MM(pg, grep, tri)
                            yield
                            pre1 = smx.next()
                            STT(pre1, pg, -1.0, nm_strict, ALU.mult, ALU.add)
                            pre2 = smx.next()
                            TT(pre2, pg, nm_inclT, ALU.add)
                            egr = smx.next()
                            ACT(egr, pg, AF.Exp)
                            yield
                            Ds = smx.next()
                            ACT(Ds, pre1, AF.Exp, bias=gc_c[:, col])
                            DT = smx.next()
                            ACT(DT, pre2, AF.Exp, bias=ngc_c[:, col])
                            qgc = lgb["qg"].next()
                            TT(qgc, qTc, egr, ALU.mult)
                            yield
                            pk = small.next()
                            MM(pk, kbc, kbc)
                            Nm = smx.next()
                            STT(Nm, pk, nbeta_c[:, col], Ds, ALU.mult, ALU.mult)
                            yield
                            pq = small.next()
                            MM(pq, kbc, qbc)
                            qkT = lgb["qkT"].next()
                            TT(qkT, pq, DT, ALU.mult)
                            yield
                            pt = small.next()
                            TR(pt, Nm, ident)
                            NF = NFr.next()
                            ACT(NF[0], Nm, AF.Copy)
                            TT(NF[1], Nm, NF[0], ALU.subtract, eng=HL_ENG)
                            yield
                            MF = MFr.next()
                            ACT(MF[0], pt, AF.Copy)
                            TT(MF[1], pt, MF[0], ALU.subtract)
                            yield
                            ptk = small.next()
                            TR(ptk, kTc, ident)
                            kbg = lgb["kbg"].next()
                            ACT(kbg, ptk, AF.Identity, scale=bgc_c[:, col])
                            kdc = lgb["kdc"].next()
                            ACT(kdc, ptk, AF.Identity, scale=kdec_c[:, col])
                            yield
                            ptv = small.next()
                            TR(ptv, vTc, ident)
                            vb = lgb["vb"].next()
                            ACT(vb, ptv, AF.Identity, scale=beta_c[:, col])
                            Pn = masked(NF, bm[:, 0, :])
                            Pm = masked(MF, bm[:, 0, :])
                            yield
                            X = smx.next()
                            TT(X, Pm[0], ident, ALU.add)
                            TT(X, X, Pm[1], ALU.add)
                            Xs = hl_sbuf(X)
                            Y = smx.next()
                            TT(Y, Pn[0], ident, ALU.add)
                            TT(Y, Y, Pn[1], ALU.add)
                            Ys = hl_sbuf(Y)
                            for lev in range(1, 5):
                                yield
                                pp2 = small.next()
                                mm3(pp2, Pm, Pn)
                                Pn2 = hl_psum(pp2)
                                pp1 = small.next()
                                mm3(pp1, Pn, Pm)
                                Pm2 = hl_psum(pp1)
                                yield
                                px = small.next()
                                mm3(px, Pn2, Xs)
                                X2 = smx.next()
                                TT(X2, px, X, ALU.add)
                                Xs = hl_sbuf(X2)
                                yield
                                py = small.next()
                                mm3(py, Pm2, Ys)
                                Y2 = smx.next()
                                TT(Y2, py, Y, ALU.add)
                                Ys = hl_sbuf(Y2)
                                X, Y, Pn, Pm = X2, Y2, Pn2, Pm2
                            for ci_ in (1, 2):
                                yield
                                No = masked(NF, bm[:, ci_, :])
                                if ci_ == 1:
                                    Mo = masked(MF, bm[:, ci_, :])
                                yield
                                pt1 = small.next()
                                mm3(pt1, No, Xs)
                                T1 = hl_psum(pt1)
                                if ci_ == 1:
                                    pt2 = small.next()
                                    mm3(pt2, Mo, Ys)
                                    T2 = hl_psum(pt2)
                                yield
                                pxx = small.next()
                                mm3(pxx, Ys, T1)
                                X2 = smx.next()
                                TT(X2, pxx, X, ALU.add)
                                if ci_ == 1:
                                    pyy = small.next()
                                    mm3(pyy, Xs, T2)
                                    Y2 = smx.next()
                                    TT(Y2, pyy, Y, ALU.add)
                                    yield
                                    Xs = hl_sbuf(X2)
                                    Ys = hl_sbuf(Y2)
                                    Y = Y2
                                X = X2
                            yield
                            Xb = lgb["Xb"].next()
                            CP(Xb, X, eng="pool")
                            yield
                            pu = small.next()
                            MM(pu, Xb, vb)
                            u = lg["u"].next()
                            ACT(u, pu, AF.Copy)
                            yield
                            pw = small.next()
                            MM(pw, kbg, Xb)
                            wT = lgb["wT"].next()
                            CP(wT, pw)
                            yield
                            Sb = lgb["Sb"].next()
                            CP(Sb, S[h], eng="pool")
                            pvn = small.next()
                            MM(pvn, wT, Sb)
                            vn = lgb["vn"].next()
                            TT(vn, u, pvn, ALU.subtract)
                            yield
                            po = small.next()
                            MM(po, qgc, Sb, start=True, stop=False)
                            MM(po, qkT, vn, start=False, stop=True)
                            yield
                            pS = small.next()
                            MM(pS, kdc, vn)
                            STT(S[h], S[h], egl_c[:, col], pS, ALU.mult, ALU.add)
                            yield
                            osq = smx.next()
                            ssq = scol.next()
                            ACT(osq, po, AF.Square, accum=ssq[:, 0:1])
                            ACT(ssq[:, 0:1], ssq[:, 0:1], AF.Ln, scale=1.0 / 128, bias=eps6)
                            ACT(ssq[:, 0:1], ssq[:, 0:1], AF.Exp, scale=-0.5)
                            yield
                            on = smx.next()
                            ACT(on, po, AF.Identity, scale=ssq[:, 0:1])
                            yield
                            pT = small.next()
                            TR(pT, on, ident)
                            STT(ogT[h][:, cs], pT, pvl[:, PV_DNG:PV_DNG + 1], zs[:, cs], ALU.mult, ALU.mult)


                        xtiles = (sm_extra + [lg[n_].tiles[2] for n_ in lg] + [lgb[n_].tiles[2] for n_ in lgb]
                                  + list(NFr.tiles[2]) + list(MFr.tiles[2]))
                        alias_barrier([st1, st2, st3, Ss], xtiles)
                        pending = list(range(NCH))
                        active = []
                        rnd = 0
                        while pending or active:
                            if pending and len(active) < GCH and (rnd % STAG == 0 or not active):
                                active.append(chunk_gen(pending.pop(0)))
                            for g_ in list(active):
                                try:
                                    next(g_)
                                except StopIteration:
                                    active.remove(g_)
                            rnd += 1
                        alias_barrier(xtiles, [st1, st2, st3, Ss])
                        ckpt('dchunks')
                        if pi == 0 and l + 1 < n_layers:
                            if h == 0:
                                DMA("sp", pvn, pvec[l + 1, :, 0:64], sem="ldpn")
                            for n2 in range(h * 6, h * 6 + 6):
                                mod_piece(l + 1, n2)
                            if h == 3:
                                mod_finish(l + 1)
                        if has_s:
                            DMA("sp", o_delta_p[l, h], S[h], sem="o_S%d" % h)
                            sc = slice(1024, 1024 + NS)
                            qs, ks, vs = qkv[0][:, sc], qkv[1][:, sc], qkv[2][:, sc]
                            DMA("sp", Ss, st_delta[l, :, h].rearrange("s p v -> p s v"), sem="ldSs")

                            def bcast16(colap):
                                dg = sm.next()
                                TS(dg[:NS, :NS], ident[:NS, :NS], colap, ALU.mult)
                                pB = small.next()
                                MM(pB[:, :NS], ones[:NS, :], dg[:NS, :NS])
                                o_ = scol.next()
                                CP(o_, pB[:, :NS])
                                return o_
                            bbc = bcast16(beta_s[:, h:h + 1])
                            ebc = bcast16(eg_s[:, h:h + 1])
                            t0 = scol.next()
                            TT(t0, qs, ks, ALU.mult)
                            pqk = small.next()
                            MM(pqk[:, :NS], ones, t0)
                            qkbc = scol.next()
                            CP(qkbc, pqk[:, :NS])
                            pkS = small.next()
                            pqS = small.next()
                            for s in range(NS):
                                MM(pkS[:, s:s + 1], Ss[:, s, :], ks[:, s:s + 1])
                            for s in range(NS):
                                MM(pqS[:, s:s + 1], Ss[:, s, :], qs[:, s:s + 1])
                            t1 = scol.next()
                            TT(t1, pkS[:, :NS], ebc, ALU.mult)
                            TT(t1, vs, t1, ALU.subtract)
                            vnT = scol.next()
                            TT(vnT, t1, bbc, ALU.mult)
                            oT = scol.next()
                            TT(oT, pqS[:, :NS], ebc, ALU.mult)
                            t2 = scol.next()
                            TT(t2, qkbc, vnT, ALU.mult)
                            TT(oT, oT, t2, ALU.add)
                            t3 = scol.next()
                            TT(t3, oT, oT, ALU.mult)
                            pss = small.next()
                            MM(pss[:, :NS], ones, t3)
                            rs_ = scol.next()
                            ACT(rs_, pss[:, :NS], AF.Ln, scale=1.0 / 128, bias=eps6)
                            ACT(rs_, rs_, AF.Exp, scale=-0.5)
                            TT(oT, oT, rs_, ALU.mult)
                            STT(ogT[h][:, sc], oT, pvl[:, PV_DNG:PV_DNG + 1], zs[:, sc], ALU.mult, ALU.mult)
                            pkt = small.next()
                            TR(pkt[:NS, :], ks, ident)
                            k_tok = scr.next()[:, 0:128]
                            CP(k_tok[:NS, :], pkt[:NS, :])
                            pvt = small.next()
                            TR(pvt[:NS, :], vnT, ident)
                            vn_tok = scr.next()[:, 0:128]
                            CP(vn_tok[:NS, :], pvt[:NS, :])
                            for s in range(NS):
                                rhs_s = sm.next()
                                TS(rhs_s[:NS, :], vn_tok[:NS, :], ident[:NS, s:s + 1], ALU.mult)
                                po_ = small.next()
                                MM(po_, k_tok[:NS, :], rhs_s[:NS, :])
                                STT(Ss[:, s, :], Ss[:, s, :], ebc[:, s:s + 1], po_, ALU.mult, ALU.add)
                            DMA("sp", o_delta_s[l, :, h].rearrange("s p v -> p s v"), Ss, sem="o_Ss")

                    ckpt('delta')
                    alias_barrier(PH2, PH3)
                    for c2 in range(4):
                        wco = ws.get(wv(w_conf_out[l, :, c2 * 256:(c2 + 1) * 256]), 4, 256)
                        wdo = ws.get(wv(w_delta_out[l, :, c2 * 256:(c2 + 1) * 256]), 4, 256)
                        wma = ws.get(wv(w_in[l, :, O_MA + c2 * 256:O_MA + (c2 + 1) * 256]), KD, 256)
                        wmb = ws.get(wv(w_in[l, :, O_MB + c2 * 256:O_MB + (c2 + 1) * 256]), KD, 256)
                        for c1 in range(2):
                            c = c2 * 2 + c1
                            w0 = c1 * 128
                            for b, (b0, bw) in enumerate(blocks):
                                pya = big.next()
                                pma = big.next()
                                for k in range(4):
                                    MM(pya[:, :bw], wco[:, k, w0:w0 + 128], caT[k][:, b0:b0 + bw], start=(k == 0), stop=(k == 3))
                                for k in range(KD):
                                    MM(pma[:, :bw], wma[:, k, w0:w0 + 128], hT[k][b], start=(k == 0), stop=(k == KD - 1))
                                sa = scr.next()
                                ACT(sa[:, :bw], pma[:, :bw], AF.Sigmoid)
                                TT(sa[:, :bw], pya[:, :bw], sa[:, :bw], ALU.mult)
                                pyb = big.next()
                                pmb = big.next()
                                for k in range(4):
                                    MM(pyb[:, :bw], wdo[:, k, w0:w0 + 128], ogT[k][:, b0:b0 + bw], start=(k == 0), stop=(k == 3))
                                for k in range(KD):
                                    MM(pmb[:, :bw], wmb[:, k, w0:w0 + 128], hT[k][b], start=(k == 0), stop=(k == KD - 1))
                                sb_ = scr.next()
                                ACT(sb_[:, :bw], pmb[:, :bw], AF.Sigmoid)
                                TT(sb_[:, :bw], pyb[:, :bw], sb_[:, :bw], ALU.mult)
                                TT(merged[c][:, b0:b0 + bw], sa[:, :bw], sb_[:, :bw], ALU.add)
                    ckpt('merge')
                    for c2 in range(4):
                        wmo = ws.get(wv(w_merge_out[l, :, c2 * 256:(c2 + 1) * 256]), KD, 256)
                        for c1 in range(2):
                            c = c2 * 2 + c1
                            for b, (b0, bw) in enumerate(blocks):
                                ps = big.next()
                                for k in range(KD):
                                    MM(ps[:, :bw], wmo[:, k, c1 * 128:(c1 + 1) * 128], merged[k][:, b0:b0 + bw], start=(k == 0), stop=(k == KD - 1))
                                resid_update(c, ps, bw, gbs[b], 16)

                    ckpt('mergeout')
                    rmsnorm_mod(blocks, gbs, 8, 24)
                    for fh in range(2):
                        alias_barrier(PH1 + PH2 + PH3 + PH4, PH4)
                        for jj2 in range(6):
                            j0 = fh * 11 + jj2 * 2
                            nj = min(2, fh * 11 + 11 - j0)
                            wg = ws.get(wv(w_ffn_in[l, :, j0 * 128:(j0 + nj) * 128]), KD, nj * 128)
                            wu = ws.get(wv(w_ffn_in[l, :, D_FF + j0 * 128:D_FF + (j0 + nj) * 128]), KD, nj * 128)
                            for j1 in range(nj):
                                jl = jj2 * 2 + j1
                                for b, (b0, bw) in enumerate(blocks):
                                    pgt = big.next()
                                    pup = big.next()
                                    for k in range(KD):
                                        MM(pgt[:, :bw], wg[:, k, j1 * 128:(j1 + 1) * 128], hT[k][b], start=(k == 0), stop=(k == KD - 1))
                                    for k in range(KD):
                                        MM(pup[:, :bw], wu[:, k, j1 * 128:(j1 + 1) * 128], hT[k][b], start=(k == 0), stop=(k == KD - 1))
                                    sg = scr.next()
                                    ACT(sg[:, :bw], pgt[:, :bw], AF.Silu)
                                    TT(actT[jl][:, b0:b0 + bw], sg[:, :bw], pup[:, :bw], ALU.mult)
                        for c in range(KD):
                            wo = ws.get(wv(w_ffn_out[l, fh * 1408:(fh + 1) * 1408, c * 128:(c + 1) * 128]), 11, 128)
                            for b, (b0, bw) in enumerate(blocks):
                                ps = big.next()
                                for j in range(11):
                                    MM(ps[:, :bw], wo[:, j, :], actT[j][:, b0:b0 + bw], start=(j == 0), stop=(j == 10))
                                resid_update(c, ps, bw, gbs[b], 40)

            for gb, (g0, gw) in enumerate(GB):
                ps = big.next()
                for k in range(KD):
                    sq = scr.next()
                    ACT(sq[:, :gw], xT[k][gb], AF.Square)
                    MM(ps[:, :gw], ones, sq[:, :gw], start=(k == 0), stop=(k == KD - 1))
                rs = st1
                ACT(rs[:, :gw], ps[:, :gw], AF.Ln, scale=1.0 / D, bias=eps6)
                ACT(rs[:, :gw], rs[:, :gw], AF.Exp, scale=-0.5)
                nt = max(1, gw // 128)
                tw = min(gw, 128)
                for t in range(nt):
                    o_ = ld[t % 2]
                    for k in range(KD):
                        tmp = sm.next()
                        STT(tmp[:, :tw], xT[k][gb][:, t * 128:t * 128 + tw], pvl[:, PV_FG + k:PV_FG + k + 1], rs[:, t * 128:t * 128 + tw],
                            ALU.mult, ALU.mult)
                        p = small.next()
                        TR(p[:tw, :], tmp[:, :tw], ident)
                        if k % 2 == 0:
                            ACT(o_[:tw, k * 128:(k + 1) * 128], p[:tw, :], AF.Copy)
                        else:
                            CP(o_[:tw, k * 128:(k + 1) * 128], p[:tw, :])
                    if gb < 4:
                        r0 = g0 + t * 128
                        DMA("sp", y_p[r0:r0 + 128, :], o_, sem="o_y%d" % (t % 2))
                    else:
                        DMA("sp", y_s[:, :], o_[:NS, :], sem="o_y%d" % (t % 2))

        def build_w(Pq):
            try:
                build(Pq)
            except StopBuild:
                pass
            for sname in sem_names:
                if (sname.startswith("o_") or sname.startswith("w") or sname.startswith("ld")) and Pq.cnt.get(sname, 0) > 0:
                    Pq.q["sp"].append(("wait", sname, Pq.cnt[sname]))
        P0 = Prog()
        build_w(P0)
        ws.recording = False
        P = Prog()
        build_w(P)
        print("ops:", P.nops, {k: len(v) for k, v in P.q.items()}, "weights:", len(ws.specs))

        semh = {n: es.enter_context(nc.semaphore(n)) for n in sem_names}
        block = es.enter_context(nc.Block())

        def run(eng, items):
            for it in items:
                if it[0] == "wait":
                    eng.wait_ge(semh[it[1]], it[2])
                else:
                    it[1](eng).then_inc(semh[it[2]], it[3])

        @block.tensor
        def _(eng):
            run(eng, P.q["pe"])

        @block.scalar
        def _(eng):
            run(eng, P.q["act"])

        @block.vector
        def _(eng):
            run(eng, P.q["dve"])

        @block.gpsimd
        def _(eng):
            run(eng, P.q["pool"])

        @block.sync
        def _(eng):
            run(eng, P.q["sp"])
    return nc


def make_consts():
    c = np.zeros((128, 5, 128), np.float32)
    i = np.arange(128)
    c[:, 0, :] = np.eye(128, dtype=np.float32)
    c[:, 1, :] = 1.0
    c[:, 2, :] = (i[:, None] <= i[None, :]).astype(np.float32)
    c[:, 3, :] = np.where(i[:, None] > i[None, :], 0.0, NEG).astype(np.float32)
    c[:, 4, :] = np.where(i[None, :] >= i[:, None], 0.0, NEG).astype(np.float32)
    return c


def make_bmask():
    i = np.arange(128)
    b32 = (i[:, None] // 32 == i[None, :] // 32)
    b64 = (i[:, None] // 64 == i[None, :] // 64)
    m = np.zeros((128, 3, 128), np.float32)
    m[:, 0, :] = b32
    m[:, 1, :] = b64 & ~b32
    m[:, 2, :] = ~b64
    return m


def make_pvec(inp):
    pv = np.zeros((DEPTH, 128, PV_N), np.float32)
    for l in range(DEPTH):
        pv[l, :, PV_G1:PV_G1 + 8] = inp["norm1_g"][l].reshape(8, 128).T
        pv[l, :, PV_G2:PV_G2 + 8] = inp["norm2_g"][l].reshape(8, 128).T
        pv[l, :, PV_BADA:PV_BADA + 48] = inp["b_ada"][l].reshape(48, 128).T
        cw = inp["conf_dw_w"][l]
        for j in range(4):
            pv[l, :, PV_CW + j * 31:PV_CW + (j + 1) * 31] = cw[:, j * 128:(j + 1) * 128].T
        pv[l, :, PV_CB:PV_CB + 4] = inp["conf_dw_b"][l].reshape(4, 128).T
        pv[l, :, PV_LNG:PV_LNG + 4] = inp["conf_ln_g"][l].reshape(4, 128).T
        pv[l, :, PV_LNB:PV_LNB + 4] = inp["conf_ln_b"][l].reshape(4, 128).T
        sw = inp["short_conv_w"][l]
        for g in range(12):
            pv[l, :, PV_SW + g * 4:PV_SW + (g + 1) * 4] = sw[:, g * 128:(g + 1) * 128].T
        pv[l, :, PV_DNG] = inp["delta_norm_g"][l]
        pv[l, :, PV_ALOG:PV_ALOG + 32] = np.tile(inp["a_log"][l], 8)[None, :]
        pv[l, :, PV_DTB:PV_DTB + 32] = np.tile(inp["dt_bias"][l], 8)[None, :]
        pv[l, :, PV_FG:PV_FG + 8] = inp["final_norm_g"].reshape(8, 128).T
    return pv


_NC_CACHE = {}


def kernel(**inp):
    inp = {k: np.asarray(v) for k, v in inp.items()}
    n = 8
    if "nc" not in _NC_CACHE:
        _NC_CACHE["nc"] = build_nc()
    nc = _NC_CACHE["nc"]
    consts = make_consts()
    bmask_np = make_bmask()
    pv = make_pvec(inp)
    shared = {k: np.ascontiguousarray(inp[k], dtype=np.float32) for k in
              ("w_ada", "w_in", "w_conf_out", "w_delta_out", "w_merge_out", "w_ffn_in", "w_ffn_out")}
    in_maps = []
    for i in range(n):
        s0 = i * NS
        m = dict(shared)
        m["x_p"] = np.ascontiguousarray(inp["x_prompt"][i])
        m["x_s"] = np.ascontiguousarray(inp["x_sample"][s0:s0 + NS, 0, :])
        m["c_in"] = np.ascontiguousarray(np.concatenate([inp["c_prompt"][i:i + 1], inp["c_sample"][s0:s0 + NS]], axis=0))
        m["st_conf"] = np.ascontiguousarray(inp["state_conformer_conv"][:, s0:s0 + NS])
        m["st_short"] = np.ascontiguousarray(inp["state_short_conv"][:, s0:s0 + NS])
        m["st_delta"] = np.ascontiguousarray(inp["state_delta"][:, s0:s0 + NS])
        m["pvec"] = pv
        m["consts"] = consts
        m["bmask"] = bmask_np
        in_maps.append(m)
    res = run_bass_kernel_spmd(nc, in_maps, core_ids=list(range(n)))
    R = res.results
    y_p = np.stack([R[i]["y_p"] for i in range(n)], axis=0)
    y_s = np.concatenate([R[i]["y_s"] for i in range(n)], axis=0)[:, None, :]
    conf_p = np.stack([R[i]["o_conf_p"] for i in range(n)], axis=1)
    conf_s = np.concatenate([R[i]["o_conf_s"] for i in range(n)], axis=1)
    short_p = np.stack([R[i]["o_short_p"] for i in range(n)], axis=1)
    short_s = np.concatenate([R[i]["o_short_s"] for i in range(n)], axis=1)
    delta_p = np.stack([R[i]["o_delta_p"] for i in range(n)], axis=1)
    delta_s = np.concatenate([R[i]["o_delta_s"] for i in range(n)], axis=1)
    return tuple(np.ascontiguousarray(a, dtype=np.float32) for a in
                 (y_p, y_s, conf_p, conf_s, short_p, short_s, delta_p, delta_s))
```

```python
import numpy as np
from contextlib import ExitStack
import concourse.bass as bass
import concourse.mybir as mybir
from concourse.bass_utils import run_bass_kernel_spmd

F32 = mybir.dt.float32
BF16 = mybir.dt.bfloat16
AF = mybir.ActivationFunctionType
ALU = mybir.AluOpType
AX = mybir.AxisListType

DEPTH = 4
D = 1024
KD = 8
SEQ = 2048
NS = 16
NTOK = SEQ + NS
CONF_CH = 512
CONF_KW = 31
D_FF = 2816
KF = 22
O_GLU_B = 512
O_Q = 1024
O_K = 1536
O_V = 2048
O_Z = 2560
O_BETA = 3072
O_MA = 3080
O_MB = 4104
IN_DIM = 5128
NEG = -1.0e6

SAME_ENGINE_SYNC = True
GCH = 2
STAG = 15
N_LAYERS = DEPTH

PV_G1 = 0
PV_G2 = 8
PV_BADA = 16
PV_CW = 64
PV_CB = 188
PV_LNG = 192
PV_LNB = 196
PV_SW = 200
PV_DNG = 248
PV_ALOG = 249
PV_DTB = 281
PV_FG = 313
PV_N = 321


class Buf:
    __slots__ = ("name", "w", "r", "psum")

    def __init__(self, name="", psum=False):
        self.name = name
        self.w = None
        self.r = {}
        self.psum = psum


class T:
    __slots__ = ("ap", "bufs")

    def __init__(self, ap, bufs=None):
        self.ap = ap
        if bufs is None:
            bufs = [Buf()]
        elif isinstance(bufs, Buf):
            bufs = [bufs]
        self.bufs = bufs

    def __getitem__(self, idx):
        return T(self.ap[idx], self.bufs)

    def v(self, fn):
        return T(fn(self.ap), self.bufs)


def _ap(x):
    return x.ap if isinstance(x, T) else x


def _bufs(*xs):
    out = []
    for x in xs:
        if isinstance(x, T):
            out.extend(x.bufs)
    return out


class Ring:
    def __init__(self, tiles):
        self.tiles = tiles
        self.i = 0

    def next(self):
        t = self.tiles[self.i % len(self.tiles)]
        self.i += 1
        return t


class Prog:
    ENGS = ("pe", "act", "dve", "pool", "sp")

    def __init__(self):
        self.q = {n: [] for n in self.ENGS}
        self.cnt = {}
        self.seen = {n: {} for n in self.ENGS}
        self.nops = 0

    def op(self, e, fn, reads=(), writes=(), sem=None, inc=1):
        deps = {}
        for b in reads:
            if b.w is not None:
                if deps.get(b.w[0], 0) < b.w[1]:
                    deps[b.w[0]] = b.w[1]
            if b.psum:
                for s, v in b.r.items():
                    if s != e and deps.get(s, 0) < v:
                        deps[s] = v
        for b in writes:
            if b.w is not None:
                if deps.get(b.w[0], 0) < b.w[1]:
                    deps[b.w[0]] = b.w[1]
            for s, v in b.r.items():
                if deps.get(s, 0) < v:
                    deps[s] = v
        seen = self.seen[e]
        q = self.q[e]
        for s, v in deps.items():
            if s == e and (e == "pe" or not SAME_ENGINE_SYNC):
                continue
            if seen.get(s, 0) >= v:
                continue
            seen[s] = v
            q.append(("wait", s, v))
        sname = sem or e
        val = self.cnt.get(sname, 0) + inc
        self.cnt[sname] = val
        q.append(("op", fn, sname, inc))
        for b in reads:
            if b.r.get(sname, 0) < val:
                b.r[sname] = val
        for b in writes:
            b.w = (sname, val)
            b.r = {}
        self.nops += 1
        return val

    def MM(self, out, lhsT, rhs, start=True, stop=True):
        o, l, r = _ap(out), _ap(lhsT), _ap(rhs)
        self.op("pe", lambda e: e.matmul(o, lhsT=l, rhs=r, start=start, stop=stop), _bufs(lhsT, rhs), _bufs(out))

    def TR(self, out, in_, ident):
        o, i, d = _ap(out), _ap(in_), _ap(ident)
        self.op("pe", lambda e: e.transpose(o, i, d), _bufs(in_, ident), _bufs(out))

    def ACT(self, out, in_, func, bias=None, scale=None, accum=None):
        o, i = _ap(out), _ap(in_)
        kw = {}
        if bias is None and isinstance(scale, T):
            bias = self.zeroc[0:o.shape[0], :] if o.shape[0] != 128 else self.zeroc
        if bias is not None:
            kw["bias"] = _ap(bias)
        if scale is not None:
            kw["scale"] = _ap(scale)
        if accum is not None:
            kw["accum_out"] = _ap(accum)
        self.op("act", lambda e: e.activation(out=o, in_=i, func=func, **kw), _bufs(in_, bias, scale), _bufs(out, accum))

    def TT(self, out, in0, in1, op, eng="dve"):
        o, a, b = _ap(out), _ap(in0), _ap(in1)
        self.op(eng, lambda e: e.tensor_tensor(out=o, in0=a, in1=b, op=op), _bufs(in0, in1), _bufs(out))

    def TS(self, out, in0, s1, op0, s2=None, op1=None, eng="dve"):
        o, a, x1, x2 = _ap(out), _ap(in0), _ap(s1), _ap(s2)
        if op1 is None:
            self.op(eng, lambda e: e.tensor_scalar(out=o, in0=a, scalar1=x1, scalar2=None, op0=op0), _bufs(in0, s1), _bufs(out))
        else:
            self.op(eng, lambda e: e.tensor_scalar(out=o, in0=a, scalar1=x1, scalar2=x2, op0=op0, op1=op1), _bufs(in0, s1, s2), _bufs(out))

    def STT(self, out, in0, scalar, in1, op0, op1, eng="dve"):
        o, a, s, b = _ap(out), _ap(in0), _ap(scalar), _ap(in1)
        self.op(eng, lambda e: e.scalar_tensor_tensor(out=o, in0=a, scalar=s, in1=b, op0=op0, op1=op1), _bufs(in0, scalar, in1), _bufs(out))

    def CP(self, out, in_, eng="dve"):
        o, i = _ap(out), _ap(in_)
        self.op(eng, lambda e: e.tensor_copy(out=o, in_=i), _bufs(in_), _bufs(out))

    def RED(self, out, in_, op=ALU.add, eng="dve"):
        o, i = _ap(out), _ap(in_)
        self.op(eng, lambda e: e.tensor_reduce(out=o, in_=i, axis=AX.X, op=op), _bufs(in_), _bufs(out))

    def RECIP(self, out, in_):
        o, i = _ap(out), _ap(in_)
        self.op("dve", lambda e: e.reciprocal(out=o, in_=i), _bufs(in_), _bufs(out))

    def MEMSET(self, out, val, eng="dve"):
        o = _ap(out)
        self.op(eng, lambda e: e.memset(o, val), (), _bufs(out))

    def DMA(self, q, out, in_, sem, **kw):
        o, i = _ap(out), _ap(in_)
        self.op(q, lambda e: e.dma_start(out=o, in_=i, **kw), _bufs(in_), _bufs(out), sem=sem, inc=16)


class WStream:
    NSLOT = 6
    LAG = 4
    ELEMS = 2048

    def __init__(self):
        self.specs = []
        self.recording = True

    def begin(self, P, slot_aps):
        self.P = P
        self.i = 0
        self.loaded = 0
        self.slots = [T(a) for a in slot_aps]

    def _view(self, j):
        _, kc, n = self.specs[j]
        t = self.slots[j % self.NSLOT]
        return T(t.ap[:, 0:kc * n].rearrange("p (k n) -> p k n", k=kc), t.bufs)

    def get(self, dram_ap, kc, n):
        assert kc * n <= self.ELEMS
        i = self.i
        self.i += 1
        if self.recording:
            self.specs.append((dram_ap, kc, n))
            return self._view(i)
        while self.loaded < len(self.specs) and (self.loaded < self.NSLOT or self.loaded <= i - self.LAG + self.NSLOT):
            j = self.loaded
            self.P.DMA("pool", self._view(j), self.specs[j][0], sem="w%d" % (j % self.NSLOT))
            self.loaded += 1
        assert self.loaded > i
        return self._view(i)


class StopBuild(Exception):
    pass


STOP_AT = [None]


def ckpt(name):
    if STOP_AT[0] == name:
        raise StopBuild(name)


def alias_barrier(olds, news):
    acc = {}
    for t in olds:
        for b in t.bufs:
            if b.w is not None and acc.get(b.w[0], 0) < b.w[1]:
                acc[b.w[0]] = b.w[1]
            for s, v in b.r.items():
                if acc.get(s, 0) < v:
                    acc[s] = v
    for t in news:
        for b in t.bufs:
            for s, v in acc.items():
                if b.r.get(s, 0) < v:
                    b.r[s] = v


def build_nc(n_layers=N_LAYERS):
    nc = bass.Bass("TRN2", target_bir_lowering=False)

    def din(name, shape):
        return nc.dram_tensor(name, list(shape), F32, kind="ExternalInput").ap()

    def dout(name, shape):
        return nc.dram_tensor(name, list(shape), F32, kind="ExternalOutput").ap()

    x_p = din("x_p", [SEQ, D])
    x_s = din("x_s", [NS, D])
    c_in = din("c_in", [NS + 1, D])
    st_conf = din("st_conf", [DEPTH, NS, 30, CONF_CH])
    st_short = din("st_short", [DEPTH, NS, 3, 1536])
    st_delta = din("st_delta", [DEPTH, NS, 4, 128, 128])
    w_ada = din("w_ada", [DEPTH, D, 6 * D])
    w_in = din("w_in", [DEPTH, D, IN_DIM])
    w_conf_out = din("w_conf_out", [DEPTH, CONF_CH, D])
    w_delta_out = din("w_delta_out", [DEPTH, 512, D])
    w_merge_out = din("w_merge_out", [DEPTH, D, D])
    w_ffn_in = din("w_ffn_in", [DEPTH, D, 2 * D_FF])
    w_ffn_out = din("w_ffn_out", [DEPTH, D_FF, D])
    pvec = din("pvec", [DEPTH, 128, PV_N])
    consts = din("consts", [128, 5, 128])
    bmask = din("bmask", [128, 2, 128])

    y_p = dout("y_p", [SEQ, D])
    y_s = dout("y_s", [NS, D])
    o_conf_p = dout("o_conf_p", [DEPTH, 30, CONF_CH])
    o_conf_s = dout("o_conf_s", [DEPTH, NS, 30, CONF_CH])
    o_short_p = dout("o_short_p", [DEPTH, 3, 1536])
    o_short_s = dout("o_short_s", [DEPTH, NS, 3, 1536])
    o_delta_p = dout("o_delta_p", [DEPTH, 4, 128, 128])
    o_delta_s = dout("o_delta_s", [DEPTH, NS, 4, 128, 128])

    with ExitStack() as es:
        def sb(name, shape, dt=F32):
            return es.enter_context(nc.sbuf_tensor(name, list(shape), dt))

        xT_t = sb("xT", [128, KD, NTOK])
        hT_t = sb("hT", [128, KD, 1040], BF16)
        A_CA = 1070
        A_QKV = 1043
        A_ZS = A_QKV + 3 * 1040
        R_WORDS = 1070 + 4 * 1040
        A_CAT = R_WORDS
        A_OGT = A_CAT + 2080
        ARENA_WORDS = A_OGT + 2080
        assert A_ZS + 1040 <= R_WORDS and 11 * 520 <= A_OGT
        arena = sb("arena", [128, ARENA_WORDS])
        wslots = [sb("wslot%d" % i, [128, WStream.ELEMS], BF16) for i in range(WStream.NSLOT)]
        scr_t = [sb("scr%d" % i, [128, 512]) for i in range(4)]
        st_t = [sb("st%d" % i, [128, 512]) for i in range(3)]
        sm_t = [sb("sm%d" % i, [128, 128]) for i in range(16)]
        LGN = ("qkT", "kdc", "kbg", "vb", "u", "wT", "vn", "qg", "No")
        lg_t = {n: [sb("%s%d" % (n, i), [128, 128]) for i in range(2)] for n in LGN}
        S_t = [sb("S%d" % h, [128, 128]) for h in range(4)]
        Ss_t = sb("Ss", [128, NS * 128])
        cols_t = sb("cols", [128, 12, 32])
        scol_t = sb("scol", [128, 16, 16])
        modT_t2 = [sb("modT%d" % i, [128, 48, NS + 1]) for i in range(2)]
        gs_t2 = [sb("gs%d" % i, [128, 16, NS + 1]) for i in range(2)]
        pvn_t = sb("pvn", [128, 64])
        cT_t = sb("cT", [128, KD, NS + 1], BF16)
        pv_t = sb("pv", [128, PV_N])
        negA_t = sb("negA", [128, 32])
        cst_t = sb("cst", [128, 6, 128])
        upre_t = sb("upre", [128, 4, 30])
        qpre_t = sb("qpre", [128, 12, 3])
        sstT_t = sb("sstT", [128, 12, 48])
        pc_t = [sb("pc%d" % i, [NS, 128]) for i in range(4)]
        qkb_t = sb("qkb", [128, 2, 1040], BF16)
        bm_t = sb("bmask_sb", [128, 2, 128], BF16)
        ps_t = [es.enter_context(nc.psum_tensor("ps%d" % i, [128, 512], F32)) for i in range(8)]
        print("SBUF bytes remaining per partition:", nc.sbuf_bytes_remaining)

        sem_names = ["pe", "act", "dve", "pool"] + ["w%d" % i for i in range(WStream.NSLOT)] + \
            ["ld0", "ld1", "ldc", "ldp", "ldpn", "ldbm", "ldcs", "ldss", "ldSs", "o_cpo", "o_cso", "o_pc0", "o_pc1", "o_pc2", "o_pc3",
             "o_S0", "o_S1", "o_S2", "o_S3", "o_Ss", "o_y0", "o_y1", "o_d2d"]

        ws = WStream()

        def build(P):
            ws.begin(P, [w[:] for w in wslots])
            MM, TR, ACT, TT, TS, STT, CP, RED, RECIP, MEMSET, DMA = (P.MM, P.TR, P.ACT, P.TT, P.TS, P.STT, P.CP, P.RED,
                                                                     P.RECIP, P.MEMSET, P.DMA)
            GB = [(0, 512), (512, 512), (1024, 512), (1536, 512), (2048, NS)]
            xT = [[T(xT_t[:, k, b0:b0 + bw]) for (b0, bw) in GB] for k in range(KD)]
            LB = [(0, 512), (512, 512), (1024, NS)]
            hT = [[T(hT_t[:, k, b0:b0 + bw]) for (b0, bw) in LB] for k in range(KD)]
            ar = arena
            upad = T(ar[:, 0:1070])
            ca = [T(ar[:, A_CA + j * 1040:A_CA + (j + 1) * 1040]) for j in range(4)]
            pad = T(ar[:, 0:1043])
            qkv = [T(ar[:, A_QKV + c * 1040:A_QKV + (c + 1) * 1040]) for c in range(3)]
            zs = T(ar[:, A_ZS:A_ZS + 1040])
            merged_v = ar[:, 0:4160].bitcast(BF16).rearrange("p (k n) -> p k n", k=8)
            merged = [T(merged_v[:, c, :]) for c in range(8)]
            caT_v = ar[:, A_CAT:A_CAT + 2080].bitcast(BF16).rearrange("p (k n) -> p k n", k=4)
            caT = [T(caT_v[:, j, :]) for j in range(4)]
            ogT_v = ar[:, A_OGT:A_OGT + 2080].bitcast(BF16).rearrange("p (k n) -> p k n", k=4)
            ogT = [T(ogT_v[:, h, :]) for h in range(4)]
            act_v = ar[:, 0:11 * 520].bitcast(BF16).rearrange("p (k n) -> p k n", k=11)
            actT = [T(act_v[:, j, :]) for j in range(11)]
            cs_tok = T(ar[:120, A_OGT:A_OGT + 2048].rearrange("p (r c) -> p r c", r=4))
            PH1 = [upad] + ca + caT + [cs_tok]
            PH2 = [pad] + qkv + [zs] + ogT
            PH3 = merged
            PH4 = actT

            scr = Ring([T(t[:]) for t in scr_t])
            st1, st2, st3 = [T(t[:]) for t in st_t]
            cpo = T(st_t[0][:30, :], st1.bufs)
            cso = T(st_t[2][:NS, :], st3.bufs)
            cstT = T(st_t[1][:, 0:NS * 31].rearrange("p (s j) -> p s j", j=31), st2.bufs)
            sm = Ring([T(t[:]) for t in sm_t])
            lg = {n: Ring([T(t[:]) for t in v]) for n, v in lg_t.items()}
            lgb = {n: Ring([T(t[:, 0:64].bitcast(BF16)) for t in v]) for n, v in lg_t.items()}
            lgb["Xb"] = Ring([T(t[:, 64:128].bitcast(BF16)) for t in lg_t["qkT"]])
            lgb["Sb"] = Ring([T(t[:, 64:128].bitcast(BF16)) for t in lg_t["kdc"]])
            qkb = [T(qkb_t[:, i, :]) for i in range(2)]
            S = [T(t[:]) for t in S_t]
            Ss = T(Ss_t[:].rearrange("p (s v) -> p s v", v=128))
            ld = [T(Ss_t[:, i * 1024:(i + 1) * 1024], Ss.bufs) for i in range(2)]
            ss_tok = T(Ss_t[:48, 0:1536], Ss.bufs)
            big = Ring([T(ps_t[i][:], Buf("pb%d" % i, True)) for i in range(4)])
            sbank = [Buf("psb%d" % i, True) for i in range(4)]
            small = Ring([T(ps_t[4 + i % 4][:, (i // 4) * 128:(i // 4 + 1) * 128], sbank[i % 4]) for i in range(16)])
            colsv = [T(cols_t[:, i, :]) for i in range(12)]
            beta_c, g_c, gc_c, ngc_c, egc_c, bgc_c, nbeta_c, kdec_c, egl_c, tmp_c, tmp2_c, colr = colsv
            scol = Ring([T(scol_t[:, i, :]) for i in range(16)])
            modTs = [T(t[:]) for t in modT_t2]
            gss = [T(t[:]) for t in gs_t2]
            pvn = T(pvn_t[:])
            cur = {}
            cT = T(cT_t[:])
            pvl = T(pv_t[:])
            negA = T(negA_t[:])
            cst = T(cst_t[:, 0:5, :])
            ident = cst[:, 0, :]
            ones = cst[:, 1, :]
            tri = cst[:, 2, :]
            nm_strict = cst[:, 3, :]
            nm_inclT = cst[:, 4, :]
            eps6 = T(cst_t[:, 5, 0:1])
            eps5 = T(cst_t[:, 5, 1:2])
            onec = T(cst_t[:, 5, 2:3])
            zeroc = T(cst_t[:, 5, 3:4])
            P.zeroc = zeroc
            upre = T(upre_t[:])
            qpre = T(qpre_t[:])
            sstT = T(sstT_t[:])
            pcs = [T(t[:]) for t in pc_t]
            pc_i = [0]

            def piece_out(dram_ap, psrc, rows):
                i = pc_i[0] % 4
                pc_i[0] += 1
                CP(pcs[i][:rows, :], psrc)
                DMA("sp", dram_ap, pcs[i][:rows, :], sem="o_pc%d" % i)

            DMA("sp", cst, consts[:, :, :], sem="ldc")
            bm = T(bm_t[:])
            DMA("pool", bm, bmask[:, :, :], sem="ldbm")
            MEMSET(eps6, 1e-6)
            MEMSET(eps5, 1e-5)
            MEMSET(onec, 1.0)
            MEMSET(zeroc, 0.0)

            for t in range(SEQ // 128):
                l_ = ld[t % 2]
                DMA("sp", l_, x_p[t * 128:(t + 1) * 128, :], sem="ld%d" % (t % 2))
                gb = t // 4
                for k in range(KD):
                    p = small.next()
                    TR(p, l_[:, k * 128:(k + 1) * 128], ident)
                    dst = T(xT_t[:, k, t * 128:(t + 1) * 128], xT[k][gb].bufs)
                    if k % 2 == 0:
                        ACT(dst, p, AF.Copy)
                    else:
                        CP(dst, p)
            l_ = ld[0]
            DMA("sp", l_[:NS, :], x_s[:, :], sem="ld0")
            for k in range(KD):
                p = small.next()
                TR(p[:, :NS], l_[:NS, k * 128:(k + 1) * 128], ident[:NS, :NS])
                CP(xT[k][4], p[:, :NS])
            l_ = ld[1]
            DMA("sp", l_[:NS + 1, :], c_in[:, :], sem="ld1")
            ACT(l_[:NS + 1, :], l_[:NS + 1, :], AF.Silu)
            for k in range(KD):
                p = small.next()
                TR(p[:, :NS + 1], l_[:NS + 1, k * 128:(k + 1) * 128], ident[:NS + 1, :NS + 1])
                CP(cT[:, k, :], p[:, :NS + 1])

            ckpt('init')
            def rmsnorm_mod(blocks, gbs, gsoff, shoff):
                for b, (b0, bw) in enumerate(blocks):
                    gb = gbs[b]
                    ps = big.next()
                    for k in range(KD):
                        sq = scr.next()
                        ACT(sq[:, :bw], xT[k][gb], AF.Square)
                        MM(ps[:, :bw], ones, sq[:, :bw], start=(k == 0), stop=(k == KD - 1))
                    rs = st1
                    ACT(rs[:, :bw], ps[:, :bw], AF.Ln, scale=1.0 / D, bias=eps6)
                    ACT(rs[:, :bw], rs[:, :bw], AF.Exp, scale=-0.5)
                    for k in range(KD):
                        tmp = scr.next()
                        TT(tmp[:, :bw], xT[k][gb], rs[:, :bw], ALU.mult)
                        if bw == NS:
                            TT(tmp[:, :bw], tmp[:, :bw], gs[:, gsoff + k, 1:NS + 1], ALU.mult)
                            TT(hT[k][b], tmp[:, :bw], modT[:, shoff + k, 1:NS + 1], ALU.add)
                        else:
                            ACT(hT[k][b], tmp[:, :bw], AF.Identity, scale=gs[:, gsoff + k, 0:1], bias=modT[:, shoff + k, 0:1])

            def resid_update(c, ps, bw, gb, gtoff):
                if bw == NS:
                    tmp = scr.next()
                    TT(tmp[:, :bw], ps[:, :bw], modT[:, gtoff + c, 1:NS + 1], ALU.mult)
                    TT(xT[c][gb], xT[c][gb], tmp[:, :bw], ALU.add)
                else:
                    STT(xT[c][gb], ps[:, :bw], modT[:, gtoff + c, 0:1], xT[c][gb], ALU.mult, ALU.add)

            def wv(ap2d):
                return ap2d.rearrange("(k p) n -> p k n", p=128)

            def mod_piece(ll, n2):
                mT = modTs[ll % 2]
                w = ws.get(wv(w_ada[ll, :, n2 * 256:(n2 + 1) * 256]), KD, 256)
                for half in range(2):
                    n = n2 * 2 + half
                    p = small.next()
                    for k in range(KD):
                        MM(p[:, :NS + 1], w[:, k, half * 128:(half + 1) * 128], cT[:, k, :], start=(k == 0), stop=(k == KD - 1))
                    ACT(mT[:, n, :], p[:, :NS + 1], AF.Identity, bias=pvn[:, PV_BADA + n:PV_BADA + n + 1])

            def mod_finish(ll):
                mT, g_ = modTs[ll % 2], gss[ll % 2]
                for k in range(KD):
                    TS(g_[:, k, :], mT[:, 8 + k, :], 1.0, ALU.add, pvn[:, PV_G1 + k:PV_G1 + k + 1], ALU.mult)
                    TS(g_[:, 8 + k, :], mT[:, 32 + k, :], 1.0, ALU.add, pvn[:, PV_G2 + k:PV_G2 + k + 1], ALU.mult)

            for l in range(n_layers):
                DMA("sp", pvl, pvec[l], sem="ldp")
                modT, gs = modTs[l % 2], gss[l % 2]
                cur["modT"], cur["gs"] = modT, gs
                if l == 0:
                    DMA("sp", pvn, pvec[0, :, 0:64], sem="ldpn")
                    for n2 in range(24):
                        mod_piece(0, n2)
                    mod_finish(0)
                ACT(negA, pvl[:, PV_ALOG:PV_ALOG + 32], AF.Exp)
                ACT(negA, negA, AF.Identity, scale=-1.0)
                ckpt('mod')
                DMA("sp", o_conf_s[l, :, 0:29, :], st_conf[l, :, 1:30, :], sem="o_d2d")
                DMA("sp", o_short_s[l, :, 0:2, :], st_short[l, :, 1:3, :], sem="o_d2d")
                DMA("sp", ss_tok, st_short[l].rearrange("s j c -> (s j) c"), sem="ldss")
                for gch in range(12):
                    p = small.next()
                    TR(p[:, :48], ss_tok[:48, gch * 128:(gch + 1) * 128], ident[:48, :48])
                    CP(sstT[:, gch, :], p[:, :48])

                ckpt('lstart')
                for pi in range(2):
                    blocks = LB[:2] if pi == 0 else LB
                    gbs = [0, 1] if pi == 0 else [2, 3, 4]
                    has_s = pi == 1
                    W = 1024 + (NS if has_s else 0)
                    NCH = 8

                    rmsnorm_mod(blocks, gbs, 0, 0)

                    ckpt('norm1')
                    alias_barrier(PH2 + PH3 + PH4, PH1)
                    if has_s:
                        DMA("sp", cs_tok, st_conf[l].rearrange("(r s) j c -> (s j) r c", r=4), sem="ldcs")
                    else:
                        MEMSET(upad[:, 0:30], 0.0)
                    for j in range(4):
                        if has_s:
                            CP(upad[:, 0:30], upre[:, j, :])
                        wa = ws.get(wv(w_in[l, :, j * 128:(j + 1) * 128]), KD, 128)
                        wb = ws.get(wv(w_in[l, :, O_GLU_B + j * 128:O_GLU_B + (j + 1) * 128]), KD, 128)
                        for b, (b0, bw) in enumerate(blocks):
                            pa = big.next()
                            pb = big.next()
                            for k in range(KD):
                                MM(pa[:, :bw], wa[:, k, :], hT[k][b], start=(k == 0), stop=(k == KD - 1))
                            for k in range(KD):
                                MM(pb[:, :bw], wb[:, k, :], hT[k][b], start=(k == 0), stop=(k == KD - 1))
                            sg = scr.next()
                            ACT(sg[:, :bw], pb[:, :bw], AF.Sigmoid)
                            TT(upad[:, 30 + b0:30 + b0 + bw], pa[:, :bw], sg[:, :bw], ALU.mult)
                        cw0 = PV_CW + j * 31
                        caj = ca[j][:, 0:1024]
                        TS(caj, upad[:, 0:1024], pvl[:, cw0:cw0 + 1], ALU.mult, pvl[:, PV_CB + j:PV_CB + j + 1], ALU.add)
                        for i in range(1, CONF_KW):
                            STT(caj, upad[:, i:i + 1024], pvl[:, cw0 + i:cw0 + i + 1], caj, ALU.mult, ALU.add)
                        if not has_s:
                            ACT(upre[:, j, :], upad[:, 1024:1054], AF.Copy)
                        else:
                            p = small.next()
                            TR(p[:30, :], upad[:, 1024:1054], ident)
                            CP(cpo[:30, j * 128:(j + 1) * 128], p[:30, :])
                            us_ = upad[:, 30 + 1024:30 + 1024 + NS]
                            for rt in range(4):
                                p = small.next()
                                TR(p[:, :120], cs_tok[:120, rt, j * 128:(j + 1) * 128], ident[:120, :120])
                                CP(cstT[:, 4 * rt:4 * rt + 4, 0:30], p[:, :120].v(lambda a: a.rearrange("p (s j) -> p s j", j=30)))
                            CP(cstT[:, :, 30], us_)
                            prod = scr.next()[:, 0:NS * 31].v(lambda a: a.rearrange("p (s j) -> p s j", j=31))
                            TT(prod, cstT, pvl[:, cw0:cw0 + 31].v(lambda a: a.unsqueeze(1).to_broadcast([128, NS, 31])), ALU.mult)
                            RED(ca[j][:, 1024:1024 + NS], prod)
                            TS(ca[j][:, 1024:1024 + NS], ca[j][:, 1024:1024 + NS], pvl[:, PV_CB + j:PV_CB + j + 1], ALU.add)
                            p = small.next()
                            TR(p[:NS, :], us_, ident)
                            CP(cso[:NS, j * 128:(j + 1) * 128], p[:NS, :])
                    if has_s:
                        DMA("sp", o_conf_p[l], cpo, sem="o_cpo")
                        DMA("sp", o_conf_s[l, :, 29, :], cso, sem="o_cso")
                    ckpt('conf')
                    for b, (b0, bw) in enumerate(blocks):
                        p1 = big.next()
                        p2 = big.next()
                        for j in range(4):
                            MM(p1[:, :bw], ones, ca[j][:, b0:b0 + bw], start=(j == 0), stop=(j == 3))
                        for j in range(4):
                            sq = scr.next()
                            ACT(sq[:, :bw], ca[j][:, b0:b0 + bw], AF.Square)
                            MM(p2[:, :bw], ones, sq[:, :bw], start=(j == 0), stop=(j == 3))
                        mean, msq, var = st1, st2, st3
                        ACT(mean[:, :bw], p1[:, :bw], AF.Identity, scale=1.0 / CONF_CH)
                        TT(msq[:, :bw], mean[:, :bw], mean[:, :bw], ALU.mult)
                        STT(var[:, :bw], p2[:, :bw], 1.0 / CONF_CH, msq[:, :bw], ALU.mult, ALU.subtract)
                        ACT(var[:, :bw], var[:, :bw], AF.Ln, bias=eps5)
                        ACT(var[:, :bw], var[:, :bw], AF.Exp, scale=-0.5)
                        for j in range(4):
                            t = scr.next()
                            TT(t[:, :bw], ca[j][:, b0:b0 + bw], mean[:, :bw], ALU.subtract)
                            TT(t[:, :bw], t[:, :bw], var[:, :bw], ALU.mult)
                            ACT(caT[j][:, b0:b0 + bw], t[:, :bw], AF.Silu, scale=pvl[:, PV_LNG + j:PV_LNG + j + 1],
                                bias=pvl[:, PV_LNB + j:PV_LNB + j + 1])

                    ckpt('confln')
                    alias_barrier(PH1, PH2)
                    wbg = ws.get(wv(w_in[l, :, O_BETA:O_BETA + 8]), KD, 8)
                    pbg = small.next()
                    for c in range(NCH):
                        for k in range(KD):
                            MM(pbg[:, c * 8:(c + 1) * 8], hT[k][c // 4][:, (c % 4) * 128:(c % 4 + 1) * 128], wbg[:, k, :],
                               start=(k == 0), stop=(k == KD - 1))
                    pbg3 = pbg[:, 0:64].v(lambda a: a.rearrange("p (c e) -> p c e", e=8))

                    def v3(t):
                        return t.v(lambda a: a.rearrange("p (c e) -> p c e", e=4))
                    ACT(v3(beta_c), pbg3[:, :, 0:4], AF.Sigmoid)
                    TT(v3(tmp_c), pbg3[:, :, 4:8], v3(pvl[:, PV_DTB:PV_DTB + 32]), ALU.add)
                    ACT(tmp_c, tmp_c, AF.Exp)
                    ACT(tmp_c, tmp_c, AF.Ln, bias=onec)
                    TT(g_c, tmp_c, negA, ALU.mult)
                    pgc = small.next()
                    pgl = small.next()
                    for c in range(NCH):
                        MM(pgc[:, c * 4:(c + 1) * 4], tri, g_c[:, c * 4:(c + 1) * 4])
                    for c in range(NCH):
                        MM(pgl[:, c * 4:(c + 1) * 4], ones, g_c[:, c * 4:(c + 1) * 4])
                    CP(gc_c, pgc[:, 0:32])
                    ACT(ngc_c, pgc[:, 0:32], AF.Identity, scale=-1.0)
                    ACT(egc_c, pgc[:, 0:32], AF.Exp)
                    TT(bgc_c, beta_c, egc_c, ALU.mult)
                    ACT(nbeta_c, beta_c, AF.Identity, scale=-1.0)
                    TT(tmp2_c, pgl[:, 0:32], gc_c, ALU.subtract)
                    ACT(kdec_c, tmp2_c, AF.Exp)
                    ACT(egl_c, pgl[:, 0:32], AF.Exp)
                    if has_s:
                        pbs = small.next()
                        for k in range(KD):
                            MM(pbs[:NS, 0:8], hT[k][2], wbg[:, k, :], start=(k == 0), stop=(k == KD - 1))
                        beta_s = T(cols_t[:NS, 11, 0:4], colr.bufs)
                        eg_s = T(cols_t[:NS, 11, 4:8], colr.bufs)
                        tmp_s = T(cols_t[:NS, 11, 8:12], colr.bufs)
                        ACT(beta_s, pbs[:NS, 0:4], AF.Sigmoid)
                        TT(tmp_s, pbs[:NS, 4:8], pvl[:NS, PV_DTB:PV_DTB + 4], ALU.add)
                        ACT(tmp_s, tmp_s, AF.Exp)
                        ACT(tmp_s, tmp_s, AF.Ln, bias=onec[:NS, :])
                        TT(tmp_s, tmp_s, negA[:NS, 0:4], ALU.mult)
                        ACT(eg_s, tmp_s, AF.Exp)
                    else:
                        MEMSET(pad[:, 0:3], 0.0)

                    ckpt('dprol')
                    for h in range(4):
                        wq = [ws.get(wv(w_in[l, :, off + h * 128:off + (h + 1) * 128]), KD, 128) for off in (O_Q, O_K, O_V, O_Z)]
                        for ci in range(3):
                            gch = ci * 4 + h
                            if has_s:
                                CP(pad[:, 0:3], qpre[:, gch, :])
                            for b, (b0, bw) in enumerate(blocks):
                                ps = big.next()
                                for k in range(KD):
                                    MM(ps[:, :bw], wq[ci][:, k, :], hT[k][b], start=(k == 0), stop=(k == KD - 1))
                                ACT(pad[:, 3 + b0:3 + b0 + bw], ps[:, :bw], AF.Copy)
                            sw0 = PV_SW + gch * 4
                            qc = qkv[ci][:, 0:1024]
                            TS(qc, pad[:, 0:1024], pvl[:, sw0:sw0 + 1], ALU.mult)
                            for i in range(1, 4):
                                STT(qc, pad[:, i:i + 1024], pvl[:, sw0 + i:sw0 + i + 1], qc, ALU.mult, ALU.add)
                            if not has_s:
                                ACT(qpre[:, gch, :], pad[:, 1024:1027], AF.Copy)
                            else:
                                p = small.next()
                                TR(p[:3, :], pad[:, 1024:1027], ident)
                                piece_out(o_short_p[l, :, gch * 128:(gch + 1) * 128], p[:3, :], 3)
                                xs_ = pad[:, 1027:1027 + NS]
                                qs_ = qkv[ci][:, 1024:1024 + NS]
                                st3v = sstT[:, gch, :].v(lambda a: a.rearrange("p (s j) -> p s j", j=3))
                                TS(qs_, xs_, pvl[:, sw0 + 3:sw0 + 4], ALU.mult)
                                for i in range(3):
                                    STT(qs_, st3v[:, :, i], pvl[:, sw0 + i:sw0 + i + 1], qs_, ALU.mult, ALU.add)
                                p = small.next()
                                TR(p[:NS, :], xs_, ident)
                                piece_out(o_short_s[l, :, 2, gch * 128:(gch + 1) * 128], p[:NS, :], NS)
                            ACT(qkv[ci][:, 0:W], qkv[ci][:, 0:W], AF.Silu)
                            if ci < 2:
                                for b, (b0, bw) in enumerate(blocks):
                                    sq = scr.next()
                                    ACT(sq[:, :bw], qkv[ci][:, b0:b0 + bw], AF.Square)
                                    ps = big.next()
                                    MM(ps[:, :bw], ones, sq[:, :bw])
                                    rn = scr.next()
                                    ACT(rn[:, :bw], ps[:, :bw], AF.Ln, bias=eps6)
                                    ACT(rn[:, :bw], rn[:, :bw], AF.Exp, scale=-0.5)
                                    if ci == 0:
                                        STT(qkv[ci][:, b0:b0 + bw], qkv[ci][:, b0:b0 + bw], 128.0 ** -0.5, rn[:, :bw], ALU.mult, ALU.mult)
                                    else:
                                        TT(qkv[ci][:, b0:b0 + bw], qkv[ci][:, b0:b0 + bw], rn[:, :bw], ALU.mult)
                                    if bw != NS:
                                        CP(qkb[ci][:, b0:b0 + bw], qkv[ci][:, b0:b0 + bw], eng="pool")
                        for b, (b0, bw) in enumerate(blocks):
                            ps = big.next()
                            for k in range(KD):
                                MM(ps[:, :bw], wq[3][:, k, :], hT[k][b], start=(k == 0), stop=(k == KD - 1))
                            ACT(zs[:, b0:b0 + bw], ps[:, :bw], AF.Silu)
                        if not has_s:
                            MEMSET(S[h], 0.0)

                        ckpt('dhead')
                        def chunk_gen(c):
                            cs = slice(c * 128, (c + 1) * 128)
                            ix = c * 4 + h
                            col = slice(ix, ix + 1)
                            qTc, kTc, vTc = qkv[0][:, cs], qkv[1][:, cs], qkv[2][:, cs]
                            qbc, kbc = qkb[0][:, cs], qkb[1][:, cs]
                            grep = sm.next()
                            TS(grep, ones, g_c[:, col], ALU.mult)
                            pg = small.next()
                            MM(pg, grep, tri)
                            yield
                            pre1 = sm.next()
                            STT(pre1, pg, -1.0, nm_strict, ALU.mult, ALU.add)
                            pre2 = sm.next()
                            TT(pre2, pg, nm_inclT, ALU.add)
                            egr = sm.next()
                            ACT(egr, pg, AF.Exp)
                            yield
                            Ds = sm.next()
                            ACT(Ds, pre1, AF.Exp, bias=gc_c[:, col])
                            DT = sm.next()
                            ACT(DT, pre2, AF.Exp, bias=ngc_c[:, col])
                            qgc = lgb["qg"].next()
                            TT(qgc, qTc, egr, ALU.mult)
                            yield
                            pk = small.next()
                            MM(pk, kbc, kbc)
                            Nm = sm.next()
                            STT(Nm, pk, nbeta_c[:, col], Ds, ALU.mult, ALU.mult)
                            yield
                            pq = small.next()
                            MM(pq, kbc, qbc)
                            qkT = lgb["qkT"].next()
                            TT(qkT, pq, DT, ALU.mult)
                            yield
                            Nbd = sm.next()
                            TT(Nbd, Nm, bm[:, 0, :], ALU.mult)
                            No = lg["No"].next()
                            TT(No, Nm, bm[:, 1, :], ALU.mult, eng="pool")
                            pt = small.next()
                            TR(pt, Nbd, ident)
                            Mm = sm.next()
                            ACT(Mm, pt, AF.Copy)
                            X = sm.next()
                            TT(X, Mm, ident, ALU.add)
                            yield
                            ptk = small.next()
                            TR(ptk, kTc, ident)
                            kbg = lgb["kbg"].next()
                            ACT(kbg, ptk, AF.Identity, scale=bgc_c[:, col])
                            kdc = lgb["kdc"].next()
                            ACT(kdc, ptk, AF.Identity, scale=kdec_c[:, col])
                            yield
                            ptv = small.next()
                            TR(ptv, vTc, ident)
                            vb = lgb["vb"].next()
                            ACT(vb, ptv, AF.Identity, scale=beta_c[:, col])
                            Pm, Pn = Mm, Nbd
                            for lev in range(1, 6):
                                yield
                                pp2 = small.next()
                                MM(pp2, Pm, Pn)
                                Pn2 = sm.next()
                                ACT(Pn2, pp2, AF.Copy)
                                if lev < 5:
                                    pp1 = small.next()
                                    MM(pp1, Pn, Pm)
                                    Pm2 = sm.next()
                                    CP(Pm2, pp1)
                                yield
                                px = small.next()
                                MM(px, Pn2, X)
                                X2 = sm.next()
                                TT(X2, px, X, ALU.add)
                                X, Pn = X2, Pn2
                                if lev < 5:
                                    Pm = Pm2
                            yield
                            pxt = small.next()
                            TR(pxt, X, ident)
                            Yt = sm.next()
                            ACT(Yt, pxt, AF.Copy)
                            pt1 = small.next()
                            MM(pt1, No, X)
                            T1 = sm.next()
                            CP(T1, pt1)
                            yield
                            pxx = small.next()
                            MM(pxx, Yt, T1)
                            X2 = sm.next()
                            TT(X2, pxx, X, ALU.add)
                            X = X2
                            yield
                            Xb = lgb["Xb"].next()
                            CP(Xb, X, eng="pool")
                            yield
                            pu = small.next()
                            MM(pu, Xb, vb)
                            u = lg["u"].next()
                            ACT(u, pu, AF.Copy)
                            yield
                            pw = small.next()
                            MM(pw, kbg, Xb)
                            wT = lgb["wT"].next()
                            CP(wT, pw)
                            yield
                            Sb = lgb["Sb"].next()
                            CP(Sb, S[h], eng="pool")
                            pvn = small.next()
                            MM(pvn, wT, Sb)
                            vn = lgb["vn"].next()
                            TT(vn, u, pvn, ALU.subtract)
                            yield
                            po = small.next()
                            MM(po, qgc, Sb, start=True, stop=False)
                            MM(po, qkT, vn, start=False, stop=True)
                            yield
                            pS = small.next()
                            MM(pS, kdc, vn)
                            STT(S[h], S[h], egl_c[:, col], pS, ALU.mult, ALU.add)
                            yield
                            osq = sm.next()
                            ssq = scol.next()
                            ACT(osq, po, AF.Square, accum=ssq[:, 0:1])
                            ACT(ssq[:, 0:1], ssq[:, 0:1], AF.Ln, scale=1.0 / 128, bias=eps6)
                            ACT(ssq[:, 0:1], ssq[:, 0:1], AF.Exp, scale=-0.5)
                            yield
                            on = sm.next()
                            ACT(on, po, AF.Identity, scale=ssq[:, 0:1])
                            yield
                            pT = small.next()
                            TR(pT, on, ident)
                            STT(ogT[h][:, cs], pT, pvl[:, PV_DNG:PV_DNG + 1], zs[:, cs], ALU.mult, ALU.mult)


                        pending = list(range(NCH))
                        active = []
                        rnd = 0
                        while pending or active:
                            if pending and len(active) < GCH and (rnd % STAG == 0 or not active):
                                active.append(chunk_gen(pending.pop(0)))
                            for g_ in list(active):
                                try:
                                    next(g_)
                                except StopIteration:
                                    active.remove(g_)
                            rnd += 1
                        ckpt('dchunks')
                        if pi == 0 and l + 1 < n_layers:
                            if h == 0:
                                DMA("sp", pvn, pvec[l + 1, :, 0:64], sem="ldpn")
                            for n2 in range(h * 6, h * 6 + 6):
                                mod_piece(l + 1, n2)
                            if h == 3:
                                mod_finish(l + 1)
                        if has_s:
                            DMA("sp", o_delta_p[l, h], S[h], sem="o_S%d" % h)
                            sc = slice(1024, 1024 + NS)
                            qs, ks, vs = qkv[0][:, sc], qkv[1][:, sc], qkv[2][:, sc]
                            DMA("sp", Ss, st_delta[l, :, h].rearrange("s p v -> p s v"), sem="ldSs")

                            def bcast16(colap):
                                dg = sm.next()
                                TS(dg[:NS, :NS], ident[:NS, :NS], colap, ALU.mult)
                                pB = small.next()
                                MM(pB[:, :NS], ones[:NS, :], dg[:NS, :NS])
                                o_ = scol.next()
                                CP(o_, pB[:, :NS])
                                return o_
                            bbc = bcast16(beta_s[:, h:h + 1])
                            ebc = bcast16(eg_s[:, h:h + 1])
                            t0 = scol.next()
                            TT(t0, qs, ks, ALU.mult)
                            pqk = small.next()
                            MM(pqk[:, :NS], ones, t0)
                            qkbc = scol.next()
                            CP(qkbc, pqk[:, :NS])
                            pkS = small.next()
                            pqS = small.next()
                            for s in range(NS):
                                MM(pkS[:, s:s + 1], Ss[:, s, :], ks[:, s:s + 1])
                            for s in range(NS):
                                MM(pqS[:, s:s + 1], Ss[:, s, :], qs[:, s:s + 1])
                            t1 = scol.next()
                            TT(t1, pkS[:, :NS], ebc, ALU.mult)
                            TT(t1, vs, t1, ALU.subtract)
                            vnT = scol.next()
                            TT(vnT, t1, bbc, ALU.mult)
                            oT = scol.next()
                            TT(oT, pqS[:, :NS], ebc, ALU.mult)
                            t2 = scol.next()
                            TT(t2, qkbc, vnT, ALU.mult)
                            TT(oT, oT, t2, ALU.add)
                            t3 = scol.next()
                            TT(t3, oT, oT, ALU.mult)
                            pss = small.next()
                            MM(pss[:, :NS], ones, t3)
                            rs_ = scol.next()
                            ACT(rs_, pss[:, :NS], AF.Ln, scale=1.0 / 128, bias=eps6)
                            ACT(rs_, rs_, AF.Exp, scale=-0.5)
                            TT(oT, oT, rs_, ALU.mult)
                            STT(ogT[h][:, sc], oT, pvl[:, PV_DNG:PV_DNG + 1], zs[:, sc], ALU.mult, ALU.mult)
                            pkt = small.next()
                            TR(pkt[:NS, :], ks, ident)
                            k_tok = lg["kbg"].next()
                            CP(k_tok[:NS, :], pkt[:NS, :])
                            pvt = small.next()
                            TR(pvt[:NS, :], vnT, ident)
                            vn_tok = lg["kdc"].next()
                            CP(vn_tok[:NS, :], pvt[:NS, :])
                            for s in range(NS):
                                rhs_s = sm.next()
                                TS(rhs_s[:NS, :], vn_tok[:NS, :], ident[:NS, s:s + 1], ALU.mult)
                                po_ = small.next()
                                MM(po_, k_tok[:NS, :], rhs_s[:NS, :])
                                STT(Ss[:, s, :], Ss[:, s, :], ebc[:, s:s + 1], po_, ALU.mult, ALU.add)
                            DMA("sp", o_delta_s[l, :, h].rearrange("s p v -> p s v"), Ss, sem="o_Ss")

                    ckpt('delta')
                    alias_barrier(PH2, PH3)
                    for c2 in range(4):
                        wco = ws.get(wv(w_conf_out[l, :, c2 * 256:(c2 + 1) * 256]), 4, 256)
                        wdo = ws.get(wv(w_delta_out[l, :, c2 * 256:(c2 + 1) * 256]), 4, 256)
                        wma = ws.get(wv(w_in[l, :, O_MA + c2 * 256:O_MA + (c2 + 1) * 256]), KD, 256)
                        wmb = ws.get(wv(w_in[l, :, O_MB + c2 * 256:O_MB + (c2 + 1) * 256]), KD, 256)
                        for c1 in range(2):
                            c = c2 * 2 + c1
                            w0 = c1 * 128
                            for b, (b0, bw) in enumerate(blocks):
                                pya = big.next()
                                pma = big.next()
                                for k in range(4):
                                    MM(pya[:, :bw], wco[:, k, w0:w0 + 128], caT[k][:, b0:b0 + bw], start=(k == 0), stop=(k == 3))
                                for k in range(KD):
                                    MM(pma[:, :bw], wma[:, k, w0:w0 + 128], hT[k][b], start=(k == 0), stop=(k == KD - 1))
                                sa = scr.next()
                                ACT(sa[:, :bw], pma[:, :bw], AF.Sigmoid)
                                TT(sa[:, :bw], pya[:, :bw], sa[:, :bw], ALU.mult)
                                pyb = big.next()
                                pmb = big.next()
                                for k in range(4):
                                    MM(pyb[:, :bw], wdo[:, k, w0:w0 + 128], ogT[k][:, b0:b0 + bw], start=(k == 0), stop=(k == 3))
                                for k in range(KD):
                                    MM(pmb[:, :bw], wmb[:, k, w0:w0 + 128], hT[k][b], start=(k == 0), stop=(k == KD - 1))
                                sb_ = scr.next()
                                ACT(sb_[:, :bw], pmb[:, :bw], AF.Sigmoid)
                                TT(sb_[:, :bw], pyb[:, :bw], sb_[:, :bw], ALU.mult)
                                TT(merged[c][:, b0:b0 + bw], sa[:, :bw], sb_[:, :bw], ALU.add)
                    ckpt('merge')
                    for c2 in range(4):
                        wmo = ws.get(wv(w_merge_out[l, :, c2 * 256:(c2 + 1) * 256]), KD, 256)
                        for c1 in range(2):
                            c = c2 * 2 + c1
                            for b, (b0, bw) in enumerate(blocks):
                                ps = big.next()
                                for k in range(KD):
                                    MM(ps[:, :bw], wmo[:, k, c1 * 128:(c1 + 1) * 128], merged[k][:, b0:b0 + bw], start=(k == 0), stop=(k == KD - 1))
                                resid_update(c, ps, bw, gbs[b], 16)

                    ckpt('mergeout')
                    rmsnorm_mod(blocks, gbs, 8, 24)
                    for fh in range(2):
                        alias_barrier(PH1 + PH2 + PH3 + PH4, PH4)
                        for jj2 in range(6):
                            j0 = fh * 11 + jj2 * 2
                            nj = min(2, fh * 11 + 11 - j0)
                            wg = ws.get(wv(w_ffn_in[l, :, j0 * 128:(j0 + nj) * 128]), KD, nj * 128)
                            wu = ws.get(wv(w_ffn_in[l, :, D_FF + j0 * 128:D_FF + (j0 + nj) * 128]), KD, nj * 128)
                            for j1 in range(nj):
                                jl = jj2 * 2 + j1
                                for b, (b0, bw) in enumerate(blocks):
                                    pgt = big.next()
                                    pup = big.next()
                                    for k in range(KD):
                                        MM(pgt[:, :bw], wg[:, k, j1 * 128:(j1 + 1) * 128], hT[k][b], start=(k == 0), stop=(k == KD - 1))
                                    for k in range(KD):
                                        MM(pup[:, :bw], wu[:, k, j1 * 128:(j1 + 1) * 128], hT[k][b], start=(k == 0), stop=(k == KD - 1))
                                    sg = scr.next()
                                    ACT(sg[:, :bw], pgt[:, :bw], AF.Silu)
                                    TT(actT[jl][:, b0:b0 + bw], sg[:, :bw], pup[:, :bw], ALU.mult)
                        for c in range(KD):
                            wo = ws.get(wv(w_ffn_out[l, fh * 1408:(fh + 1) * 1408, c * 128:(c + 1) * 128]), 11, 128)
                            for b, (b0, bw) in enumerate(blocks):
                                ps = big.next()
                                for j in range(11):
                                    MM(ps[:, :bw], wo[:, j, :], actT[j][:, b0:b0 + bw], start=(j == 0), stop=(j == 10))
                                resid_update(c, ps, bw, gbs[b], 40)

            for gb, (g0, gw) in enumerate(GB):
                ps = big.next()
                for k in range(KD):
                    sq = scr.next()
                    ACT(sq[:, :gw], xT[k][gb], AF.Square)
                    MM(ps[:, :gw], ones, sq[:, :gw], start=(k == 0), stop=(k == KD - 1))
                rs = st1
                ACT(rs[:, :gw], ps[:, :gw], AF.Ln, scale=1.0 / D, bias=eps6)
                ACT(rs[:, :gw], rs[:, :gw], AF.Exp, scale=-0.5)
                nt = max(1, gw // 128)
                tw = min(gw, 128)
                for t in range(nt):
                    o_ = ld[t % 2]
                    for k in range(KD):
                        tmp = sm.next()
                        STT(tmp[:, :tw], xT[k][gb][:, t * 128:t * 128 + tw], pvl[:, PV_FG + k:PV_FG + k + 1], rs[:, t * 128:t * 128 + tw],
                            ALU.mult, ALU.mult)
                        p = small.next()
                        TR(p[:tw, :], tmp[:, :tw], ident)
                        if k % 2 == 0:
                            ACT(o_[:tw, k * 128:(k + 1) * 128], p[:tw, :], AF.Copy)
                        else:
                            CP(o_[:tw, k * 128:(k + 1) * 128], p[:tw, :])
                    if gb < 4:
                        r0 = g0 + t * 128
                        DMA("sp", y_p[r0:r0 + 128, :], o_, sem="o_y%d" % (t % 2))
                    else:
                        DMA("sp", y_s[:, :], o_[:NS, :], sem="o_y%d" % (t % 2))

        def build_w(Pq):
            try:
                build(Pq)
            except StopBuild:
                pass
            for sname in sem_names:
                if (sname.startswith("o_") or sname.startswith("w") or sname.startswith("ld")) and Pq.cnt.get(sname, 0) > 0:
                    Pq.q["sp"].append(("wait", sname, Pq.cnt[sname]))
        P0 = Prog()
        build_w(P0)
        ws.recording = False
        P = Prog()
        build_w(P)
        print("ops:", P.nops, {k: len(v) for k, v in P.q.items()}, "weights:", len(ws.specs))

        semh = {n: es.enter_context(nc.semaphore(n)) for n in sem_names}
        block = es.enter_context(nc.Block())

        def run(eng, items):
            for it in items:
                if it[0] == "wait":
                    eng.wait_ge(semh[it[1]], it[2])
                else:
                    it[1](eng).then_inc(semh[it[2]], it[3])

        @block.tensor
        def _(eng):
            run(eng, P.q["pe"])

        @block.scalar
        def _(eng):
            run(eng, P.q["act"])

        @block.vector
        def _(eng):
            run(eng, P.q["dve"])

        @block.gpsimd
        def _(eng):
            run(eng, P.q["pool"])

        @block.sync
        def _(eng):
            run(eng, P.q["sp"])
    return nc


def make_consts():
    c = np.zeros((128, 5, 128), np.float32)
    i = np.arange(128)
    c[:, 0, :] = np.eye(128, dtype=np.float32)
    c[:, 1, :] = 1.0
    c[:, 2, :] = (i[:, None] <= i[None, :]).astype(np.float32)
    c[:, 3, :] = np.where(i[:, None] > i[None, :], 0.0, NEG).astype(np.float32)
    c[:, 4, :] = np.where(i[None, :] >= i[:, None], 0.0, NEG).astype(np.float32)
    return c


def make_bmask():
    i = np.arange(128)
    b64 = (i[:, None] // 64 == i[None, :] // 64)
    m = np.zeros((128, 2, 128), np.float32)
    m[:, 0, :] = b64
    m[:, 1, :] = ~b64
    return m


def make_pvec(inp):
    pv = np.zeros((DEPTH, 128, PV_N), np.float32)
    for l in range(DEPTH):
        pv[l, :, PV_G1:PV_G1 + 8] = inp["norm1_g"][l].reshape(8, 128).T
        pv[l, :, PV_G2:PV_G2 + 8] = inp["norm2_g"][l].reshape(8, 128).T
        pv[l, :, PV_BADA:PV_BADA + 48] = inp["b_ada"][l].reshape(48, 128).T
        cw = inp["conf_dw_w"][l]
        for j in range(4):
            pv[l, :, PV_CW + j * 31:PV_CW + (j + 1) * 31] = cw[:, j * 128:(j + 1) * 128].T
        pv[l, :, PV_CB:PV_CB + 4] = inp["conf_dw_b"][l].reshape(4, 128).T
        pv[l, :, PV_LNG:PV_LNG + 4] = inp["conf_ln_g"][l].reshape(4, 128).T
        pv[l, :, PV_LNB:PV_LNB + 4] = inp["conf_ln_b"][l].reshape(4, 128).T
        sw = inp["short_conv_w"][l]
        for g in range(12):
            pv[l, :, PV_SW + g * 4:PV_SW + (g + 1) * 4] = sw[:, g * 128:(g + 1) * 128].T
        pv[l, :, PV_DNG] = inp["delta_norm_g"][l]
        pv[l, :, PV_ALOG:PV_ALOG + 32] = np.tile(inp["a_log"][l], 8)[None, :]
        pv[l, :, PV_DTB:PV_DTB + 32] = np.tile(inp["dt_bias"][l], 8)[None, :]
        pv[l, :, PV_FG:PV_FG + 8] = inp["final_norm_g"].reshape(8, 128).T
    return pv


_NC_CACHE = {}


def kernel(**inp):
    inp = {k: np.asarray(v) for k, v in inp.items()}
    n = 8
    if "nc" not in _NC_CACHE:
        _NC_CACHE["nc"] = build_nc()
    nc = _NC_CACHE["nc"]
    consts = make_consts()
    bmask_np = make_bmask()
    pv = make_pvec(inp)
    shared = {k: np.ascontiguousarray(inp[k], dtype=np.float32) for k in
              ("w_ada", "w_in", "w_conf_out", "w_delta_out", "w_merge_out", "w_ffn_in", "w_ffn_out")}
    in_maps = []
    for i in range(n):
        s0 = i * NS
        m = dict(shared)
        m["x_p"] = np.ascontiguousarray(inp["x_prompt"][i])
        m["x_s"] = np.ascontiguousarray(inp["x_sample"][s0:s0 + NS, 0, :])
        m["c_in"] = np.ascontiguousarray(np.concatenate([inp["c_prompt"][i:i + 1], inp["c_sample"][s0:s0 + NS]], axis=0))
        m["st_conf"] = np.ascontiguousarray(inp["state_conformer_conv"][:, s0:s0 + NS])
        m["st_short"] = np.ascontiguousarray(inp["state_short_conv"][:, s0:s0 + NS])
        m["st_delta"] = np.ascontiguousarray(inp["state_delta"][:, s0:s0 + NS])
        m["pvec"] = pv
        m["consts"] = consts
        m["bmask"] = bmask_np
        in_maps.append(m)
    res = run_bass_kernel_spmd(nc, in_maps, core_ids=list(range(n)))
    R = res.results
    y_p = np.stack([R[i]["y_p"] for i in range(n)], axis=0)
    y_s = np.concatenate([R[i]["y_s"] for i in range(n)], axis=0)[:, None, :]
    conf_p = np.stack([R[i]["o_conf_p"] for i in range(n)], axis=1)
    conf_s = np.concatenate([R[i]["o_conf_s"] for i in range(n)], axis=1)
    short_p = np.stack([R[i]["o_short_p"] for i in range(n)], axis=1)
    short_s = np.concatenate([R[i]["o_short_s"] for i in range(n)], axis=1)
    delta_p = np.stack([R[i]["o_delta_p"] for i in range(n)], axis=1)
    delta_s = np.concatenate([R[i]["o_delta_s"] for i in range(n)], axis=1)
    return tuple(np.ascontiguousarray(a, dtype=np.float32) for a in
                 (y_p, y_s, conf_p, conf_s, short_p, short_s, delta_p, delta_s))
```

```python
import numpy as np
from contextlib import ExitStack
import concourse.bass as bass
import concourse.mybir as mybir
from concourse.bass_utils import run_bass_kernel_spmd

F32 = mybir.dt.float32
BF16 = mybir.dt.bfloat16
AF = mybir.ActivationFunctionType
ALU = mybir.AluOpType
AX = mybir.AxisListType

DEPTH = 4
D = 1024
KD = 8
SEQ = 2048
NS = 16
NTOK = SEQ + NS
CONF_CH = 512
CONF_KW = 31
D_FF = 2816
KF = 22
O_GLU_B = 512
O_Q = 1024
O_K = 1536
O_V = 2048
O_Z = 2560
O_BETA = 3072
O_MA = 3080
O_MB = 4104
IN_DIM = 5128
NEG = -1.0e6

SAME_ENGINE_SYNC = True
GCH = 2
STAG = 15
N_LAYERS = DEPTH

PV_G1 = 0
PV_G2 = 8
PV_BADA = 16
PV_CW = 64
PV_CB = 188
PV_LNG = 192
PV_LNB = 196
PV_SW = 200
PV_DNG = 248
PV_ALOG = 249
PV_DTB = 281
PV_FG = 313
PV_N = 321


class Buf:
    __slots__ = ("name", "w", "r", "psum")

    def __init__(self, name="", psum=False):
        self.name = name
        self.w = None
        self.r = {}
        self.psum = psum


class T:
    __slots__ = ("ap", "bufs")

    def __init__(self, ap, bufs=None):
        self.ap = ap
        if bufs is None:
            bufs = [Buf()]
        elif isinstance(bufs, Buf):
            bufs = [bufs]
        self.bufs = bufs

    def __getitem__(self, idx):
        return T(self.ap[idx], self.bufs)

    def v(self, fn):
        return T(fn(self.ap), self.bufs)


def _ap(x):
    return x.ap if isinstance(x, T) else x


def _bufs(*xs):
    out = []
    for x in xs:
        if isinstance(x, T):
            out.extend(x.bufs)
    return out


class Ring:
    def __init__(self, tiles):
        self.tiles = tiles
        self.i = 0

    def next(self):
        t = self.tiles[self.i % len(self.tiles)]
        self.i += 1
        return t


class Prog:
    ENGS = ("pe", "act", "dve", "pool", "sp")

    def __init__(self):
        self.q = {n: [] for n in self.ENGS}
        self.cnt = {}
        self.seen = {n: {} for n in self.ENGS}
        self.nops = 0

    def op(self, e, fn, reads=(), writes=(), sem=None, inc=1):
        deps = {}
        for b in reads:
            if b.w is not None:
                if deps.get(b.w[0], 0) < b.w[1]:
                    deps[b.w[0]] = b.w[1]
            if b.psum:
                for s, v in b.r.items():
                    if s != e and deps.get(s, 0) < v:
                        deps[s] = v
        for b in writes:
            if b.w is not None:
                if deps.get(b.w[0], 0) < b.w[1]:
                    deps[b.w[0]] = b.w[1]
            for s, v in b.r.items():
                if deps.get(s, 0) < v:
                    deps[s] = v
        seen = self.seen[e]
        q = self.q[e]
        for s, v in deps.items():
            if s == e and (e == "pe" or not SAME_ENGINE_SYNC):
                continue
            if seen.get(s, 0) >= v:
                continue
            seen[s] = v
            q.append(("wait", s, v))
        sname = sem or e
        val = self.cnt.get(sname, 0) + inc
        self.cnt[sname] = val
        q.append(("op", fn, sname, inc))
        for b in reads:
            if b.r.get(sname, 0) < val:
                b.r[sname] = val
        for b in writes:
            b.w = (sname, val)
            b.r = {}
        self.nops += 1
        return val

    def MM(self, out, lhsT, rhs, start=True, stop=True):
        o, l, r = _ap(out), _ap(lhsT), _ap(rhs)
        self.op("pe", lambda e: e.matmul(o, lhsT=l, rhs=r, start=start, stop=stop), _bufs(lhsT, rhs), _bufs(out))

    def TR(self, out, in_, ident):
        o, i, d = _ap(out), _ap(in_), _ap(ident)
        self.op("pe", lambda e: e.transpose(o, i, d), _bufs(in_, ident), _bufs(out))

    def ACT(self, out, in_, func, bias=None, scale=None, accum=None):
        o, i = _ap(out), _ap(in_)
        kw = {}
        if bias is None and isinstance(scale, T):
            bias = self.zeroc[0:o.shape[0], :] if o.shape[0] != 128 else self.zeroc
        if bias is not None:
            kw["bias"] = _ap(bias)
        if scale is not None:
            kw["scale"] = _ap(scale)
        if accum is not None:
            kw["accum_out"] = _ap(accum)
        self.op("act", lambda e: e.activation(out=o, in_=i, func=func, **kw), _bufs(in_, bias, scale), _bufs(out, accum))

    def TT(self, out, in0, in1, op, eng="dve"):
        o, a, b = _ap(out), _ap(in0), _ap(in1)
        self.op(eng, lambda e: e.tensor_tensor(out=o, in0=a, in1=b, op=op), _bufs(in0, in1), _bufs(out))

    def TS(self, out, in0, s1, op0, s2=None, op1=None, eng="dve"):
        o, a, x1, x2 = _ap(out), _ap(in0), _ap(s1), _ap(s2)
        if op1 is None:
            self.op(eng, lambda e: e.tensor_scalar(out=o, in0=a, scalar1=x1, scalar2=None, op0=op0), _bufs(in0, s1), _bufs(out))
        else:
            self.op(eng, lambda e: e.tensor_scalar(out=o, in0=a, scalar1=x1, scalar2=x2, op0=op0, op1=op1), _bufs(in0, s1, s2), _bufs(out))

    def STT(self, out, in0, scalar, in1, op0, op1, eng="dve"):
        o, a, s, b = _ap(out), _ap(in0), _ap(scalar), _ap(in1)
        self.op(eng, lambda e: e.scalar_tensor_tensor(out=o, in0=a, scalar=s, in1=b, op0=op0, op1=op1), _bufs(in0, scalar, in1), _bufs(out))

    def CP(self, out, in_, eng="dve"):
        o, i = _ap(out), _ap(in_)
        self.op(eng, lambda e: e.tensor_copy(out=o, in_=i), _bufs(in_), _bufs(out))

    def RED(self, out, in_, op=ALU.add, eng="dve"):
        o, i = _ap(out), _ap(in_)
        self.op(eng, lambda e: e.tensor_reduce(out=o, in_=i, axis=AX.X, op=op), _bufs(in_), _bufs(out))

    def RECIP(self, out, in_):
        o, i = _ap(out), _ap(in_)
        self.op("dve", lambda e: e.reciprocal(out=o, in_=i), _bufs(in_), _bufs(out))

    def MEMSET(self, out, val, eng="dve"):
        o = _ap(out)
        self.op(eng, lambda e: e.memset(o, val), (), _bufs(out))

    def DMA(self, q, out, in_, sem, **kw):
        o, i = _ap(out), _ap(in_)
        self.op(q, lambda e: e.dma_start(out=o, in_=i, **kw), _bufs(in_), _bufs(out), sem=sem, inc=16)


class WStream:
    NSLOT = 6
    LAG = 4
    ELEMS = 2048

    def __init__(self):
        self.specs = []
        self.recording = True

    def begin(self, P, slot_aps):
        self.P = P
        self.i = 0
        self.loaded = 0
        self.slots = [T(a) for a in slot_aps]

    def _view(self, j):
        _, kc, n = self.specs[j]
        t = self.slots[j % self.NSLOT]
        return T(t.ap[:, 0:kc * n].rearrange("p (k n) -> p k n", k=kc), t.bufs)

    def get(self, dram_ap, kc, n):
        assert kc * n <= self.ELEMS
        i = self.i
        self.i += 1
        if self.recording:
            self.specs.append((dram_ap, kc, n))
            return self._view(i)
        while self.loaded < len(self.specs) and (self.loaded < self.NSLOT or self.loaded <= i - self.LAG + self.NSLOT):
            j = self.loaded
            self.P.DMA("pool", self._view(j), self.specs[j][0], sem="w%d" % (j % self.NSLOT))
            self.loaded += 1
        assert self.loaded > i
        return self._view(i)


class StopBuild(Exception):
    pass


STOP_AT = [None]


def ckpt(name):
    if STOP_AT[0] == name:
        raise StopBuild(name)


def alias_barrier(olds, news):
    acc = {}
    for t in olds:
        for b in t.bufs:
            if b.w is not None and acc.get(b.w[0], 0) < b.w[1]:
                acc[b.w[0]] = b.w[1]
            for s, v in b.r.items():
                if acc.get(s, 0) < v:
                    acc[s] = v
    for t in news:
        for b in t.bufs:
            for s, v in acc.items():
                if b.r.get(s, 0) < v:
                    b.r[s] = v


def build_nc(n_layers=N_LAYERS):
    nc = bass.Bass("TRN2", target_bir_lowering=False)

    def din(name, shape):
        return nc.dram_tensor(name, list(shape), F32, kind="ExternalInput").ap()

    def dout(name, shape):
        return nc.dram_tensor(name, list(shape), F32, kind="ExternalOutput").ap()

    x_p = din("x_p", [SEQ, D])
    x_s = din("x_s", [NS, D])
    c_in = din("c_in", [NS + 1, D])
    st_conf = din("st_conf", [DEPTH, NS, 30, CONF_CH])
    st_short = din("st_short", [DEPTH, NS, 3, 1536])
    st_delta = din("st_delta", [DEPTH, NS, 4, 128, 128])
    w_ada = din("w_ada", [DEPTH, D, 6 * D])
    w_in = din("w_in", [DEPTH, D, IN_DIM])
    w_conf_out = din("w_conf_out", [DEPTH, CONF_CH, D])
    w_delta_out = din("w_delta_out", [DEPTH, 512, D])
    w_merge_out = din("w_merge_out", [DEPTH, D, D])
    w_ffn_in = din("w_ffn_in", [DEPTH, D, 2 * D_FF])
    w_ffn_out = din("w_ffn_out", [DEPTH, D_FF, D])
    pvec = din("pvec", [DEPTH, 128, PV_N])
    consts = din("consts", [128, 5, 128])
    bmask = din("bmask", [128, 3, 128])

    y_p = dout("y_p", [SEQ, D])
    y_s = dout("y_s", [NS, D])
    o_conf_p = dout("o_conf_p", [DEPTH, 30, CONF_CH])
    o_conf_s = dout("o_conf_s", [DEPTH, NS, 30, CONF_CH])
    o_short_p = dout("o_short_p", [DEPTH, 3, 1536])
    o_short_s = dout("o_short_s", [DEPTH, NS, 3, 1536])
    o_delta_p = dout("o_delta_p", [DEPTH, 4, 128, 128])
    o_delta_s = dout("o_delta_s", [DEPTH, NS, 4, 128, 128])

    with ExitStack() as es:
        def sb(name, shape, dt=F32):
            return es.enter_context(nc.sbuf_tensor(name, list(shape), dt))

        xT_t = sb("xT", [128, KD, NTOK])
        hT_t = sb("hT", [128, KD, 1040], BF16)
        A_CA = 1070
        A_QKV = 1043
        A_ZS = A_QKV + 3 * 1040
        R_WORDS = 1070 + 4 * 1040
        A_CAT = R_WORDS
        A_OGT = A_CAT + 2080
        ARENA_WORDS = A_OGT + 2080
        assert A_ZS + 1040 <= R_WORDS and 11 * 520 <= A_OGT
        arena = sb("arena", [128, ARENA_WORDS])
        wslots = [sb("wslot%d" % i, [128, WStream.ELEMS], BF16) for i in range(WStream.NSLOT)]
        scr_t = [sb("scr%d" % i, [128, 512]) for i in range(4)]
        st_t = [sb("st%d" % i, [128, 512]) for i in range(3)]
        sm_t = [sb("sm%d" % i, [128, 128]) for i in range(16)]
        LGN = ("qkT", "kdc", "kbg", "vb", "u", "wT", "vn", "qg", "No")
        lg_t = {n: [sb("%s%d" % (n, i), [128, 128]) for i in range(2)] for n in LGN}
        S_t = [sb("S%d" % h, [128, 128]) for h in range(4)]
        Ss_t = sb("Ss", [128, NS * 128])
        cols_t = sb("cols", [128, 12, 32])
        scol_t = sb("scol", [128, 12, 16])
        modT_t2 = [sb("modT%d" % i, [128, 48, NS + 1]) for i in range(2)]
        gs_t2 = [sb("gs%d" % i, [128, 16, NS + 1]) for i in range(2)]
        pvn_t = sb("pvn", [128, 64])
        cT_t = sb("cT", [128, KD, NS + 1], BF16)
        pv_t = sb("pv", [128, PV_N])
        negA_t = sb("negA", [128, 32])
        cst_t = sb("cst", [128, 6, 128])
        upre_t = sb("upre", [128, 4, 30])
        qpre_t = sb("qpre", [128, 12, 3])
        sstT_t = sb("sstT", [128, 12, 48])
        pc_t = [sb("pc%d" % i, [NS, 128]) for i in range(4)]
        qkb_t = sb("qkb", [128, 2, 1040], BF16)
        bm_t = sb("bmask_sb", [128, 3, 128], BF16)
        ps_t = [es.enter_context(nc.psum_tensor("ps%d" % i, [128, 512], F32)) for i in range(8)]
        print("SBUF bytes remaining per partition:", nc.sbuf_bytes_remaining)

        sem_names = ["pe", "act", "dve", "pool"] + ["w%d" % i for i in range(WStream.NSLOT)] + \
            ["ld0", "ld1", "ldc", "ldp", "ldpn", "ldbm", "ldcs", "ldss", "ldSs", "o_cpo", "o_cso", "o_pc0", "o_pc1", "o_pc2", "o_pc3",
             "o_S0", "o_S1", "o_S2", "o_S3", "o_Ss", "o_y0", "o_y1", "o_d2d"]

        ws = WStream()

        def build(P):
            ws.begin(P, [w[:] for w in wslots])
            MM, TR, ACT, TT, TS, STT, CP, RED, RECIP, MEMSET, DMA = (P.MM, P.TR, P.ACT, P.TT, P.TS, P.STT, P.CP, P.RED,
                                                                     P.RECIP, P.MEMSET, P.DMA)
            GB = [(0, 512), (512, 512), (1024, 512), (1536, 512), (2048, NS)]
            xT = [[T(xT_t[:, k, b0:b0 + bw]) for (b0, bw) in GB] for k in range(KD)]
            LB = [(0, 512), (512, 512), (1024, NS)]
            hT = [[T(hT_t[:, k, b0:b0 + bw]) for (b0, bw) in LB] for k in range(KD)]
            ar = arena
            upad = T(ar[:, 0:1070])
            ca = [T(ar[:, A_CA + j * 1040:A_CA + (j + 1) * 1040]) for j in range(4)]
            pad = T(ar[:, 0:1043])
            qkv = [T(ar[:, A_QKV + c * 1040:A_QKV + (c + 1) * 1040]) for c in range(3)]
            zs = T(ar[:, A_ZS:A_ZS + 1040])
            merged_v = ar[:, 0:4160].bitcast(BF16).rearrange("p (k n) -> p k n", k=8)
            merged = [T(merged_v[:, c, :]) for c in range(8)]
            caT_v = ar[:, A_CAT:A_CAT + 2080].bitcast(BF16).rearrange("p (k n) -> p k n", k=4)
            caT = [T(caT_v[:, j, :]) for j in range(4)]
            ogT_v = ar[:, A_OGT:A_OGT + 2080].bitcast(BF16).rearrange("p (k n) -> p k n", k=4)
            ogT = [T(ogT_v[:, h, :]) for h in range(4)]
            act_v = ar[:, 0:11 * 520].bitcast(BF16).rearrange("p (k n) -> p k n", k=11)
            actT = [T(act_v[:, j, :]) for j in range(11)]
            cs_tok = T(ar[:120, A_OGT:A_OGT + 2048].rearrange("p (r c) -> p r c", r=4))
            PH1 = [upad] + ca + caT + [cs_tok]
            PH2 = [pad] + qkv + [zs] + ogT
            PH3 = merged
            PH4 = actT

            scr = Ring([T(t[:]) for t in scr_t])
            st1, st2, st3 = [T(t[:]) for t in st_t]
            cpo = T(st_t[0][:30, :], st1.bufs)
            cso = T(st_t[2][:NS, :], st3.bufs)
            cstT = T(st_t[1][:, 0:NS * 31].rearrange("p (s j) -> p s j", j=31), st2.bufs)
            sm = Ring([T(t[:]) for t in sm_t])
            lg = {n: Ring([T(t[:]) for t in v]) for n, v in lg_t.items()}
            lgb = {n: Ring([T(t[:, 0:64].bitcast(BF16)) for t in v]) for n, v in lg_t.items()}
            lgb["Xb"] = Ring([T(t[:, 64:128].bitcast(BF16)) for t in lg_t["qkT"]])
            lgb["Sb"] = Ring([T(t[:, 64:128].bitcast(BF16)) for t in lg_t["kdc"]])
            qkb = [T(qkb_t[:, i, :]) for i in range(2)]
            S = [T(t[:]) for t in S_t]
            Ss = T(Ss_t[:].rearrange("p (s v) -> p s v", v=128))
            ld = [T(Ss_t[:, i * 1024:(i + 1) * 1024], Ss.bufs) for i in range(2)]
            ss_tok = T(Ss_t[:48, 0:1536], Ss.bufs)
            big = Ring([T(ps_t[i][:], Buf("pb%d" % i, True)) for i in range(4)])
            sbank = [Buf("psb%d" % i, True) for i in range(4)]
            small = Ring([T(ps_t[4 + i % 4][:, (i // 4) * 128:(i // 4 + 1) * 128], sbank[i % 4]) for i in range(16)])
            colsv = [T(cols_t[:, i, :]) for i in range(12)]
            beta_c, g_c, gc_c, ngc_c, egc_c, bgc_c, nbeta_c, kdec_c, egl_c, tmp_c, tmp2_c, colr = colsv
            scol = Ring([T(scol_t[:, i, :]) for i in range(12)])
            modTs = [T(t[:]) for t in modT_t2]
            gss = [T(t[:]) for t in gs_t2]
            pvn = T(pvn_t[:])
            cur = {}
            cT = T(cT_t[:])
            pvl = T(pv_t[:])
            negA = T(negA_t[:])
            cst = T(cst_t[:, 0:5, :])
            ident = cst[:, 0, :]
            ones = cst[:, 1, :]
            tri = cst[:, 2, :]
            nm_strict = cst[:, 3, :]
            nm_inclT = cst[:, 4, :]
            eps6 = T(cst_t[:, 5, 0:1])
            eps5 = T(cst_t[:, 5, 1:2])
            onec = T(cst_t[:, 5, 2:3])
            zeroc = T(cst_t[:, 5, 3:4])
            P.zeroc = zeroc
            upre = T(upre_t[:])
            qpre = T(qpre_t[:])
            sstT = T(sstT_t[:])
            pcs = [T(t[:]) for t in pc_t]
            pc_i = [0]

            def piece_out(dram_ap, psrc, rows):
                i = pc_i[0] % 4
                pc_i[0] += 1
                CP(pcs[i][:rows, :], psrc)
                DMA("sp", dram_ap, pcs[i][:rows, :], sem="o_pc%d" % i)

            DMA("sp", cst, consts[:, :, :], sem="ldc")
            bm = T(bm_t[:])
            DMA("pool", bm, bmask[:, :, :], sem="ldbm")
            ones_b = bm[:, 2, :]

            def sqb_tile():
                t_ = scr.next()
                return T(t_.ap[:, 0:256].bitcast(BF16), t_.bufs)
            MEMSET(eps6, 1e-6)
            MEMSET(eps5, 1e-5)
            MEMSET(onec, 1.0)
            MEMSET(zeroc, 0.0)

            for t in range(SEQ // 128):
                l_ = ld[t % 2]
                DMA("sp", l_, x_p[t * 128:(t + 1) * 128, :], sem="ld%d" % (t % 2))
                gb = t // 4
                for k in range(KD):
                    p = small.next()
                    TR(p, l_[:, k * 128:(k + 1) * 128], ident)
                    dst = T(xT_t[:, k, t * 128:(t + 1) * 128], xT[k][gb].bufs)
                    if k % 2 == 0:
                        ACT(dst, p, AF.Copy)
                    else:
                        CP(dst, p)
            l_ = ld[0]
            DMA("sp", l_[:NS, :], x_s[:, :], sem="ld0")
            for k in range(KD):
                p = small.next()
                TR(p[:, :NS], l_[:NS, k * 128:(k + 1) * 128], ident[:NS, :NS])
                CP(xT[k][4], p[:, :NS])
            l_ = ld[1]
            DMA("sp", l_[:NS + 1, :], c_in[:, :], sem="ld1")
            ACT(l_[:NS + 1, :], l_[:NS + 1, :], AF.Silu)
            for k in range(KD):
                p = small.next()
                TR(p[:, :NS + 1], l_[:NS + 1, k * 128:(k + 1) * 128], ident[:NS + 1, :NS + 1])
                CP(cT[:, k, :], p[:, :NS + 1])

            ckpt('init')
            def rmsnorm_mod(blocks, gbs, gsoff, shoff):
                for b, (b0, bw) in enumerate(blocks):
                    gb = gbs[b]
                    ps = big.next()
                    for k in range(KD):
                        sq = sqb_tile()
                        ACT(sq[:, :bw], xT[k][gb], AF.Square)
                        MM(ps[:, :bw], ones_b, sq[:, :bw], start=(k == 0), stop=(k == KD - 1))
                    rs = st1
                    ACT(rs[:, :bw], ps[:, :bw], AF.Ln, scale=1.0 / D, bias=eps6)
                    ACT(rs[:, :bw], rs[:, :bw], AF.Exp, scale=-0.5)
                    for k in range(KD):
                        tmp = scr.next()
                        TT(tmp[:, :bw], xT[k][gb], rs[:, :bw], ALU.mult)
                        if bw == NS:
                            TT(tmp[:, :bw], tmp[:, :bw], gs[:, gsoff + k, 1:NS + 1], ALU.mult)
                            TT(hT[k][b], tmp[:, :bw], modT[:, shoff + k, 1:NS + 1], ALU.add)
                        else:
                            ACT(hT[k][b], tmp[:, :bw], AF.Identity, scale=gs[:, gsoff + k, 0:1], bias=modT[:, shoff + k, 0:1])

            def resid_update(c, ps, bw, gb, gtoff):
                if bw == NS:
                    tmp = scr.next()
                    TT(tmp[:, :bw], ps[:, :bw], modT[:, gtoff + c, 1:NS + 1], ALU.mult)
                    TT(xT[c][gb], xT[c][gb], tmp[:, :bw], ALU.add)
                else:
                    STT(xT[c][gb], ps[:, :bw], modT[:, gtoff + c, 0:1], xT[c][gb], ALU.mult, ALU.add)

            def wv(ap2d):
                return ap2d.rearrange("(k p) n -> p k n", p=128)

            def mod_piece(ll, n2):
                mT = modTs[ll % 2]
                w = ws.get(wv(w_ada[ll, :, n2 * 256:(n2 + 1) * 256]), KD, 256)
                for half in range(2):
                    n = n2 * 2 + half
                    p = small.next()
                    for k in range(KD):
                        MM(p[:, :NS + 1], w[:, k, half * 128:(half + 1) * 128], cT[:, k, :], start=(k == 0), stop=(k == KD - 1))
                    ACT(mT[:, n, :], p[:, :NS + 1], AF.Identity, bias=pvn[:, PV_BADA + n:PV_BADA + n + 1])

            def mod_finish(ll):
                mT, g_ = modTs[ll % 2], gss[ll % 2]
                for k in range(KD):
                    TS(g_[:, k, :], mT[:, 8 + k, :], 1.0, ALU.add, pvn[:, PV_G1 + k:PV_G1 + k + 1], ALU.mult)
                    TS(g_[:, 8 + k, :], mT[:, 32 + k, :], 1.0, ALU.add, pvn[:, PV_G2 + k:PV_G2 + k + 1], ALU.mult)

            for l in range(n_layers):
                DMA("sp", pvl, pvec[l], sem="ldp")
                modT, gs = modTs[l % 2], gss[l % 2]
                cur["modT"], cur["gs"] = modT, gs
                if l == 0:
                    DMA("sp", pvn, pvec[0, :, 0:64], sem="ldpn")
                    for n2 in range(24):
                        mod_piece(0, n2)
                    mod_finish(0)
                ACT(negA, pvl[:, PV_ALOG:PV_ALOG + 32], AF.Exp)
                ACT(negA, negA, AF.Identity, scale=-1.0)
                ckpt('mod')
                DMA("sp", o_conf_s[l, :, 0:29, :], st_conf[l, :, 1:30, :], sem="o_d2d")
                DMA("sp", o_short_s[l, :, 0:2, :], st_short[l, :, 1:3, :], sem="o_d2d")
                DMA("sp", ss_tok, st_short[l].rearrange("s j c -> (s j) c"), sem="ldss")
                for gch in range(12):
                    p = small.next()
                    TR(p[:, :48], ss_tok[:48, gch * 128:(gch + 1) * 128], ident[:48, :48])
                    CP(sstT[:, gch, :], p[:, :48])

                ckpt('lstart')
                for pi in range(2):
                    blocks = LB[:2] if pi == 0 else LB
                    gbs = [0, 1] if pi == 0 else [2, 3, 4]
                    has_s = pi == 1
                    W = 1024 + (NS if has_s else 0)
                    NCH = 8

                    rmsnorm_mod(blocks, gbs, 0, 0)

                    ckpt('norm1')
                    alias_barrier(PH2 + PH3 + PH4, PH1)
                    if has_s:
                        DMA("sp", cs_tok, st_conf[l].rearrange("(r s) j c -> (s j) r c", r=4), sem="ldcs")
                    else:
                        MEMSET(upad[:, 0:30], 0.0)
                    for j in range(4):
                        if has_s:
                            CP(upad[:, 0:30], upre[:, j, :])
                        wa = ws.get(wv(w_in[l, :, j * 128:(j + 1) * 128]), KD, 128)
                        wb = ws.get(wv(w_in[l, :, O_GLU_B + j * 128:O_GLU_B + (j + 1) * 128]), KD, 128)
                        for b, (b0, bw) in enumerate(blocks):
                            pa = big.next()
                            pb = big.next()
                            for k in range(KD):
                                MM(pa[:, :bw], wa[:, k, :], hT[k][b], start=(k == 0), stop=(k == KD - 1))
                            for k in range(KD):
                                MM(pb[:, :bw], wb[:, k, :], hT[k][b], start=(k == 0), stop=(k == KD - 1))
                            sg = scr.next()
                            ACT(sg[:, :bw], pb[:, :bw], AF.Sigmoid)
                            TT(upad[:, 30 + b0:30 + b0 + bw], pa[:, :bw], sg[:, :bw], ALU.mult)
                        cw0 = PV_CW + j * 31
                        caj = ca[j][:, 0:1024]
                        TS(caj, upad[:, 0:1024], pvl[:, cw0:cw0 + 1], ALU.mult, pvl[:, PV_CB + j:PV_CB + j + 1], ALU.add)
                        for i in range(1, CONF_KW):
                            STT(caj, upad[:, i:i + 1024], pvl[:, cw0 + i:cw0 + i + 1], caj, ALU.mult, ALU.add)
                        if not has_s:
                            ACT(upre[:, j, :], upad[:, 1024:1054], AF.Copy)
                        else:
                            p = small.next()
                            TR(p[:30, :], upad[:, 1024:1054], ident)
                            CP(cpo[:30, j * 128:(j + 1) * 128], p[:30, :])
                            us_ = upad[:, 30 + 1024:30 + 1024 + NS]
                            for rt in range(4):
                                p = small.next()
                                TR(p[:, :120], cs_tok[:120, rt, j * 128:(j + 1) * 128], ident[:120, :120])
                                CP(cstT[:, 4 * rt:4 * rt + 4, 0:30], p[:, :120].v(lambda a: a.rearrange("p (s j) -> p s j", j=30)))
                            CP(cstT[:, :, 30], us_)
                            prod = scr.next()[:, 0:NS * 31].v(lambda a: a.rearrange("p (s j) -> p s j", j=31))
                            TT(prod, cstT, pvl[:, cw0:cw0 + 31].v(lambda a: a.unsqueeze(1).to_broadcast([128, NS, 31])), ALU.mult)
                            RED(ca[j][:, 1024:1024 + NS], prod)
                            TS(ca[j][:, 1024:1024 + NS], ca[j][:, 1024:1024 + NS], pvl[:, PV_CB + j:PV_CB + j + 1], ALU.add)
                            p = small.next()
                            TR(p[:NS, :], us_, ident)
                            CP(cso[:NS, j * 128:(j + 1) * 128], p[:NS, :])
                    if has_s:
                        DMA("sp", o_conf_p[l], cpo, sem="o_cpo")
                        DMA("sp", o_conf_s[l, :, 29, :], cso, sem="o_cso")
                    ckpt('conf')
                    for b, (b0, bw) in enumerate(blocks):
                        p1 = big.next()
                        p2 = big.next()
                        for j in range(4):
                            MM(p1[:, :bw], ones, ca[j][:, b0:b0 + bw], start=(j == 0), stop=(j == 3))
                        for j in range(4):
                            sq = scr.next()
                            ACT(sq[:, :bw], ca[j][:, b0:b0 + bw], AF.Square)
                            MM(p2[:, :bw], ones, sq[:, :bw], start=(j == 0), stop=(j == 3))
                        mean, msq, var = st1, st2, st3
                        ACT(mean[:, :bw], p1[:, :bw], AF.Identity, scale=1.0 / CONF_CH)
                        TT(msq[:, :bw], mean[:, :bw], mean[:, :bw], ALU.mult)
                        STT(var[:, :bw], p2[:, :bw], 1.0 / CONF_CH, msq[:, :bw], ALU.mult, ALU.subtract)
                        ACT(var[:, :bw], var[:, :bw], AF.Ln, bias=eps5)
                        ACT(var[:, :bw], var[:, :bw], AF.Exp, scale=-0.5)
                        for j in range(4):
                            t = scr.next()
                            TT(t[:, :bw], ca[j][:, b0:b0 + bw], mean[:, :bw], ALU.subtract)
                            TT(t[:, :bw], t[:, :bw], var[:, :bw], ALU.mult)
                            ACT(caT[j][:, b0:b0 + bw], t[:, :bw], AF.Silu, scale=pvl[:, PV_LNG + j:PV_LNG + j + 1],
                                bias=pvl[:, PV_LNB + j:PV_LNB + j + 1])

                    ckpt('confln')
                    alias_barrier(PH1, PH2)
                    wbg = ws.get(wv(w_in[l, :, O_BETA:O_BETA + 8]), KD, 8)
                    pbg = small.next()
                    for c in range(NCH):
                        for k in range(KD):
                            MM(pbg[:, c * 8:(c + 1) * 8], hT[k][c // 4][:, (c % 4) * 128:(c % 4 + 1) * 128], wbg[:, k, :],
                               start=(k == 0), stop=(k == KD - 1))
                    pbg3 = pbg[:, 0:64].v(lambda a: a.rearrange("p (c e) -> p c e", e=8))

                    def v3(t):
                        return t.v(lambda a: a.rearrange("p (c e) -> p c e", e=4))
                    ACT(v3(beta_c), pbg3[:, :, 0:4], AF.Sigmoid)
                    TT(v3(tmp_c), pbg3[:, :, 4:8], v3(pvl[:, PV_DTB:PV_DTB + 32]), ALU.add)
                    ACT(tmp_c, tmp_c, AF.Exp)
                    ACT(tmp_c, tmp_c, AF.Ln, bias=onec)
                    TT(g_c, tmp_c, negA, ALU.mult)
                    pgc = small.next()
                    pgl = small.next()
                    for c in range(NCH):
                        MM(pgc[:, c * 4:(c + 1) * 4], tri, g_c[:, c * 4:(c + 1) * 4])
                    for c in range(NCH):
                        MM(pgl[:, c * 4:(c + 1) * 4], ones, g_c[:, c * 4:(c + 1) * 4])
                    CP(gc_c, pgc[:, 0:32])
                    ACT(ngc_c, pgc[:, 0:32], AF.Identity, scale=-1.0)
                    ACT(egc_c, pgc[:, 0:32], AF.Exp)
                    TT(bgc_c, beta_c, egc_c, ALU.mult)
                    ACT(nbeta_c, beta_c, AF.Identity, scale=-1.0)
                    TT(tmp2_c, pgl[:, 0:32], gc_c, ALU.subtract)
                    ACT(kdec_c, tmp2_c, AF.Exp)
                    ACT(egl_c, pgl[:, 0:32], AF.Exp)
                    if has_s:
                        pbs = small.next()
                        for k in range(KD):
                            MM(pbs[:NS, 0:8], hT[k][2], wbg[:, k, :], start=(k == 0), stop=(k == KD - 1))
                        beta_s = T(cols_t[:NS, 11, 0:4], colr.bufs)
                        eg_s = T(cols_t[:NS, 11, 4:8], colr.bufs)
                        tmp_s = T(cols_t[:NS, 11, 8:12], colr.bufs)
                        ACT(beta_s, pbs[:NS, 0:4], AF.Sigmoid)
                        TT(tmp_s, pbs[:NS, 4:8], pvl[:NS, PV_DTB:PV_DTB + 4], ALU.add)
                        ACT(tmp_s, tmp_s, AF.Exp)
                        ACT(tmp_s, tmp_s, AF.Ln, bias=onec[:NS, :])
                        TT(tmp_s, tmp_s, negA[:NS, 0:4], ALU.mult)
                        ACT(eg_s, tmp_s, AF.Exp)
                    else:
                        MEMSET(pad[:, 0:3], 0.0)

                    ckpt('dprol')
                    for h in range(4):
                        wq = [ws.get(wv(w_in[l, :, off + h * 128:off + (h + 1) * 128]), KD, 128) for off in (O_Q, O_K, O_V, O_Z)]
                        for ci in range(3):
                            gch = ci * 4 + h
                            if has_s:
                                CP(pad[:, 0:3], qpre[:, gch, :])
                            for b, (b0, bw) in enumerate(blocks):
                                ps = big.next()
                                for k in range(KD):
                                    MM(ps[:, :bw], wq[ci][:, k, :], hT[k][b], start=(k == 0), stop=(k == KD - 1))
                                ACT(pad[:, 3 + b0:3 + b0 + bw], ps[:, :bw], AF.Copy)
                            sw0 = PV_SW + gch * 4
                            qc = qkv[ci][:, 0:1024]
                            TS(qc, pad[:, 0:1024], pvl[:, sw0:sw0 + 1], ALU.mult)
                            for i in range(1, 4):
                                STT(qc, pad[:, i:i + 1024], pvl[:, sw0 + i:sw0 + i + 1], qc, ALU.mult, ALU.add)
                            if not has_s:
                                ACT(qpre[:, gch, :], pad[:, 1024:1027], AF.Copy)
                            else:
                                p = small.next()
                                TR(p[:3, :], pad[:, 1024:1027], ident)
                                piece_out(o_short_p[l, :, gch * 128:(gch + 1) * 128], p[:3, :], 3)
                                xs_ = pad[:, 1027:1027 + NS]
                                qs_ = qkv[ci][:, 1024:1024 + NS]
                                st3v = sstT[:, gch, :].v(lambda a: a.rearrange("p (s j) -> p s j", j=3))
                                TS(qs_, xs_, pvl[:, sw0 + 3:sw0 + 4], ALU.mult)
                                for i in range(3):
                                    STT(qs_, st3v[:, :, i], pvl[:, sw0 + i:sw0 + i + 1], qs_, ALU.mult, ALU.add)
                                p = small.next()
                                TR(p[:NS, :], xs_, ident)
                                piece_out(o_short_s[l, :, 2, gch * 128:(gch + 1) * 128], p[:NS, :], NS)
                            ACT(qkv[ci][:, 0:W], qkv[ci][:, 0:W], AF.Silu)
                            if ci < 2:
                                for b, (b0, bw) in enumerate(blocks):
                                    sq = sqb_tile()
                                    ACT(sq[:, :bw], qkv[ci][:, b0:b0 + bw], AF.Square)
                                    ps = big.next()
                                    MM(ps[:, :bw], ones_b, sq[:, :bw])
                                    rn = scr.next()
                                    ACT(rn[:, :bw], ps[:, :bw], AF.Ln, bias=eps6)
                                    ACT(rn[:, :bw], rn[:, :bw], AF.Exp, scale=-0.5)
                                    if ci == 0:
                                        STT(qkv[ci][:, b0:b0 + bw], qkv[ci][:, b0:b0 + bw], 128.0 ** -0.5, rn[:, :bw], ALU.mult, ALU.mult)
                                    else:
                                        TT(qkv[ci][:, b0:b0 + bw], qkv[ci][:, b0:b0 + bw], rn[:, :bw], ALU.mult)
                                    if bw != NS:
                                        CP(qkb[ci][:, b0:b0 + bw], qkv[ci][:, b0:b0 + bw], eng="pool")
                        for b, (b0, bw) in enumerate(blocks):
                            ps = big.next()
                            for k in range(KD):
                                MM(ps[:, :bw], wq[3][:, k, :], hT[k][b], start=(k == 0), stop=(k == KD - 1))
                            ACT(zs[:, b0:b0 + bw], ps[:, :bw], AF.Silu)
                        if not has_s:
                            MEMSET(S[h], 0.0)

                        ckpt('dhead')
                        def chunk_gen(c):
                            cs = slice(c * 128, (c + 1) * 128)
                            ix = c * 4 + h
                            col = slice(ix, ix + 1)
                            qTc, kTc, vTc = qkv[0][:, cs], qkv[1][:, cs], qkv[2][:, cs]
                            qbc, kbc = qkb[0][:, cs], qkb[1][:, cs]
                            grep = sm.next()
                            TS(grep, ones, g_c[:, col], ALU.mult)
                            pg = small.next()
                            MM(pg, grep, tri)
                            yield
                            pre1 = sm.next()
                            STT(pre1, pg, -1.0, nm_strict, ALU.mult, ALU.add)
                            pre2 = sm.next()
                            TT(pre2, pg, nm_inclT, ALU.add)
                            egr = sm.next()
                            ACT(egr, pg, AF.Exp)
                            yield
                            Ds = sm.next()
                            ACT(Ds, pre1, AF.Exp, bias=gc_c[:, col])
                            DT = sm.next()
                            ACT(DT, pre2, AF.Exp, bias=ngc_c[:, col])
                            qgc = lgb["qg"].next()
                            TT(qgc, qTc, egr, ALU.mult)
                            yield
                            pk = small.next()
                            MM(pk, kbc, kbc)
                            Nm = sm.next()
                            STT(Nm, pk, nbeta_c[:, col], Ds, ALU.mult, ALU.mult)
                            yield
                            pq = small.next()
                            MM(pq, kbc, qbc)
                            qkT = lgb["qkT"].next()
                            TT(qkT, pq, DT, ALU.mult)
                            yield
                            Nbd = sm.next()
                            TT(Nbd, Nm, bm[:, 0, :], ALU.mult)
                            No = lg["No"].next()
                            TT(No, Nm, bm[:, 1, :], ALU.mult, eng="pool")
                            pt = small.next()
                            TR(pt, Nbd, ident)
                            Mm = sm.next()
                            ACT(Mm, pt, AF.Copy)
                            X = sm.next()
                            TT(X, Mm, ident, ALU.add)
                            yield
                            ptk = small.next()
                            TR(ptk, kTc, ident)
                            kbg = lgb["kbg"].next()
                            ACT(kbg, ptk, AF.Identity, scale=bgc_c[:, col])
                            kdc = lgb["kdc"].next()
                            ACT(kdc, ptk, AF.Identity, scale=kdec_c[:, col])
                            yield
                            ptv = small.next()
                            TR(ptv, vTc, ident)
                            vb = lgb["vb"].next()
                            ACT(vb, ptv, AF.Identity, scale=beta_c[:, col])
                            Pm, Pn = Mm, Nbd
                            for lev in range(1, 6):
                                yield
                                pp2 = small.next()
                                MM(pp2, Pm, Pn)
                                Pn2 = sm.next()
                                ACT(Pn2, pp2, AF.Copy)
                                if lev < 5:
                                    pp1 = small.next()
                                    MM(pp1, Pn, Pm)
                                    Pm2 = sm.next()
                                    CP(Pm2, pp1)
                                yield
                                px = small.next()
                                MM(px, Pn2, X)
                                X2 = sm.next()
                                TT(X2, px, X, ALU.add)
                                X, Pn = X2, Pn2
                                if lev < 5:
                                    Pm = Pm2
                            yield
                            pxt = small.next()
                            TR(pxt, X, ident)
                            Yt = sm.next()
                            ACT(Yt, pxt, AF.Copy)
                            pt1 = small.next()
                            MM(pt1, No, X)
                            T1 = sm.next()
                            CP(T1, pt1)
                            yield
                            pxx = small.next()
                            MM(pxx, Yt, T1)
                            X2 = sm.next()
                            TT(X2, pxx, X, ALU.add)
                            X = X2
                            yield
                            Xb = lgb["Xb"].next()
                            CP(Xb, X, eng="pool")
                            yield
                            pu = small.next()
                            MM(pu, Xb, vb)
                            u = lg["u"].next()
                            ACT(u, pu, AF.Copy)
                            yield
                            pw = small.next()
                            MM(pw, kbg, Xb)
                            wT = lgb["wT"].next()
                            CP(wT, pw)
                            yield
                            Sb = lgb["Sb"].next()
                            CP(Sb, S[h], eng="pool")
                            pvn = small.next()
                            MM(pvn, wT, Sb)
                            vn = lgb["vn"].next()
                            TT(vn, u, pvn, ALU.subtract)
                            yield
                            po = small.next()
                            MM(po, qgc, Sb, start=True, stop=False)
                            MM(po, qkT, vn, start=False, stop=True)
                            yield
                            pS = small.next()
                            MM(pS, kdc, vn)
                            STT(S[h], S[h], egl_c[:, col], pS, ALU.mult, ALU.add)
                            yield
                            osq = sm.next()
                            ssq = scol.next()
                            ACT(osq, po, AF.Square, accum=ssq[:, 0:1])
                            ACT(ssq[:, 0:1], ssq[:, 0:1], AF.Ln, scale=1.0 / 128, bias=eps6)
                            ACT(ssq[:, 0:1], ssq[:, 0:1], AF.Exp, scale=-0.5)
                            yield
                            on = sm.next()
                            ACT(on, po, AF.Identity, scale=ssq[:, 0:1])
                            yield
                            pT = small.next()
                            TR(pT, on, ident)
                            STT(ogT[h][:, cs], pT, pvl[:, PV_DNG:PV_DNG + 1], zs[:, cs], ALU.mult, ALU.mult)


                        pending = list(range(NCH))
                        active = []
                        rnd = 0
                        while pending or active:
                            if pending and len(active) < GCH and (rnd % STAG == 0 or not active):
                                active.append(chunk_gen(pending.pop(0)))
                            for g_ in list(active):
                                try:
                                    next(g_)
                                except StopIteration:
                                    active.remove(g_)
                            rnd += 1
                        ckpt('dchunks')
                        if pi == 0 and l + 1 < n_layers:
                            if h == 0:
                                DMA("sp", pvn, pvec[l + 1, :, 0:64], sem="ldpn")
                            for n2 in range(h * 6, h * 6 + 6):
                                mod_piece(l + 1, n2)
                            if h == 3:
                                mod_finish(l + 1)
                        if has_s:
                            DMA("sp", o_delta_p[l, h], S[h], sem="o_S%d" % h)
                            sc = slice(1024, 1024 + NS)
                            qs, ks, vs = qkv[0][:, sc], qkv[1][:, sc], qkv[2][:, sc]
                            DMA("sp", Ss, st_delta[l, :, h].rearrange("s p v -> p s v"), sem="ldSs")

                            def bcast16(colap):
                                dg = sm.next()
                                TS(dg[:NS, :NS], ident[:NS, :NS], colap, ALU.mult)
                                pB = small.next()
                                MM(pB[:, :NS], ones[:NS, :], dg[:NS, :NS])
                                o_ = scol.next()
                                CP(o_, pB[:, :NS])
                                return o_
                            bbc = bcast16(beta_s[:, h:h + 1])
                            ebc = bcast16(eg_s[:, h:h + 1])
                            t0 = scol.next()
                            TT(t0, qs, ks, ALU.mult)
                            pqk = small.next()
                            MM(pqk[:, :NS], ones, t0)
                            qkbc = scol.next()
                            CP(qkbc, pqk[:, :NS])
                            pkS = small.next()
                            pqS = small.next()
                            for s in range(NS):
                                MM(pkS[:, s:s + 1], Ss[:, s, :], ks[:, s:s + 1])
                            for s in range(NS):
                                MM(pqS[:, s:s + 1], Ss[:, s, :], qs[:, s:s + 1])
                            t1 = scol.next()
                            TT(t1, pkS[:, :NS], ebc, ALU.mult)
                            TT(t1, vs, t1, ALU.subtract)
                            vnT = scol.next()
                            TT(vnT, t1, bbc, ALU.mult)
                            oT = scol.next()
                            TT(oT, pqS[:, :NS], ebc, ALU.mult)
                            t2 = scol.next()
                            TT(t2, qkbc, vnT, ALU.mult)
                            TT(oT, oT, t2, ALU.add)
                            t3 = scol.next()
                            TT(t3, oT, oT, ALU.mult)
                            pss = small.next()
                            MM(pss[:, :NS], ones, t3)
                            rs_ = scol.next()
                            ACT(rs_, pss[:, :NS], AF.Ln, scale=1.0 / 128, bias=eps6)
                            ACT(rs_, rs_, AF.Exp, scale=-0.5)
                            TT(oT, oT, rs_, ALU.mult)
                            STT(ogT[h][:, sc], oT, pvl[:, PV_DNG:PV_DNG + 1], zs[:, sc], ALU.mult, ALU.mult)
                            pkt = small.next()
                            TR(pkt[:NS, :], ks, ident)
                            k_tok = lg["kbg"].next()
                            CP(k_tok[:NS, :], pkt[:NS, :])
                            pvt = small.next()
                            TR(pvt[:NS, :], vnT, ident)
                            vn_tok = lg["kdc"].next()
                            CP(vn_tok[:NS, :], pvt[:NS, :])
                            for s in range(NS):
                                rhs_s = sm.next()
                                TS(rhs_s[:NS, :], vn_tok[:NS, :], ident[:NS, s:s + 1], ALU.mult)
                                po_ = small.next()
                                MM(po_, k_tok[:NS, :], rhs_s[:NS, :])
                                STT(Ss[:, s, :], Ss[:, s, :], ebc[:, s:s + 1], po_, ALU.mult, ALU.add)
                            DMA("sp", o_delta_s[l, :, h].rearrange("s p v -> p s v"), Ss, sem="o_Ss")

                    ckpt('delta')
                    alias_barrier(PH2, PH3)
                    for c2 in range(4):
                        wco = ws.get(wv(w_conf_out[l, :, c2 * 256:(c2 + 1) * 256]), 4, 256)
                        wdo = ws.get(wv(w_delta_out[l, :, c2 * 256:(c2 + 1) * 256]), 4, 256)
                        wma = ws.get(wv(w_in[l, :, O_MA + c2 * 256:O_MA + (c2 + 1) * 256]), KD, 256)
                        wmb = ws.get(wv(w_in[l, :, O_MB + c2 * 256:O_MB + (c2 + 1) * 256]), KD, 256)
                        for c1 in range(2):
                            c = c2 * 2 + c1
                            w0 = c1 * 128
                            for b, (b0, bw) in enumerate(blocks):
                                pya = big.next()
                                pma = big.next()
                                for k in range(4):
                                    MM(pya[:, :bw], wco[:, k, w0:w0 + 128], caT[k][:, b0:b0 + bw], start=(k == 0), stop=(k == 3))
                                for k in range(KD):
                                    MM(pma[:, :bw], wma[:, k, w0:w0 + 128], hT[k][b], start=(k == 0), stop=(k == KD - 1))
                                sa = scr.next()
                                ACT(sa[:, :bw], pma[:, :bw], AF.Sigmoid)
                                TT(sa[:, :bw], pya[:, :bw], sa[:, :bw], ALU.mult)
                                pyb = big.next()
                                pmb = big.next()
                                for k in range(4):
                                    MM(pyb[:, :bw], wdo[:, k, w0:w0 + 128], ogT[k][:, b0:b0 + bw], start=(k == 0), stop=(k == 3))
                                for k in range(KD):
                                    MM(pmb[:, :bw], wmb[:, k, w0:w0 + 128], hT[k][b], start=(k == 0), stop=(k == KD - 1))
                                sb_ = scr.next()
                                ACT(sb_[:, :bw], pmb[:, :bw], AF.Sigmoid)
                                TT(sb_[:, :bw], pyb[:, :bw], sb_[:, :bw], ALU.mult)
                                TT(merged[c][:, b0:b0 + bw], sa[:, :bw], sb_[:, :bw], ALU.add)
                    ckpt('merge')
                    for c2 in range(4):
                        wmo = ws.get(wv(w_merge_out[l, :, c2 * 256:(c2 + 1) * 256]), KD, 256)
                        for c1 in range(2):
                            c = c2 * 2 + c1
                            for b, (b0, bw) in enumerate(blocks):
                                ps = big.next()
                                for k in range(KD):
                                    MM(ps[:, :bw], wmo[:, k, c1 * 128:(c1 + 1) * 128], merged[k][:, b0:b0 + bw], start=(k == 0), stop=(k == KD - 1))
                                resid_update(c, ps, bw, gbs[b], 16)

                    ckpt('mergeout')
                    rmsnorm_mod(blocks, gbs, 8, 24)
                    for fh in range(2):
                        alias_barrier(PH1 + PH2 + PH3 + PH4, PH4)
                        for jj2 in range(6):
                            j0 = fh * 11 + jj2 * 2
                            nj = min(2, fh * 11 + 11 - j0)
                            wg = ws.get(wv(w_ffn_in[l, :, j0 * 128:(j0 + nj) * 128]), KD, nj * 128)
                            wu = ws.get(wv(w_ffn_in[l, :, D_FF + j0 * 128:D_FF + (j0 + nj) * 128]), KD, nj * 128)
                            for j1 in range(nj):
                                jl = jj2 * 2 + j1
                                for b, (b0, bw) in enumerate(blocks):
                                    pgt = big.next()
                                    pup = big.next()
                                    for k in range(KD):
                                        MM(pgt[:, :bw], wg[:, k, j1 * 128:(j1 + 1) * 128], hT[k][b], start=(k == 0), stop=(k == KD - 1))
                                    for k in range(KD):
                                        MM(pup[:, :bw], wu[:, k, j1 * 128:(j1 + 1) * 128], hT[k][b], start=(k == 0), stop=(k == KD - 1))
                                    sg = scr.next()
                                    ACT(sg[:, :bw], pgt[:, :bw], AF.Silu)
                                    TT(actT[jl][:, b0:b0 + bw], sg[:, :bw], pup[:, :bw], ALU.mult)
                        for c in range(KD):
                            wo = ws.get(wv(w_ffn_out[l, fh * 1408:(fh + 1) * 1408, c * 128:(c + 1) * 128]), 11, 128)
                            for b, (b0, bw) in enumerate(blocks):
                                ps = big.next()
                                for j in range(11):
                                    MM(ps[:, :bw], wo[:, j, :], actT[j][:, b0:b0 + bw], start=(j == 0), stop=(j == 10))
                                resid_update(c, ps, bw, gbs[b], 40)

            for gb, (g0, gw) in enumerate(GB):
                ps = big.next()
                for k in range(KD):
                    sq = sqb_tile()
                    ACT(sq[:, :gw], xT[k][gb], AF.Square)
                    MM(ps[:, :gw], ones_b, sq[:, :gw], start=(k == 0), stop=(k == KD - 1))
                rs = st1
                ACT(rs[:, :gw], ps[:, :gw], AF.Ln, scale=1.0 / D, bias=eps6)
                ACT(rs[:, :gw], rs[:, :gw], AF.Exp, scale=-0.5)
                nt = max(1, gw // 128)
                tw = min(gw, 128)
                for t in range(nt):
                    o_ = ld[t % 2]
                    for k in range(KD):
                        tmp = sm.next()
                        STT(tmp[:, :tw], xT[k][gb][:, t * 128:t * 128 + tw], pvl[:, PV_FG + k:PV_FG + k + 1], rs[:, t * 128:t * 128 + tw],
                            ALU.mult, ALU.mult)
                        p = small.next()
                        TR(p[:tw, :], tmp[:, :tw], ident)
                        if k % 2 == 0:
                            ACT(o_[:tw, k * 128:(k + 1) * 128], p[:tw, :], AF.Copy)
                        else:
                            CP(o_[:tw, k * 128:(k + 1) * 128], p[:tw, :])
                    if gb < 4:
                        r0 = g0 + t * 128
                        DMA("sp", y_p[r0:r0 + 128, :], o_, sem="o_y%d" % (t % 2))
                    else:
                        DMA("sp", y_s[:, :], o_[:NS, :], sem="o_y%d" % (t % 2))

        def build_w(Pq):
            try:
                build(Pq)
            except StopBuild:
                pass
            for sname in sem_names:
                if (sname.startswith("o_") or sname.startswith("w") or sname.startswith("ld")) and Pq.cnt.get(sname, 0) > 0:
                    Pq.q["sp"].append(("wait", sname, Pq.cnt[sname]))
        P0 = Prog()
        build_w(P0)
        ws.recording = False
        P = Prog()
        build_w(P)
        print("ops:", P.nops, {k: len(v) for k, v in P.q.items()}, "weights:", len(ws.specs))

        semh = {n: es.enter_context(nc.semaphore(n)) for n in sem_names}
        block = es.enter_context(nc.Block())

        def run(eng, items):
            for it in items:
                if it[0] == "wait":
                    eng.wait_ge(semh[it[1]], it[2])
                else:
                    it[1](eng).then_inc(semh[it[2]], it[3])

        @block.tensor
        def _(eng):
            run(eng, P.q["pe"])

        @block.scalar
        def _(eng):
            run(eng, P.q["act"])

        @block.vector
        def _(eng):
            run(eng, P.q["dve"])

        @block.gpsimd
        def _(eng):
            run(eng, P.q["pool"])

        @block.sync
        def _(eng):
            run(eng, P.q["sp"])
    return nc


def make_consts():
    c = np.zeros((128, 5, 128), np.float32)
    i = np.arange(128)
    c[:, 0, :] = np.eye(128, dtype=np.float32)
    c[:, 1, :] = 1.0
    c[:, 2, :] = (i[:, None] <= i[None, :]).astype(np.float32)
    c[:, 3, :] = np.where(i[:, None] > i[None, :], 0.0, NEG).astype(np.float32)
    c[:, 4, :] = np.where(i[None, :] >= i[:, None], 0.0, NEG).astype(np.float32)
    return c


def make_bmask():
    i = np.arange(128)
    b64 = (i[:, None] // 64 == i[None, :] // 64)
    m = np.zeros((128, 3, 128), np.float32)
    m[:, 0, :] = b64
    m[:, 1, :] = ~b64
    m[:, 2, :] = 1.0
    return m


def make_pvec(inp):
    pv = np.zeros((DEPTH, 128, PV_N), np.float32)
    for l in range(DEPTH):
        pv[l, :, PV_G1:PV_G1 + 8] = inp["norm1_g"][l].reshape(8, 128).T
        pv[l, :, PV_G2:PV_G2 + 8] = inp["norm2_g"][l].reshape(8, 128).T
        pv[l, :, PV_BADA:PV_BADA + 48] = inp["b_ada"][l].reshape(48, 128).T
        cw = inp["conf_dw_w"][l]
        for j in range(4):
            pv[l, :, PV_CW + j * 31:PV_CW + (j + 1) * 31] = cw[:, j * 128:(j + 1) * 128].T
        pv[l, :, PV_CB:PV_CB + 4] = inp["conf_dw_b"][l].reshape(4, 128).T
        pv[l, :, PV_LNG:PV_LNG + 4] = inp["conf_ln_g"][l].reshape(4, 128).T
        pv[l, :, PV_LNB:PV_LNB + 4] = inp["conf_ln_b"][l].reshape(4, 128).T
        sw = inp["short_conv_w"][l]
        for g in range(12):
            pv[l, :, PV_SW + g * 4:PV_SW + (g + 1) * 4] = sw[:, g * 128:(g + 1) * 128].T
        pv[l, :, PV_DNG] = inp["delta_norm_g"][l]
        pv[l, :, PV_ALOG:PV_ALOG + 32] = np.tile(inp["a_log"][l], 8)[None, :]
        pv[l, :, PV_DTB:PV_DTB + 32] = np.tile(inp["dt_bias"][l], 8)[None, :]
        pv[l, :, PV_FG:PV_FG + 8] = inp["final_norm_g"].reshape(8, 128).T
    return pv


_NC_CACHE = {}


def kernel(**inp):
    inp = {k: np.asarray(v) for k, v in inp.items()}
    n = 8
    if "nc" not in _NC_CACHE:
        _NC_CACHE["nc"] = build_nc()
    nc = _NC_CACHE["nc"]
    consts = make_consts()
    bmask_np = make_bmask()
    pv = make_pvec(inp)
    shared = {k: np.ascontiguousarray(inp[k], dtype=np.float32) for k in
              ("w_ada", "w_in", "w_conf_out", "w_delta_out", "w_merge_out", "w_ffn_in", "w_ffn_out")}
    in_maps = []
    for i in range(n):
        s0 = i * NS
        m = dict(shared)
        m["x_p"] = np.ascontiguousarray(inp["x_prompt"][i])
        m["x_s"] = np.ascontiguousarray(inp["x_sample"][s0:s0 + NS, 0, :])
        m["c_in"] = np.ascontiguousarray(np.concatenate([inp["c_prompt"][i:i + 1], inp["c_sample"][s0:s0 + NS]], axis=0))
        m["st_conf"] = np.ascontiguousarray(inp["state_conformer_conv"][:, s0:s0 + NS])
        m["st_short"] = np.ascontiguousarray(inp["state_short_conv"][:, s0:s0 + NS])
        m["st_delta"] = np.ascontiguousarray(inp["state_delta"][:, s0:s0 + NS])
        m["pvec"] = pv
        m["consts"] = consts
        m["bmask"] = bmask_np
        in_maps.append(m)
    res = run_bass_kernel_spmd(nc, in_maps, core_ids=list(range(n)))
    R = res.results
    y_p = np.stack([R[i]["y_p"] for i in range(n)], axis=0)
    y_s = np.concatenate([R[i]["y_s"] for i in range(n)], axis=0)[:, None, :]
    conf_p = np.stack([R[i]["o_conf_p"] for i in range(n)], axis=1)
    conf_s = np.concatenate([R[i]["o_conf_s"] for i in range(n)], axis=1)
    short_p = np.stack([R[i]["o_short_p"] for i in range(n)], axis=1)
    short_s = np.concatenate([R[i]["o_short_s"] for i in range(n)], axis=1)
    delta_p = np.stack([R[i]["o_delta_p"] for i in range(n)], axis=1)
    delta_s = np.concatenate([R[i]["o_delta_s"] for i in range(n)], axis=1)
    return tuple(np.ascontiguousarray(a, dtype=np.float32) for a in
                 (y_p, y_s, conf_p, conf_s, short_p, short_s, delta_p, delta_s))
```

```python
import numpy as np
from contextlib import ExitStack
import concourse.bass as bass
import concourse.mybir as mybir
from concourse.bass_utils import run_bass_kernel_spmd

F32 = mybir.dt.float32
BF16 = mybir.dt.bfloat16
AF = mybir.ActivationFunctionType
ALU = mybir.AluOpType
AX = mybir.AxisListType

DEPTH = 4
D = 1024
KD = 8
SEQ = 2048
NS = 16
NTOK = SEQ + NS
CONF_CH = 512
CONF_KW = 31
D_FF = 2816
KF = 22
O_GLU_B = 512
O_Q = 1024
O_K = 1536
O_V = 2048
O_Z = 2560
O_BETA = 3072
O_MA = 3080
O_MB = 4104
IN_DIM = 5128
NEG = -1.0e6

SAME_ENGINE_SYNC = True
GCH = 3
STAG = 10
N_LAYERS = DEPTH

PV_G1 = 0
PV_G2 = 8
PV_BADA = 16
PV_CW = 64
PV_CB = 188
PV_LNG = 192
PV_LNB = 196
PV_SW = 200
PV_DNG = 248
PV_ALOG = 249
PV_DTB = 281
PV_FG = 313
PV_N = 321


class Buf:
    __slots__ = ("name", "w", "r", "psum")

    def __init__(self, name="", psum=False):
        self.name = name
        self.w = None
        self.r = {}
        self.psum = psum


class T:
    __slots__ = ("ap", "bufs")

    def __init__(self, ap, bufs=None):
        self.ap = ap
        if bufs is None:
            bufs = [Buf()]
        elif isinstance(bufs, Buf):
            bufs = [bufs]
        self.bufs = bufs

    def __getitem__(self, idx):
        return T(self.ap[idx], self.bufs)

    def v(self, fn):
        return T(fn(self.ap), self.bufs)


def _ap(x):
    return x.ap if isinstance(x, T) else x


def _bufs(*xs):
    out = []
    for x in xs:
        if isinstance(x, T):
            out.extend(x.bufs)
    return out


class Ring:
    def __init__(self, tiles):
        self.tiles = tiles
        self.i = 0

    def next(self):
        t = self.tiles[self.i % len(self.tiles)]
        self.i += 1
        return t


class Prog:
    ENGS = ("pe", "act", "dve", "pool", "sp")

    def __init__(self):
        self.q = {n: [] for n in self.ENGS}
        self.cnt = {}
        self.seen = {n: {} for n in self.ENGS}
        self.nops = 0

    def op(self, e, fn, reads=(), writes=(), sem=None, inc=1):
        deps = {}
        for b in reads:
            if b.w is not None:
                if deps.get(b.w[0], 0) < b.w[1]:
                    deps[b.w[0]] = b.w[1]
            if b.psum:
                for s, v in b.r.items():
                    if s != e and deps.get(s, 0) < v:
                        deps[s] = v
        for b in writes:
            if b.w is not None:
                if deps.get(b.w[0], 0) < b.w[1]:
                    deps[b.w[0]] = b.w[1]
            for s, v in b.r.items():
                if deps.get(s, 0) < v:
                    deps[s] = v
        seen = self.seen[e]
        q = self.q[e]
        for s, v in deps.items():
            if s == e and (e == "pe" or not SAME_ENGINE_SYNC):
                continue
            if seen.get(s, 0) >= v:
                continue
            seen[s] = v
            q.append(("wait", s, v))
        sname = sem or e
        val = self.cnt.get(sname, 0) + inc
        self.cnt[sname] = val
        q.append(("op", fn, sname, inc))
        for b in reads:
            if b.r.get(sname, 0) < val:
                b.r[sname] = val
        for b in writes:
            b.w = (sname, val)
            b.r = {}
        self.nops += 1
        return val

    def MM(self, out, lhsT, rhs, start=True, stop=True):
        o, l, r = _ap(out), _ap(lhsT), _ap(rhs)
        self.op("pe", lambda e: e.matmul(o, lhsT=l, rhs=r, start=start, stop=stop), _bufs(lhsT, rhs), _bufs(out))

    def TR(self, out, in_, ident):
        o, i, d = _ap(out), _ap(in_), _ap(ident)
        self.op("pe", lambda e: e.transpose(o, i, d), _bufs(in_, ident), _bufs(out))

    def ACT(self, out, in_, func, bias=None, scale=None, accum=None):
        o, i = _ap(out), _ap(in_)
        kw = {}
        if bias is None and isinstance(scale, T):
            bias = self.zeroc[0:o.shape[0], :] if o.shape[0] != 128 else self.zeroc
        if bias is not None:
            kw["bias"] = _ap(bias)
        if scale is not None:
            kw["scale"] = _ap(scale)
        if accum is not None:
            kw["accum_out"] = _ap(accum)
        self.op("act", lambda e: e.activation(out=o, in_=i, func=func, **kw), _bufs(in_, bias, scale), _bufs(out, accum))

    def TT(self, out, in0, in1, op, eng="dve"):
        o, a, b = _ap(out), _ap(in0), _ap(in1)
        self.op(eng, lambda e: e.tensor_tensor(out=o, in0=a, in1=b, op=op), _bufs(in0, in1), _bufs(out))

    def TS(self, out, in0, s1, op0, s2=None, op1=None, eng="dve"):
        o, a, x1, x2 = _ap(out), _ap(in0), _ap(s1), _ap(s2)
        if op1 is None:
            self.op(eng, lambda e: e.tensor_scalar(out=o, in0=a, scalar1=x1, scalar2=None, op0=op0), _bufs(in0, s1), _bufs(out))
        else:
            self.op(eng, lambda e: e.tensor_scalar(out=o, in0=a, scalar1=x1, scalar2=x2, op0=op0, op1=op1), _bufs(in0, s1, s2), _bufs(out))

    def STT(self, out, in0, scalar, in1, op0, op1, eng="dve"):
        o, a, s, b = _ap(out), _ap(in0), _ap(scalar), _ap(in1)
        self.op(eng, lambda e: e.scalar_tensor_tensor(out=o, in0=a, scalar=s, in1=b, op0=op0, op1=op1), _bufs(in0, scalar, in1), _bufs(out))

    def CP(self, out, in_, eng="dve"):
        o, i = _ap(out), _ap(in_)
        self.op(eng, lambda e: e.tensor_copy(out=o, in_=i), _bufs(in_), _bufs(out))

    def RED(self, out, in_, op=ALU.add, eng="dve"):
        o, i = _ap(out), _ap(in_)
        self.op(eng, lambda e: e.tensor_reduce(out=o, in_=i, axis=AX.X, op=op), _bufs(in_), _bufs(out))

    def RECIP(self, out, in_):
        o, i = _ap(out), _ap(in_)
        self.op("dve", lambda e: e.reciprocal(out=o, in_=i), _bufs(in_), _bufs(out))

    def MEMSET(self, out, val, eng="dve"):
        o = _ap(out)
        self.op(eng, lambda e: e.memset(o, val), (), _bufs(out))

    def DMA(self, q, out, in_, sem, **kw):
        o, i = _ap(out), _ap(in_)
        self.op(q, lambda e: e.dma_start(out=o, in_=i, **kw), _bufs(in_), _bufs(out), sem=sem, inc=16)


class WStream:
    NSLOT = 6
    LAG = 4
    ELEMS = 2048

    def __init__(self):
        self.specs = []
        self.recording = True

    def begin(self, P, slot_aps):
        self.P = P
        self.i = 0
        self.loaded = 0
        self.slots = [T(a) for a in slot_aps]

    def _view(self, j):
        _, kc, n = self.specs[j]
        t = self.slots[j % self.NSLOT]
        return T(t.ap[:, 0:kc * n].rearrange("p (k n) -> p k n", k=kc), t.bufs)

    def get(self, dram_ap, kc, n):
        assert kc * n <= self.ELEMS
        i = self.i
        self.i += 1
        if self.recording:
            self.specs.append((dram_ap, kc, n))
            return self._view(i)
        while self.loaded < len(self.specs) and (self.loaded < self.NSLOT or self.loaded <= i - self.LAG + self.NSLOT):
            j = self.loaded
            self.P.DMA("pool", self._view(j), self.specs[j][0], sem="w%d" % (j % self.NSLOT))
            self.loaded += 1
        assert self.loaded > i
        return self._view(i)


class StopBuild(Exception):
    pass


STOP_AT = [None]


def ckpt(name):
    if STOP_AT[0] == name:
        raise StopBuild(name)


def alias_barrier(olds, news):
    acc = {}
    for t in olds:
        for b in t.bufs:
            if b.w is not None and acc.get(b.w[0], 0) < b.w[1]:
                acc[b.w[0]] = b.w[1]
            for s, v in b.r.items():
                if acc.get(s, 0) < v:
                    acc[s] = v
    for t in news:
        for b in t.bufs:
            for s, v in acc.items():
                if b.r.get(s, 0) < v:
                    b.r[s] = v


def build_nc(n_layers=N_LAYERS):
    nc = bass.Bass("TRN2", target_bir_lowering=False)

    def din(name, shape):
        return nc.dram_tensor(name, list(shape), F32, kind="ExternalInput").ap()

    def dout(name, shape):
        return nc.dram_tensor(name, list(shape), F32, kind="ExternalOutput").ap()

    x_p = din("x_p", [SEQ, D])
    x_s = din("x_s", [NS, D])
    c_in = din("c_in", [NS + 1, D])
    st_conf = din("st_conf", [DEPTH, NS, 30, CONF_CH])
    st_short = din("st_short", [DEPTH, NS, 3, 1536])
    st_delta = din("st_delta", [DEPTH, NS, 4, 128, 128])
    w_ada = din("w_ada", [DEPTH, D, 6 * D])
    w_in = din("w_in", [DEPTH, D, IN_DIM])
    w_conf_out = din("w_conf_out", [DEPTH, CONF_CH, D])
    w_delta_out = din("w_delta_out", [DEPTH, 512, D])
    w_merge_out = din("w_merge_out", [DEPTH, D, D])
    w_ffn_in = din("w_ffn_in", [DEPTH, D, 2 * D_FF])
    w_ffn_out = din("w_ffn_out", [DEPTH, D_FF, D])
    pvec = din("pvec", [DEPTH, 128, PV_N])
    consts = din("consts", [128, 5, 128])
    bmask = din("bmask", [128, 3, 128])

    y_p = dout("y_p", [SEQ, D])
    y_s = dout("y_s", [NS, D])
    o_conf_p = dout("o_conf_p", [DEPTH, 30, CONF_CH])
    o_conf_s = dout("o_conf_s", [DEPTH, NS, 30, CONF_CH])
    o_short_p = dout("o_short_p", [DEPTH, 3, 1536])
    o_short_s = dout("o_short_s", [DEPTH, NS, 3, 1536])
    o_delta_p = dout("o_delta_p", [DEPTH, 4, 128, 128])
    o_delta_s = dout("o_delta_s", [DEPTH, NS, 4, 128, 128])

    with ExitStack() as es:
        def sb(name, shape, dt=F32):
            return es.enter_context(nc.sbuf_tensor(name, list(shape), dt))

        xT_t = sb("xT", [128, KD, NTOK])
        hT_t = sb("hT", [128, KD, 1040], BF16)
        A_CA = 1070
        A_QKV = 1043
        A_ZS = A_QKV + 3 * 1040
        R_WORDS = 1070 + 4 * 1040
        A_CAT = R_WORDS
        A_OGT = A_CAT + 2080
        ARENA_WORDS = A_OGT + 2080
        assert A_ZS + 1040 <= R_WORDS and 11 * 520 <= A_OGT
        arena = sb("arena", [128, ARENA_WORDS])
        wslots = [sb("wslot%d" % i, [128, WStream.ELEMS], BF16) for i in range(WStream.NSLOT)]
        scr_t = [sb("scr%d" % i, [128, 512]) for i in range(4)]
        st_t = [sb("st%d" % i, [128, 512]) for i in range(3)]
        sm_t = [sb("sm%d" % i, [128, 128]) for i in range(16)]
        LGN = ("qkT", "kdc", "kbg", "vb", "u", "wT", "vn", "qg", "No")
        lg_t = {n: [sb("%s%d" % (n, i), [128, 128]) for i in range(2)] for n in LGN}
        S_t = [sb("S%d" % h, [128, 128]) for h in range(4)]
        Ss_t = sb("Ss", [128, NS * 128])
        cols_t = sb("cols", [128, 12, 32])
        scol_t = sb("scol", [128, 12, 16])
        modT_t2 = [sb("modT%d" % i, [128, 48, NS + 1]) for i in range(2)]
        gs_t2 = [sb("gs%d" % i, [128, 16, NS + 1]) for i in range(2)]
        pvn_t = sb("pvn", [128, 64])
        cT_t = sb("cT", [128, KD, NS + 1], BF16)
        pv_t = sb("pv", [128, PV_N])
        negA_t = sb("negA", [128, 32])
        cst_t = sb("cst", [128, 6, 128])
        upre_t = sb("upre", [128, 4, 30])
        qpre_t = sb("qpre", [128, 12, 3])
        sstT_t = sb("sstT", [128, 12, 48])
        pc_t = [sb("pc%d" % i, [NS, 128]) for i in range(4)]
        qkb_t = sb("qkb", [128, 2, 1040], BF16)
        bm_t = sb("bmask_sb", [128, 3, 128], BF16)
        ps_t = [es.enter_context(nc.psum_tensor("ps%d" % i, [128, 512], F32)) for i in range(8)]
        print("SBUF bytes remaining per partition:", nc.sbuf_bytes_remaining)

        sem_names = ["pe", "act", "dve", "pool"] + ["w%d" % i for i in range(WStream.NSLOT)] + \
            ["ld0", "ld1", "ldc", "ldp", "ldpn", "ldbm", "ldcs", "ldss", "ldSs", "o_cpo", "o_cso", "o_pc0", "o_pc1", "o_pc2", "o_pc3",
             "o_S0", "o_S1", "o_S2", "o_S3", "o_Ss", "o_y0", "o_y1", "o_d2d"]

        ws = WStream()

        def build(P):
            ws.begin(P, [w[:] for w in wslots])
            MM, TR, ACT, TT, TS, STT, CP, RED, RECIP, MEMSET, DMA = (P.MM, P.TR, P.ACT, P.TT, P.TS, P.STT, P.CP, P.RED,
                                                                     P.RECIP, P.MEMSET, P.DMA)
            GB = [(0, 512), (512, 512), (1024, 512), (1536, 512), (2048, NS)]
            xT = [[T(xT_t[:, k, b0:b0 + bw]) for (b0, bw) in GB] for k in range(KD)]
            LB = [(0, 512), (512, 512), (1024, NS)]
            hT = [[T(hT_t[:, k, b0:b0 + bw]) for (b0, bw) in LB] for k in range(KD)]
            ar = arena
            upad = T(ar[:, 0:1070])
            ca = [T(ar[:, A_CA + j * 1040:A_CA + (j + 1) * 1040]) for j in range(4)]
            pad = T(ar[:, 0:1043])
            qkv = [T(ar[:, A_QKV + c * 1040:A_QKV + (c + 1) * 1040]) for c in range(3)]
            zs = T(ar[:, A_ZS:A_ZS + 1040])
            merged_v = ar[:, 0:4160].bitcast(BF16).rearrange("p (k n) -> p k n", k=8)
            merged = [T(merged_v[:, c, :]) for c in range(8)]
            caT_v = ar[:, A_CAT:A_CAT + 2080].bitcast(BF16).rearrange("p (k n) -> p k n", k=4)
            caT = [T(caT_v[:, j, :]) for j in range(4)]
            ogT_v = ar[:, A_OGT:A_OGT + 2080].bitcast(BF16).rearrange("p (k n) -> p k n", k=4)
            ogT = [T(ogT_v[:, h, :]) for h in range(4)]
            act_v = ar[:, 0:11 * 520].bitcast(BF16).rearrange("p (k n) -> p k n", k=11)
            actT = [T(act_v[:, j, :]) for j in range(11)]
            cs_tok = T(ar[:120, A_OGT:A_OGT + 2048].rearrange("p (r c) -> p r c", r=4))
            PH1 = [upad] + ca + caT + [cs_tok]
            PH2 = [pad] + qkv + [zs] + ogT
            PH3 = merged
            PH4 = actT

            scr = Ring([T(t[:]) for t in scr_t])
            st1, st2, st3 = [T(t[:]) for t in st_t]
            cpo = T(st_t[0][:30, :], st1.bufs)
            cso = T(st_t[2][:NS, :], st3.bufs)
            cstT = T(st_t[1][:, 0:NS * 31].rearrange("p (s j) -> p s j", j=31), st2.bufs)
            sm_base = [T(t[:]) for t in sm_t]
            sm = Ring(sm_base)
            sm_extra = [T(st_t[i][:, k_ * 128:(k_ + 1) * 128]) for i in range(3) for k_ in range(4)]
            smx = Ring(sm_base + sm_extra)
            lgx_t = {n: list(v) + [Ss_t[:, j * 128:(j + 1) * 128]] for j, (n, v) in enumerate(lg_t.items())}
            lg = {n: Ring([T(t[:]) for t in v]) for n, v in lgx_t.items()}
            lgb = {n: Ring([T(t[:, 0:64].bitcast(BF16)) for t in v]) for n, v in lgx_t.items()}
            lgb["Xb"] = Ring([T(t[:, 64:128].bitcast(BF16)) for t in lgx_t["qkT"]])
            lgb["Sb"] = Ring([T(t[:, 64:128].bitcast(BF16)) for t in lgx_t["kdc"]])
            qkb = [T(qkb_t[:, i, :]) for i in range(2)]
            S = [T(t[:]) for t in S_t]
            Ss = T(Ss_t[:].rearrange("p (s v) -> p s v", v=128))
            ld = [T(Ss_t[:, i * 1024:(i + 1) * 1024], Ss.bufs) for i in range(2)]
            ss_tok = T(Ss_t[:48, 0:1536], Ss.bufs)
            big = Ring([T(ps_t[i][:], Buf("pb%d" % i, True)) for i in range(4)])
            sbank = [Buf("psb%d" % i, True) for i in range(4)]
            small = Ring([T(ps_t[4 + i % 4][:, (i // 4) * 128:(i // 4 + 1) * 128], sbank[i % 4]) for i in range(16)])
            colsv = [T(cols_t[:, i, :]) for i in range(12)]
            beta_c, g_c, gc_c, ngc_c, egc_c, bgc_c, nbeta_c, kdec_c, egl_c, tmp_c, tmp2_c, colr = colsv
            scol = Ring([T(scol_t[:, i, :]) for i in range(12)])
            modTs = [T(t[:]) for t in modT_t2]
            gss = [T(t[:]) for t in gs_t2]
            pvn = T(pvn_t[:])
            cur = {}
            cT = T(cT_t[:])
            pvl = T(pv_t[:])
            negA = T(negA_t[:])
            cst = T(cst_t[:, 0:5, :])
            ident = cst[:, 0, :]
            ones = cst[:, 1, :]
            tri = cst[:, 2, :]
            nm_strict = cst[:, 3, :]
            nm_inclT = cst[:, 4, :]
            eps6 = T(cst_t[:, 5, 0:1])
            eps5 = T(cst_t[:, 5, 1:2])
            onec = T(cst_t[:, 5, 2:3])
            zeroc = T(cst_t[:, 5, 3:4])
            P.zeroc = zeroc
            upre = T(upre_t[:])
            qpre = T(qpre_t[:])
            sstT = T(sstT_t[:])
            pcs = [T(t[:]) for t in pc_t]
            pc_i = [0]

            def piece_out(dram_ap, psrc, rows):
                i = pc_i[0] % 4
                pc_i[0] += 1
                CP(pcs[i][:rows, :], psrc)
                DMA("sp", dram_ap, pcs[i][:rows, :], sem="o_pc%d" % i)

            DMA("sp", cst, consts[:, :, :], sem="ldc")
            bm = T(bm_t[:])
            DMA("pool", bm, bmask[:, :, :], sem="ldbm")
            ones_b = bm[:, 2, :]

            def sqb_tile():
                t_ = scr.next()
                return T(t_.ap[:, 0:256].bitcast(BF16), t_.bufs)
            MEMSET(eps6, 1e-6)
            MEMSET(eps5, 1e-5)
            MEMSET(onec, 1.0)
            MEMSET(zeroc, 0.0)

            for t in range(SEQ // 128):
                l_ = ld[t % 2]
                DMA("sp", l_, x_p[t * 128:(t + 1) * 128, :], sem="ld%d" % (t % 2))
                gb = t // 4
                for k in range(KD):
                    p = small.next()
                    TR(p, l_[:, k * 128:(k + 1) * 128], ident)
                    dst = T(xT_t[:, k, t * 128:(t + 1) * 128], xT[k][gb].bufs)
                    if k % 2 == 0:
                        ACT(dst, p, AF.Copy)
                    else:
                        CP(dst, p)
            l_ = ld[0]
            DMA("sp", l_[:NS, :], x_s[:, :], sem="ld0")
            for k in range(KD):
                p = small.next()
                TR(p[:, :NS], l_[:NS, k * 128:(k + 1) * 128], ident[:NS, :NS])
                CP(xT[k][4], p[:, :NS])
            l_ = ld[1]
            DMA("sp", l_[:NS + 1, :], c_in[:, :], sem="ld1")
            ACT(l_[:NS + 1, :], l_[:NS + 1, :], AF.Silu)
            for k in range(KD):
                p = small.next()
                TR(p[:, :NS + 1], l_[:NS + 1, k * 128:(k + 1) * 128], ident[:NS + 1, :NS + 1])
                CP(cT[:, k, :], p[:, :NS + 1])

            ckpt('init')
            def rmsnorm_mod(blocks, gbs, gsoff, shoff):
                for b, (b0, bw) in enumerate(blocks):
                    gb = gbs[b]
                    ps = big.next()
                    for k in range(KD):
                        sq = sqb_tile()
                        ACT(sq[:, :bw], xT[k][gb], AF.Square)
                        MM(ps[:, :bw], ones_b, sq[:, :bw], start=(k == 0), stop=(k == KD - 1))
                    rs = st1
                    ACT(rs[:, :bw], ps[:, :bw], AF.Ln, scale=1.0 / D, bias=eps6)
                    ACT(rs[:, :bw], rs[:, :bw], AF.Exp, scale=-0.5)
                    for k in range(KD):
                        tmp = scr.next()
                        TT(tmp[:, :bw], xT[k][gb], rs[:, :bw], ALU.mult)
                        if bw == NS:
                            TT(tmp[:, :bw], tmp[:, :bw], gs[:, gsoff + k, 1:NS + 1], ALU.mult)
                            TT(hT[k][b], tmp[:, :bw], modT[:, shoff + k, 1:NS + 1], ALU.add)
                        else:
                            ACT(hT[k][b], tmp[:, :bw], AF.Identity, scale=gs[:, gsoff + k, 0:1], bias=modT[:, shoff + k, 0:1])

            def resid_update(c, ps, bw, gb, gtoff):
                if bw == NS:
                    tmp = scr.next()
                    TT(tmp[:, :bw], ps[:, :bw], modT[:, gtoff + c, 1:NS + 1], ALU.mult)
                    TT(xT[c][gb], xT[c][gb], tmp[:, :bw], ALU.add)
                else:
                    STT(xT[c][gb], ps[:, :bw], modT[:, gtoff + c, 0:1], xT[c][gb], ALU.mult, ALU.add)

            def wv(ap2d):
                return ap2d.rearrange("(k p) n -> p k n", p=128)

            def mod_piece(ll, n2):
                mT = modTs[ll % 2]
                w = ws.get(wv(w_ada[ll, :, n2 * 256:(n2 + 1) * 256]), KD, 256)
                for half in range(2):
                    n = n2 * 2 + half
                    p = small.next()
                    for k in range(KD):
                        MM(p[:, :NS + 1], w[:, k, half * 128:(half + 1) * 128], cT[:, k, :], start=(k == 0), stop=(k == KD - 1))
                    ACT(mT[:, n, :], p[:, :NS + 1], AF.Identity, bias=pvn[:, PV_BADA + n:PV_BADA + n + 1])

            def mod_finish(ll):
                mT, g_ = modTs[ll % 2], gss[ll % 2]
                for k in range(KD):
                    TS(g_[:, k, :], mT[:, 8 + k, :], 1.0, ALU.add, pvn[:, PV_G1 + k:PV_G1 + k + 1], ALU.mult)
                    TS(g_[:, 8 + k, :], mT[:, 32 + k, :], 1.0, ALU.add, pvn[:, PV_G2 + k:PV_G2 + k + 1], ALU.mult)

            for l in range(n_layers):
                DMA("sp", pvl, pvec[l], sem="ldp")
                modT, gs = modTs[l % 2], gss[l % 2]
                cur["modT"], cur["gs"] = modT, gs
                if l == 0:
                    DMA("sp", pvn, pvec[0, :, 0:64], sem="ldpn")
                    for n2 in range(24):
                        mod_piece(0, n2)
                    mod_finish(0)
                ACT(negA, pvl[:, PV_ALOG:PV_ALOG + 32], AF.Exp)
                ACT(negA, negA, AF.Identity, scale=-1.0)
                ckpt('mod')
                DMA("sp", o_conf_s[l, :, 0:29, :], st_conf[l, :, 1:30, :], sem="o_d2d")
                DMA("sp", o_short_s[l, :, 0:2, :], st_short[l, :, 1:3, :], sem="o_d2d")
                DMA("sp", ss_tok, st_short[l].rearrange("s j c -> (s j) c"), sem="ldss")
                for gch in range(12):
                    p = small.next()
                    TR(p[:, :48], ss_tok[:48, gch * 128:(gch + 1) * 128], ident[:48, :48])
                    CP(sstT[:, gch, :], p[:, :48])

                ckpt('lstart')
                for pi in range(2):
                    blocks = LB[:2] if pi == 0 else LB
                    gbs = [0, 1] if pi == 0 else [2, 3, 4]
                    has_s = pi == 1
                    W = 1024 + (NS if has_s else 0)
                    NCH = 8

                    rmsnorm_mod(blocks, gbs, 0, 0)

                    ckpt('norm1')
                    alias_barrier(PH2 + PH3 + PH4, PH1)
                    if has_s:
                        DMA("sp", cs_tok, st_conf[l].rearrange("(r s) j c -> (s j) r c", r=4), sem="ldcs")
                    else:
                        MEMSET(upad[:, 0:30], 0.0)
                    for j in range(4):
                        if has_s:
                            CP(upad[:, 0:30], upre[:, j, :])
                        wa = ws.get(wv(w_in[l, :, j * 128:(j + 1) * 128]), KD, 128)
                        wb = ws.get(wv(w_in[l, :, O_GLU_B + j * 128:O_GLU_B + (j + 1) * 128]), KD, 128)
                        for b, (b0, bw) in enumerate(blocks):
                            pa = big.next()
                            pb = big.next()
                            for k in range(KD):
                                MM(pa[:, :bw], wa[:, k, :], hT[k][b], start=(k == 0), stop=(k == KD - 1))
                            for k in range(KD):
                                MM(pb[:, :bw], wb[:, k, :], hT[k][b], start=(k == 0), stop=(k == KD - 1))
                            sg = scr.next()
                            ACT(sg[:, :bw], pb[:, :bw], AF.Sigmoid)
                            TT(upad[:, 30 + b0:30 + b0 + bw], pa[:, :bw], sg[:, :bw], ALU.mult)
                        cw0 = PV_CW + j * 31
                        caj = ca[j][:, 0:1024]
                        TS(caj, upad[:, 0:1024], pvl[:, cw0:cw0 + 1], ALU.mult, pvl[:, PV_CB + j:PV_CB + j + 1], ALU.add)
                        for i in range(1, CONF_KW):
                            STT(caj, upad[:, i:i + 1024], pvl[:, cw0 + i:cw0 + i + 1], caj, ALU.mult, ALU.add)
                        if not has_s:
                            ACT(upre[:, j, :], upad[:, 1024:1054], AF.Copy)
                        else:
                            p = small.next()
                            TR(p[:30, :], upad[:, 1024:1054], ident)
                            CP(cpo[:30, j * 128:(j + 1) * 128], p[:30, :])
                            us_ = upad[:, 30 + 1024:30 + 1024 + NS]
                            for rt in range(4):
                                p = small.next()
                                TR(p[:, :120], cs_tok[:120, rt, j * 128:(j + 1) * 128], ident[:120, :120])
                                CP(cstT[:, 4 * rt:4 * rt + 4, 0:30], p[:, :120].v(lambda a: a.rearrange("p (s j) -> p s j", j=30)))
                            CP(cstT[:, :, 30], us_)
                            prod = scr.next()[:, 0:NS * 31].v(lambda a: a.rearrange("p (s j) -> p s j", j=31))
                            TT(prod, cstT, pvl[:, cw0:cw0 + 31].v(lambda a: a.unsqueeze(1).to_broadcast([128, NS, 31])), ALU.mult)
                            RED(ca[j][:, 1024:1024 + NS], prod)
                            TS(ca[j][:, 1024:1024 + NS], ca[j][:, 1024:1024 + NS], pvl[:, PV_CB + j:PV_CB + j + 1], ALU.add)
                            p = small.next()
                            TR(p[:NS, :], us_, ident)
                            CP(cso[:NS, j * 128:(j + 1) * 128], p[:NS, :])
                    if has_s:
                        DMA("sp", o_conf_p[l], cpo, sem="o_cpo")
                        DMA("sp", o_conf_s[l, :, 29, :], cso, sem="o_cso")
                    ckpt('conf')
                    for b, (b0, bw) in enumerate(blocks):
                        p1 = big.next()
                        p2 = big.next()
                        for j in range(4):
                            cb_ = sqb_tile()
                            ACT(cb_[:, :bw], ca[j][:, b0:b0 + bw], AF.Copy)
                            MM(p1[:, :bw], ones_b, cb_[:, :bw], start=(j == 0), stop=(j == 3))
                        for j in range(4):
                            sq = sqb_tile()
                            ACT(sq[:, :bw], ca[j][:, b0:b0 + bw], AF.Square)
                            MM(p2[:, :bw], ones_b, sq[:, :bw], start=(j == 0), stop=(j == 3))
                        mean, msq, var = st1, st2, st3
                        ACT(mean[:, :bw], p1[:, :bw], AF.Identity, scale=1.0 / CONF_CH)
                        TT(msq[:, :bw], mean[:, :bw], mean[:, :bw], ALU.mult)
                        STT(var[:, :bw], p2[:, :bw], 1.0 / CONF_CH, msq[:, :bw], ALU.mult, ALU.subtract)
                        ACT(var[:, :bw], var[:, :bw], AF.Ln, bias=eps5)
                        ACT(var[:, :bw], var[:, :bw], AF.Exp, scale=-0.5)
                        for j in range(4):
                            t = scr.next()
                            TT(t[:, :bw], ca[j][:, b0:b0 + bw], mean[:, :bw], ALU.subtract)
                            TT(t[:, :bw], t[:, :bw], var[:, :bw], ALU.mult)
                            ACT(caT[j][:, b0:b0 + bw], t[:, :bw], AF.Silu, scale=pvl[:, PV_LNG + j:PV_LNG + j + 1],
                                bias=pvl[:, PV_LNB + j:PV_LNB + j + 1])

                    ckpt('confln')
                    alias_barrier(PH1, PH2)
                    wbg = ws.get(wv(w_in[l, :, O_BETA:O_BETA + 8]), KD, 8)
                    pbg = small.next()
                    for c in range(NCH):
                        for k in range(KD):
                            MM(pbg[:, c * 8:(c + 1) * 8], hT[k][c // 4][:, (c % 4) * 128:(c % 4 + 1) * 128], wbg[:, k, :],
                               start=(k == 0), stop=(k == KD - 1))
                    pbg3 = pbg[:, 0:64].v(lambda a: a.rearrange("p (c e) -> p c e", e=8))

                    def v3(t):
                        return t.v(lambda a: a.rearrange("p (c e) -> p c e", e=4))
                    ACT(v3(beta_c), pbg3[:, :, 0:4], AF.Sigmoid)
                    TT(v3(tmp_c), pbg3[:, :, 4:8], v3(pvl[:, PV_DTB:PV_DTB + 32]), ALU.add)
                    ACT(tmp_c, tmp_c, AF.Exp)
                    ACT(tmp_c, tmp_c, AF.Ln, bias=onec)
                    TT(g_c, tmp_c, negA, ALU.mult)
                    pgc = small.next()
                    pgl = small.next()
                    for c in range(NCH):
                        MM(pgc[:, c * 4:(c + 1) * 4], tri, g_c[:, c * 4:(c + 1) * 4])
                    for c in range(NCH):
                        MM(pgl[:, c * 4:(c + 1) * 4], ones, g_c[:, c * 4:(c + 1) * 4])
                    CP(gc_c, pgc[:, 0:32])
                    ACT(ngc_c, pgc[:, 0:32], AF.Identity, scale=-1.0)
                    ACT(egc_c, pgc[:, 0:32], AF.Exp)
                    TT(bgc_c, beta_c, egc_c, ALU.mult)
                    ACT(nbeta_c, beta_c, AF.Identity, scale=-1.0)
                    TT(tmp2_c, pgl[:, 0:32], gc_c, ALU.subtract)
                    ACT(kdec_c, tmp2_c, AF.Exp)
                    ACT(egl_c, pgl[:, 0:32], AF.Exp)
                    if has_s:
                        pbs = small.next()
                        for k in range(KD):
                            MM(pbs[:NS, 0:8], hT[k][2], wbg[:, k, :], start=(k == 0), stop=(k == KD - 1))
                        beta_s = T(cols_t[:NS, 11, 0:4], colr.bufs)
                        eg_s = T(cols_t[:NS, 11, 4:8], colr.bufs)
                        tmp_s = T(cols_t[:NS, 11, 8:12], colr.bufs)
                        ACT(beta_s, pbs[:NS, 0:4], AF.Sigmoid)
                        TT(tmp_s, pbs[:NS, 4:8], pvl[:NS, PV_DTB:PV_DTB + 4], ALU.add)
                        ACT(tmp_s, tmp_s, AF.Exp)
                        ACT(tmp_s, tmp_s, AF.Ln, bias=onec[:NS, :])
                        TT(tmp_s, tmp_s, negA[:NS, 0:4], ALU.mult)
                        ACT(eg_s, tmp_s, AF.Exp)
                    else:
                        MEMSET(pad[:, 0:3], 0.0)

                    ckpt('dprol')
                    for h in range(4):
                        wq = [ws.get(wv(w_in[l, :, off + h * 128:off + (h + 1) * 128]), KD, 128) for off in (O_Q, O_K, O_V, O_Z)]
                        for ci in range(3):
                            gch = ci * 4 + h
                            if has_s:
                                CP(pad[:, 0:3], qpre[:, gch, :])
                            for b, (b0, bw) in enumerate(blocks):
                                ps = big.next()
                                for k in range(KD):
                                    MM(ps[:, :bw], wq[ci][:, k, :], hT[k][b], start=(k == 0), stop=(k == KD - 1))
                                ACT(pad[:, 3 + b0:3 + b0 + bw], ps[:, :bw], AF.Copy)
                            sw0 = PV_SW + gch * 4
                            qc = qkv[ci][:, 0:1024]
                            TS(qc, pad[:, 0:1024], pvl[:, sw0:sw0 + 1], ALU.mult)
                            for i in range(1, 4):
                                STT(qc, pad[:, i:i + 1024], pvl[:, sw0 + i:sw0 + i + 1], qc, ALU.mult, ALU.add)
                            if not has_s:
                                ACT(qpre[:, gch, :], pad[:, 1024:1027], AF.Copy)
                            else:
                                p = small.next()
                                TR(p[:3, :], pad[:, 1024:1027], ident)
                                piece_out(o_short_p[l, :, gch * 128:(gch + 1) * 128], p[:3, :], 3)
                                xs_ = pad[:, 1027:1027 + NS]
                                qs_ = qkv[ci][:, 1024:1024 + NS]
                                st3v = sstT[:, gch, :].v(lambda a: a.rearrange("p (s j) -> p s j", j=3))
                                TS(qs_, xs_, pvl[:, sw0 + 3:sw0 + 4], ALU.mult)
                                for i in range(3):
                                    STT(qs_, st3v[:, :, i], pvl[:, sw0 + i:sw0 + i + 1], qs_, ALU.mult, ALU.add)
                                p = small.next()
                                TR(p[:NS, :], xs_, ident)
                                piece_out(o_short_s[l, :, 2, gch * 128:(gch + 1) * 128], p[:NS, :], NS)
                            ACT(qkv[ci][:, 0:W], qkv[ci][:, 0:W], AF.Silu)
                            if ci < 2:
                                for b, (b0, bw) in enumerate(blocks):
                                    sq = sqb_tile()
                                    ACT(sq[:, :bw], qkv[ci][:, b0:b0 + bw], AF.Square)
                                    ps = big.next()
                                    MM(ps[:, :bw], ones_b, sq[:, :bw])
                                    rn = scr.next()
                                    ACT(rn[:, :bw], ps[:, :bw], AF.Ln, bias=eps6)
                                    ACT(rn[:, :bw], rn[:, :bw], AF.Exp, scale=-0.5)
                                    if ci == 0:
                                        STT(qkv[ci][:, b0:b0 + bw], qkv[ci][:, b0:b0 + bw], 128.0 ** -0.5, rn[:, :bw], ALU.mult, ALU.mult)
                                    else:
                                        TT(qkv[ci][:, b0:b0 + bw], qkv[ci][:, b0:b0 + bw], rn[:, :bw], ALU.mult)
                                    if bw != NS:
                                        CP(qkb[ci][:, b0:b0 + bw], qkv[ci][:, b0:b0 + bw], eng="pool")
                        for b, (b0, bw) in enumerate(blocks):
                            ps = big.next()
                            for k in range(KD):
                                MM(ps[:, :bw], wq[3][:, k, :], hT[k][b], start=(k == 0), stop=(k == KD - 1))
                            ACT(zs[:, b0:b0 + bw], ps[:, :bw], AF.Silu)
                        if not has_s:
                            MEMSET(S[h], 0.0)

                        ckpt('dhead')
                        def chunk_gen(c):
                            cs = slice(c * 128, (c + 1) * 128)
                            ix = c * 4 + h
                            col = slice(ix, ix + 1)
                            qTc, kTc, vTc = qkv[0][:, cs], qkv[1][:, cs], qkv[2][:, cs]
                            qbc, kbc = qkb[0][:, cs], qkb[1][:, cs]
                            grep = smx.next()
                            TS(grep, ones, g_c[:, col], ALU.mult)
                            pg = small.next()
                            MM(pg, grep, tri)
                            yield
                            pre1 = smx.next()
                            STT(pre1, pg, -1.0, nm_strict, ALU.mult, ALU.add)
                            pre2 = smx.next()
                            TT(pre2, pg, nm_inclT, ALU.add)
                            egr = smx.next()
                            ACT(egr, pg, AF.Exp)
                            yield
                            Ds = smx.next()
                            ACT(Ds, pre1, AF.Exp, bias=gc_c[:, col])
                            DT = smx.next()
                            ACT(DT, pre2, AF.Exp, bias=ngc_c[:, col])
                            qgc = lgb["qg"].next()
                            TT(qgc, qTc, egr, ALU.mult)
                            yield
                            pk = small.next()
                            MM(pk, kbc, kbc)
                            Nm = smx.next()
                            STT(Nm, pk, nbeta_c[:, col], Ds, ALU.mult, ALU.mult)
                            yield
                            pq = small.next()
                            MM(pq, kbc, qbc)
                            qkT = lgb["qkT"].next()
                            TT(qkT, pq, DT, ALU.mult)
                            yield
                            Nbd = smx.next()
                            TT(Nbd, Nm, bm[:, 0, :], ALU.mult)
                            No = lg["No"].next()
                            TT(No, Nm, bm[:, 1, :], ALU.mult, eng="pool")
                            pt = small.next()
                            TR(pt, Nbd, ident)
                            Mm = smx.next()
                            ACT(Mm, pt, AF.Copy)
                            X = smx.next()
                            TT(X, Mm, ident, ALU.add)
                            yield
                            ptk = small.next()
                            TR(ptk, kTc, ident)
                            kbg = lgb["kbg"].next()
                            ACT(kbg, ptk, AF.Identity, scale=bgc_c[:, col])
                            kdc = lgb["kdc"].next()
                            ACT(kdc, ptk, AF.Identity, scale=kdec_c[:, col])
                            yield
                            ptv = small.next()
                            TR(ptv, vTc, ident)
                            vb = lgb["vb"].next()
                            ACT(vb, ptv, AF.Identity, scale=beta_c[:, col])
                            Pm, Pn = Mm, Nbd
                            for lev in range(1, 6):
                                yield
                                pp2 = small.next()
                                MM(pp2, Pm, Pn)
                                Pn2 = smx.next()
                                ACT(Pn2, pp2, AF.Copy)
                                if lev < 5:
                                    pp1 = small.next()
                                    MM(pp1, Pn, Pm)
                                    Pm2 = smx.next()
                                    CP(Pm2, pp1)
                                yield
                                px = small.next()
                                MM(px, Pn2, X)
                                X2 = smx.next()
                                TT(X2, px, X, ALU.add)
                                X, Pn = X2, Pn2
                                if lev < 5:
                                    Pm = Pm2
                            yield
                            pxt = small.next()
                            TR(pxt, X, ident)
                            Yt = smx.next()
                            ACT(Yt, pxt, AF.Copy)
                            pt1 = small.next()
                            MM(pt1, No, X)
                            T1 = smx.next()
                            CP(T1, pt1)
                            yield
                            pxx = small.next()
                            MM(pxx, Yt, T1)
                            X2 = smx.next()
                            TT(X2, pxx, X, ALU.add)
                            X = X2
                            yield
                            Xb = lgb["Xb"].next()
                            CP(Xb, X, eng="pool")
                            yield
                            pu = small.next()
                            MM(pu, Xb, vb)
                            u = lg["u"].next()
                            ACT(u, pu, AF.Copy)
                            yield
                            pw = small.next()
                            MM(pw, kbg, Xb)
                            wT = lgb["wT"].next()
                            CP(wT, pw)
                            yield
                            Sb = lgb["Sb"].next()
                            CP(Sb, S[h], eng="pool")
                            pvn = small.next()
                            MM(pvn, wT, Sb)
                            vn = lgb["vn"].next()
                            TT(vn, u, pvn, ALU.subtract)
                            yield
                            po = small.next()
                            MM(po, qgc, Sb, start=True, stop=False)
                            MM(po, qkT, vn, start=False, stop=True)
                            yield
                            pS = small.next()
                            MM(pS, kdc, vn)
                            STT(S[h], S[h], egl_c[:, col], pS, ALU.mult, ALU.add)
                            yield
                            osq = smx.next()
                            ssq = scol.next()
                            ACT(osq, po, AF.Square, accum=ssq[:, 0:1])
                            ACT(ssq[:, 0:1], ssq[:, 0:1], AF.Ln, scale=1.0 / 128, bias=eps6)
                            ACT(ssq[:, 0:1], ssq[:, 0:1], AF.Exp, scale=-0.5)
                            yield
                            on = smx.next()
                            ACT(on, po, AF.Identity, scale=ssq[:, 0:1])
                            yield
                            pT = small.next()
                            TR(pT, on, ident)
                            STT(ogT[h][:, cs], pT, pvl[:, PV_DNG:PV_DNG + 1], zs[:, cs], ALU.mult, ALU.mult)


                        xtiles = sm_extra + [lg[n_].tiles[2] for n_ in lg] + [lgb[n_].tiles[2] for n_ in lgb]
                        alias_barrier([st1, st2, st3, Ss], xtiles)
                        pending = list(range(NCH))
                        active = []
                        rnd = 0
                        while pending or active:
                            if pending and len(active) < GCH and (rnd % STAG == 0 or not active):
                                active.append(chunk_gen(pending.pop(0)))
                            for g_ in list(active):
                                try:
                                    next(g_)
                                except StopIteration:
                                    active.remove(g_)
                            rnd += 1
                        alias_barrier(xtiles, [st1, st2, st3, Ss])
                        ckpt('dchunks')
                        if pi == 0 and l + 1 < n_layers:
                            if h == 0:
                                DMA("sp", pvn, pvec[l + 1, :, 0:64], sem="ldpn")
                            for n2 in range(h * 6, h * 6 + 6):
                                mod_piece(l + 1, n2)
                            if h == 3:
                                mod_finish(l + 1)
                        if has_s:
                            DMA("sp", o_delta_p[l, h], S[h], sem="o_S%d" % h)
                            sc = slice(1024, 1024 + NS)
                            qs, ks, vs = qkv[0][:, sc], qkv[1][:, sc], qkv[2][:, sc]
                            DMA("sp", Ss, st_delta[l, :, h].rearrange("s p v -> p s v"), sem="ldSs")

                            def bcast16(colap):
                                dg = sm.next()
                                TS(dg[:NS, :NS], ident[:NS, :NS], colap, ALU.mult)
                                pB = small.next()
                                MM(pB[:, :NS], ones[:NS, :], dg[:NS, :NS])
                                o_ = scol.next()
                                CP(o_, pB[:, :NS])
                                return o_
                            bbc = bcast16(beta_s[:, h:h + 1])
                            ebc = bcast16(eg_s[:, h:h + 1])
                            t0 = scol.next()
                            TT(t0, qs, ks, ALU.mult)
                            pqk = small.next()
                            MM(pqk[:, :NS], ones, t0)
                            qkbc = scol.next()
                            CP(qkbc, pqk[:, :NS])
                            pkS = small.next()
                            pqS = small.next()
                            for s in range(NS):
                                MM(pkS[:, s:s + 1], Ss[:, s, :], ks[:, s:s + 1])
                            for s in range(NS):
                                MM(pqS[:, s:s + 1], Ss[:, s, :], qs[:, s:s + 1])
                            t1 = scol.next()
                            TT(t1, pkS[:, :NS], ebc, ALU.mult)
                            TT(t1, vs, t1, ALU.subtract)
                            vnT = scol.next()
                            TT(vnT, t1, bbc, ALU.mult)
                            oT = scol.next()
                            TT(oT, pqS[:, :NS], ebc, ALU.mult)
                            t2 = scol.next()
                            TT(t2, qkbc, vnT, ALU.mult)
                            TT(oT, oT, t2, ALU.add)
                            t3 = scol.next()
                            TT(t3, oT, oT, ALU.mult)
                            pss = small.next()
                            MM(pss[:, :NS], ones, t3)
                            rs_ = scol.next()
                            ACT(rs_, pss[:, :NS], AF.Ln, scale=1.0 / 128, bias=eps6)
                            ACT(rs_, rs_, AF.Exp, scale=-0.5)
                            TT(oT, oT, rs_, ALU.mult)
                            STT(ogT[h][:, sc], oT, pvl[:, PV_DNG:PV_DNG + 1], zs[:, sc], ALU.mult, ALU.mult)
                            pkt = small.next()
                            TR(pkt[:NS, :], ks, ident)
                            k_tok = scr.next()[:, 0:128]
                            CP(k_tok[:NS, :], pkt[:NS, :])
                            pvt = small.next()
                            TR(pvt[:NS, :], vnT, ident)
                            vn_tok = scr.next()[:, 0:128]
                            CP(vn_tok[:NS, :], pvt[:NS, :])
                            for s in range(NS):
                                rhs_s = sm.next()
                                TS(rhs_s[:NS, :], vn_tok[:NS, :], ident[:NS, s:s + 1], ALU.mult)
                                po_ = small.next()
                                MM(po_, k_tok[:NS, :], rhs_s[:NS, :])
                                STT(Ss[:, s, :], Ss[:, s, :], ebc[:, s:s + 1], po_, ALU.mult, ALU.add)
                            DMA("sp", o_delta_s[l, :, h].rearrange("s p v -> p s v"), Ss, sem="o_Ss")

                    ckpt('delta')
                    alias_barrier(PH2, PH3)
                    for c2 in range(4):
                        wco = ws.get(wv(w_conf_out[l, :, c2 * 256:(c2 + 1) * 256]), 4, 256)
                        wdo = ws.get(wv(w_delta_out[l, :, c2 * 256:(c2 + 1) * 256]), 4, 256)
                        wma = ws.get(wv(w_in[l, :, O_MA + c2 * 256:O_MA + (c2 + 1) * 256]), KD, 256)
                        wmb = ws.get(wv(w_in[l, :, O_MB + c2 * 256:O_MB + (c2 + 1) * 256]), KD, 256)
                        for c1 in range(2):
                            c = c2 * 2 + c1
                            w0 = c1 * 128
                            for b, (b0, bw) in enumerate(blocks):
                                pya = big.next()
                                pma = big.next()
                                for k in range(4):
                                    MM(pya[:, :bw], wco[:, k, w0:w0 + 128], caT[k][:, b0:b0 + bw], start=(k == 0), stop=(k == 3))
                                for k in range(KD):
                                    MM(pma[:, :bw], wma[:, k, w0:w0 + 128], hT[k][b], start=(k == 0), stop=(k == KD - 1))
                                sa = scr.next()
                                ACT(sa[:, :bw], pma[:, :bw], AF.Sigmoid)
                                TT(sa[:, :bw], pya[:, :bw], sa[:, :bw], ALU.mult)
                                pyb = big.next()
                                pmb = big.next()
                                for k in range(4):
                                    MM(pyb[:, :bw], wdo[:, k, w0:w0 + 128], ogT[k][:, b0:b0 + bw], start=(k == 0), stop=(k == 3))
                                for k in range(KD):
                                    MM(pmb[:, :bw], wmb[:, k, w0:w0 + 128], hT[k][b], start=(k == 0), stop=(k == KD - 1))
                                sb_ = scr.next()
                                ACT(sb_[:, :bw], pmb[:, :bw], AF.Sigmoid)
                                TT(sb_[:, :bw], pyb[:, :bw], sb_[:, :bw], ALU.mult)
                                TT(merged[c][:, b0:b0 + bw], sa[:, :bw], sb_[:, :bw], ALU.add)
                    ckpt('merge')
                    for c2 in range(4):
                        wmo = ws.get(wv(w_merge_out[l, :, c2 * 256:(c2 + 1) * 256]), KD, 256)
                        for c1 in range(2):
                            c = c2 * 2 + c1
                            for b, (b0, bw) in enumerate(blocks):
                                ps = big.next()
                                for k in range(KD):
                                    MM(ps[:, :bw], wmo[:, k, c1 * 128:(c1 + 1) * 128], merged[k][:, b0:b0 + bw], start=(k == 0), stop=(k == KD - 1))
                                resid_update(c, ps, bw, gbs[b], 16)

                    ckpt('mergeout')
                    rmsnorm_mod(blocks, gbs, 8, 24)
                    for fh in range(2):
                        alias_barrier(PH1 + PH2 + PH3 + PH4, PH4)
                        for jj2 in range(6):
                            j0 = fh * 11 + jj2 * 2
                            nj = min(2, fh * 11 + 11 - j0)
                            wg = ws.get(wv(w_ffn_in[l, :, j0 * 128:(j0 + nj) * 128]), KD, nj * 128)
                            wu = ws.get(wv(w_ffn_in[l, :, D_FF + j0 * 128:D_FF + (j0 + nj) * 128]), KD, nj * 128)
                            for j1 in range(nj):
                                jl = jj2 * 2 + j1
                                for b, (b0, bw) in enumerate(blocks):
                                    pgt = big.next()
                                    pup = big.next()
                                    for k in range(KD):
                                        MM(pgt[:, :bw], wg[:, k, j1 * 128:(j1 + 1) * 128], hT[k][b], start=(k == 0), stop=(k == KD - 1))
                                    for k in range(KD):
                                        MM(pup[:, :bw], wu[:, k, j1 * 128:(j1 + 1) * 128], hT[k][b], start=(k == 0), stop=(k == KD - 1))
                                    sg = scr.next()
                                    ACT(sg[:, :bw], pgt[:, :bw], AF.Silu)
                                    TT(actT[jl][:, b0:b0 + bw], sg[:, :bw], pup[:, :bw], ALU.mult)
                        for c in range(KD):
                            wo = ws.get(wv(w_ffn_out[l, fh * 1408:(fh + 1) * 1408, c * 128:(c + 1) * 128]), 11, 128)
                            for b, (b0, bw) in enumerate(blocks):
                                ps = big.next()
                                for j in range(11):
                                    MM(ps[:, :bw], wo[:, j, :], actT[j][:, b0:b0 + bw], start=(j == 0), stop=(j == 10))
                                resid_update(c, ps, bw, gbs[b], 40)

            for gb, (g0, gw) in enumerate(GB):
                ps = big.next()
                for k in range(KD):
                    sq = sqb_tile()
                    ACT(sq[:, :gw], xT[k][gb], AF.Square)
                    MM(ps[:, :gw], ones_b, sq[:, :gw], start=(k == 0), stop=(k == KD - 1))
                rs = st1
                ACT(rs[:, :gw], ps[:, :gw], AF.Ln, scale=1.0 / D, bias=eps6)
                ACT(rs[:, :gw], rs[:, :gw], AF.Exp, scale=-0.5)
                nt = max(1, gw // 128)
                tw = min(gw, 128)
                for t in range(nt):
                    o_ = ld[t % 2]
                    for k in range(KD):
                        tmp = sm.next()
                        STT(tmp[:, :tw], xT[k][gb][:, t * 128:t * 128 + tw], pvl[:, PV_FG + k:PV_FG + k + 1], rs[:, t * 128:t * 128 + tw],
                            ALU.mult, ALU.mult)
                        p = small.next()
                        TR(p[:tw, :], tmp[:, :tw], ident)
                        if k % 2 == 0:
                            ACT(o_[:tw, k * 128:(k + 1) * 128], p[:tw, :], AF.Copy)
                        else:
                            CP(o_[:tw, k * 128:(k + 1) * 128], p[:tw, :])
                    if gb < 4:
                        r0 = g0 + t * 128
                        DMA("sp", y_p[r0:r0 + 128, :], o_, sem="o_y%d" % (t % 2))
                    else:
                        DMA("sp", y_s[:, :], o_[:NS, :], sem="o_y%d" % (t % 2))

        def build_w(Pq):
            try:
                build(Pq)
            except StopBuild:
                pass
            for sname in sem_names:
                if (sname.startswith("o_") or sname.startswith("w") or sname.startswith("ld")) and Pq.cnt.get(sname, 0) > 0:
                    Pq.q["sp"].append(("wait", sname, Pq.cnt[sname]))
        P0 = Prog()
        build_w(P0)
        ws.recording = False
        P = Prog()
        build_w(P)
        print("ops:", P.nops, {k: len(v) for k, v in P.q.items()}, "weights:", len(ws.specs))

        semh = {n: es.enter_context(nc.semaphore(n)) for n in sem_names}
        block = es.enter_context(nc.Block())

        def run(eng, items):
            for it in items:
                if it[0] == "wait":
                    eng.wait_ge(semh[it[1]], it[2])
                else:
                    it[1](eng).then_inc(semh[it[2]], it[3])

        @block.tensor
        def _(eng):
            run(eng, P.q["pe"])

        @block.scalar
        def _(eng):
            run(eng, P.q["act"])

        @block.vector
        def _(eng):
            run(eng, P.q["dve"])

        @block.gpsimd
        def _(eng):
            run(eng, P.q["pool"])

        @block.sync
        def _(eng):
            run(eng, P.q["sp"])
    return nc


def make_consts():
    c = np.zeros((128, 5, 128), np.float32)
    i = np.arange(128)
    c[:, 0, :] = np.eye(128, dtype=np.float32)
    c[:, 1, :] = 1.0
    c[:, 2, :] = (i[:, None] <= i[None, :]).astype(np.float32)
    c[:, 3, :] = np.where(i[:, None] > i[None, :], 0.0, NEG).astype(np.float32)
    c[:, 4, :] = np.where(i[None, :] >= i[:, None], 0.0, NEG).astype(np.float32)
    return c


def make_bmask():
    i = np.arange(128)
    b64 = (i[:, None] // 64 == i[None, :] // 64)
    m = np.zeros((128, 3, 128), np.float32)
    m[:, 0, :] = b64
    m[:, 1, :] = ~b64
    m[:, 2, :] = 1.0
    return m


def make_pvec(inp):
    pv = np.zeros((DEPTH, 128, PV_N), np.float32)
    for l in range(DEPTH):
        pv[l, :, PV_G1:PV_G1 + 8] = inp["norm1_g"][l].reshape(8, 128).T
        pv[l, :, PV_G2:PV_G2 + 8] = inp["norm2_g"][l].reshape(8, 128).T
        pv[l, :, PV_BADA:PV_BADA + 48] = inp["b_ada"][l].reshape(48, 128).T
        cw = inp["conf_dw_w"][l]
        for j in range(4):
            pv[l, :, PV_CW + j * 31:PV_CW + (j + 1) * 31] = cw[:, j * 128:(j + 1) * 128].T
        pv[l, :, PV_CB:PV_CB + 4] = inp["conf_dw_b"][l].reshape(4, 128).T
        pv[l, :, PV_LNG:PV_LNG + 4] = inp["conf_ln_g"][l].reshape(4, 128).T
        pv[l, :, PV_LNB:PV_LNB + 4] = inp["conf_ln_b"][l].reshape(4, 128).T
        sw = inp["short_conv_w"][l]
        for g in range(12):
            pv[l, :, PV_SW + g * 4:PV_SW + (g + 1) * 4] = sw[:, g * 128:(g + 1) * 128].T
        pv[l, :, PV_DNG] = inp["delta_norm_g"][l]
        pv[l, :, PV_ALOG:PV_ALOG + 32] = np.tile(inp["a_log"][l], 8)[None, :]
        pv[l, :, PV_DTB:PV_DTB + 32] = np.tile(inp["dt_bias"][l], 8)[None, :]
        pv[l, :, PV_FG:PV_FG + 8] = inp["final_norm_g"].reshape(8, 128).T
    return pv


_NC_CACHE = {}


def kernel(**inp):
    inp = {k: np.asarray(v) for k, v in inp.items()}
    n = 8
    if "nc" not in _NC_CACHE:
        _NC_CACHE["nc"] = build_nc()
    nc = _NC_CACHE["nc"]
    consts = make_consts()
    bmask_np = make_bmask()
    pv = make_pvec(inp)
    shared = {k: np.ascontiguousarray(inp[k], dtype=np.float32) for k in
              ("w_ada", "w_in", "w_conf_out", "w_delta_out", "w_merge_out", "w_ffn_in", "w_ffn_out")}
    in_maps = []
    for i in range(n):
        s0 = i * NS
        m = dict(shared)
        m["x_p"] = np.ascontiguousarray(inp["x_prompt"][i])
        m["x_s"] = np.ascontiguousarray(inp["x_sample"][s0:s0 + NS, 0, :])
        m["c_in"] = np.ascontiguousarray(np.concatenate([inp["c_prompt"][i:i + 1], inp["c_sample"][s0:s0 + NS]], axis=0))
        m["st_conf"] = np.ascontiguousarray(inp["state_conformer_conv"][:, s0:s0 + NS])
        m["st_short"] = np.ascontiguousarray(inp["state_short_conv"][:, s0:s0 + NS])
        m["st_delta"] = np.ascontiguousarray(inp["state_delta"][:, s0:s0 + NS])
        m["pvec"] = pv
        m["consts"] = consts
        m["bmask"] = bmask_np
        in_maps.append(m)
    res = run_bass_kernel_spmd(nc, in_maps, core_ids=list(range(n)))
    R = res.results
    y_p = np.stack([R[i]["y_p"] for i in range(n)], axis=0)
    y_s = np.concatenate([R[i]["y_s"] for i in range(n)], axis=0)[:, None, :]
    conf_p = np.stack([R[i]["o_conf_p"] for i in range(n)], axis=1)
    conf_s = np.concatenate([R[i]["o_conf_s"] for i in range(n)], axis=1)
    short_p = np.stack([R[i]["o_short_p"] for i in range(n)], axis=1)
    short_s = np.concatenate([R[i]["o_short_s"] for i in range(n)], axis=1)
    delta_p = np.stack([R[i]["o_delta_p"] for i in range(n)], axis=1)
    delta_s = np.concatenate([R[i]["o_delta_s"] for i in range(n)], axis=1)
    return tuple(np.ascontiguousarray(a, dtype=np.float32) for a in
                 (y_p, y_s, conf_p, conf_s, short_p, short_s, delta_p, delta_s))
```

```python
import numpy as np
from contextlib import ExitStack
import concourse.bass as bass
import concourse.mybir as mybir
from concourse.bass_utils import run_bass_kernel_spmd

F32 = mybir.dt.float32
BF16 = mybir.dt.bfloat16
AF = mybir.ActivationFunctionType
ALU = mybir.AluOpType
AX = mybir.AxisListType

DEPTH = 4
D = 1024
KD = 8
SEQ = 2048
NS = 16
NTOK = SEQ + NS
CONF_CH = 512
CONF_KW = 31
D_FF = 2816
KF = 22
O_GLU_B = 512
O_Q = 1024
O_K = 1536
O_V = 2048
O_Z = 2560
O_BETA = 3072
O_MA = 3080
O_MB = 4104
IN_DIM = 5128
NEG = -1.0e6

SAME_ENGINE_SYNC = True
GCH = 3
STAG = 10
N_LAYERS = DEPTH

PV_G1 = 0
PV_G2 = 8
PV_BADA = 16
PV_CW = 64
PV_CB = 188
PV_LNG = 192
PV_LNB = 196
PV_SW = 200
PV_DNG = 248
PV_ALOG = 249
PV_DTB = 281
PV_FG = 313
PV_N = 321


class Buf:
    __slots__ = ("name", "w", "r", "psum")

    def __init__(self, name="", psum=False):
        self.name = name
        self.w = None
        self.r = {}
        self.psum = psum


class T:
    __slots__ = ("ap", "bufs")

    def __init__(self, ap, bufs=None):
        self.ap = ap
        if bufs is None:
            bufs = [Buf()]
        elif isinstance(bufs, Buf):
            bufs = [bufs]
        self.bufs = bufs

    def __getitem__(self, idx):
        return T(self.ap[idx], self.bufs)

    def v(self, fn):
        return T(fn(self.ap), self.bufs)


def _ap(x):
    return x.ap if isinstance(x, T) else x


def _bufs(*xs):
    out = []
    for x in xs:
        if isinstance(x, T):
            out.extend(x.bufs)
    return out


class Ring:
    def __init__(self, tiles):
        self.tiles = tiles
        self.i = 0

    def next(self):
        t = self.tiles[self.i % len(self.tiles)]
        self.i += 1
        return t


class Prog:
    ENGS = ("pe", "act", "dve", "pool", "sp")

    def __init__(self):
        self.q = {n: [] for n in self.ENGS}
        self.cnt = {}
        self.seen = {n: {} for n in self.ENGS}
        self.nops = 0

    def op(self, e, fn, reads=(), writes=(), sem=None, inc=1):
        deps = {}
        for b in reads:
            if b.w is not None:
                if deps.get(b.w[0], 0) < b.w[1]:
                    deps[b.w[0]] = b.w[1]
            if b.psum:
                for s, v in b.r.items():
                    if s != e and deps.get(s, 0) < v:
                        deps[s] = v
        for b in writes:
            if b.w is not None:
                if deps.get(b.w[0], 0) < b.w[1]:
                    deps[b.w[0]] = b.w[1]
            for s, v in b.r.items():
                if deps.get(s, 0) < v:
                    deps[s] = v
        seen = self.seen[e]
        q = self.q[e]
        for s, v in deps.items():
            if s == e and (e == "pe" or not SAME_ENGINE_SYNC):
                continue
            if seen.get(s, 0) >= v:
                continue
            seen[s] = v
            q.append(("wait", s, v))
        sname = sem or e
        val = self.cnt.get(sname, 0) + inc
        self.cnt[sname] = val
        q.append(("op", fn, sname, inc))
        for b in reads:
            if b.r.get(sname, 0) < val:
                b.r[sname] = val
        for b in writes:
            b.w = (sname, val)
            b.r = {}
        self.nops += 1
        return val

    def MM(self, out, lhsT, rhs, start=True, stop=True):
        o, l, r = _ap(out), _ap(lhsT), _ap(rhs)
        self.op("pe", lambda e: e.matmul(o, lhsT=l, rhs=r, start=start, stop=stop), _bufs(lhsT, rhs), _bufs(out))

    def TR(self, out, in_, ident):
        o, i, d = _ap(out), _ap(in_), _ap(ident)
        self.op("pe", lambda e: e.transpose(o, i, d), _bufs(in_, ident), _bufs(out))

    def ACT(self, out, in_, func, bias=None, scale=None, accum=None):
        o, i = _ap(out), _ap(in_)
        kw = {}
        if bias is None and isinstance(scale, T):
            bias = self.zeroc[0:o.shape[0], :] if o.shape[0] != 128 else self.zeroc
        if bias is not None:
            kw["bias"] = _ap(bias)
        if scale is not None:
            kw["scale"] = _ap(scale)
        if accum is not None:
            kw["accum_out"] = _ap(accum)
        self.op("act", lambda e: e.activation(out=o, in_=i, func=func, **kw), _bufs(in_, bias, scale), _bufs(out, accum))

    def TT(self, out, in0, in1, op, eng="dve"):
        o, a, b = _ap(out), _ap(in0), _ap(in1)
        self.op(eng, lambda e: e.tensor_tensor(out=o, in0=a, in1=b, op=op), _bufs(in0, in1), _bufs(out))

    def TS(self, out, in0, s1, op0, s2=None, op1=None, eng="dve"):
        o, a, x1, x2 = _ap(out), _ap(in0), _ap(s1), _ap(s2)
        if op1 is None:
            self.op(eng, lambda e: e.tensor_scalar(out=o, in0=a, scalar1=x1, scalar2=None, op0=op0), _bufs(in0, s1), _bufs(out))
        else:
            self.op(eng, lambda e: e.tensor_scalar(out=o, in0=a, scalar1=x1, scalar2=x2, op0=op0, op1=op1), _bufs(in0, s1, s2), _bufs(out))

    def STT(self, out, in0, scalar, in1, op0, op1, eng="dve"):
        o, a, s, b = _ap(out), _ap(in0), _ap(scalar), _ap(in1)
        self.op(eng, lambda e: e.scalar_tensor_tensor(out=o, in0=a, scalar=s, in1=b, op0=op0, op1=op1), _bufs(in0, scalar, in1), _bufs(out))

    def CP(self, out, in_, eng="dve"):
        o, i = _ap(out), _ap(in_)
        self.op(eng, lambda e: e.tensor_copy(out=o, in_=i), _bufs(in_), _bufs(out))

    def RED(self, out, in_, op=ALU.add, eng="dve"):
        o, i = _ap(out), _ap(in_)
        self.op(eng, lambda e: e.tensor_reduce(out=o, in_=i, axis=AX.X, op=op), _bufs(in_), _bufs(out))

    def RECIP(self, out, in_):
        o, i = _ap(out), _ap(in_)
        self.op("dve", lambda e: e.reciprocal(out=o, in_=i), _bufs(in_), _bufs(out))

    def MEMSET(self, out, val, eng="dve"):
        o = _ap(out)
        self.op(eng, lambda e: e.memset(o, val), (), _bufs(out))

    def DMA(self, q, out, in_, sem, **kw):
        o, i = _ap(out), _ap(in_)
        self.op(q, lambda e: e.dma_start(out=o, in_=i, **kw), _bufs(in_), _bufs(out), sem=sem, inc=16)


class WStream:
    NSLOT = 6
    LAG = 4
    ELEMS = 2048

    def __init__(self):
        self.specs = []
        self.recording = True

    def begin(self, P, slot_aps):
        self.P = P
        self.i = 0
        self.loaded = 0
        self.slots = [T(a) for a in slot_aps]

    def _view(self, j):
        _, kc, n = self.specs[j]
        t = self.slots[j % self.NSLOT]
        return T(t.ap[:, 0:kc * n].rearrange("p (k n) -> p k n", k=kc), t.bufs)

    def get(self, dram_ap, kc, n):
        assert kc * n <= self.ELEMS
        i = self.i
        self.i += 1
        if self.recording:
            self.specs.append((dram_ap, kc, n))
            return self._view(i)
        while self.loaded < len(self.specs) and (self.loaded < self.NSLOT or self.loaded <= i - self.LAG + self.NSLOT):
            j = self.loaded
            self.P.DMA("pool", self._view(j), self.specs[j][0], sem="w%d" % (j % self.NSLOT))
            self.loaded += 1
        assert self.loaded > i
        return self._view(i)


class StopBuild(Exception):
    pass


STOP_AT = [None]


def ckpt(name):
    if STOP_AT[0] == name:
        raise StopBuild(name)


def alias_barrier(olds, news):
    acc = {}
    for t in olds:
        for b in t.bufs:
            if b.w is not None and acc.get(b.w[0], 0) < b.w[1]:
                acc[b.w[0]] = b.w[1]
            for s, v in b.r.items():
                if acc.get(s, 0) < v:
                    acc[s] = v
    for t in news:
        for b in t.bufs:
            for s, v in acc.items():
                if b.r.get(s, 0) < v:
                    b.r[s] = v


def build_nc(n_layers=N_LAYERS):
    nc = bass.Bass("TRN2", target_bir_lowering=False)

    def din(name, shape):
        return nc.dram_tensor(name, list(shape), F32, kind="ExternalInput").ap()

    def dout(name, shape):
        return nc.dram_tensor(name, list(shape), F32, kind="ExternalOutput").ap()

    x_p = din("x_p", [SEQ, D])
    x_s = din("x_s", [NS, D])
    c_in = din("c_in", [NS + 1, D])
    st_conf = din("st_conf", [DEPTH, NS, 30, CONF_CH])
    st_short = din("st_short", [DEPTH, NS, 3, 1536])
    st_delta = din("st_delta", [DEPTH, NS, 4, 128, 128])
    w_ada = din("w_ada", [DEPTH, D, 6 * D])
    w_in = din("w_in", [DEPTH, D, IN_DIM])
    w_conf_out = din("w_conf_out", [DEPTH, CONF_CH, D])
    w_delta_out = din("w_delta_out", [DEPTH, 512, D])
    w_merge_out = din("w_merge_out", [DEPTH, D, D])
    w_ffn_in = din("w_ffn_in", [DEPTH, D, 2 * D_FF])
    w_ffn_out = din("w_ffn_out", [DEPTH, D_FF, D])
    pvec = din("pvec", [DEPTH, 128, PV_N])
    consts = din("consts", [128, 5, 128])
    bmask = din("bmask", [128, 3, 128])

    y_p = dout("y_p", [SEQ, D])
    y_s = dout("y_s", [NS, D])
    o_conf_p = dout("o_conf_p", [DEPTH, 30, CONF_CH])
    o_conf_s = dout("o_conf_s", [DEPTH, NS, 30, CONF_CH])
    o_short_p = dout("o_short_p", [DEPTH, 3, 1536])
    o_short_s = dout("o_short_s", [DEPTH, NS, 3, 1536])
    o_delta_p = dout("o_delta_p", [DEPTH, 4, 128, 128])
    o_delta_s = dout("o_delta_s", [DEPTH, NS, 4, 128, 128])

    with ExitStack() as es:
        def sb(name, shape, dt=F32):
            return es.enter_context(nc.sbuf_tensor(name, list(shape), dt))

        xT_t = sb("xT", [128, KD, NTOK])
        hT_t = sb("hT", [128, KD, 1040], BF16)
        A_CA = 1070
        A_QKV = 1043
        A_ZS = A_QKV + 3 * 1040
        R_WORDS = 1070 + 4 * 1040
        A_CAT = R_WORDS
        A_OGT = A_CAT + 2080
        ARENA_WORDS = A_OGT + 2080
        assert A_ZS + 1040 <= R_WORDS and 11 * 520 <= A_OGT
        arena = sb("arena", [128, ARENA_WORDS])
        wslots = [sb("wslot%d" % i, [128, WStream.ELEMS], BF16) for i in range(WStream.NSLOT)]
        scr_t = [sb("scr%d" % i, [128, 512]) for i in range(4)]
        st_t = [sb("st%d" % i, [128, 512]) for i in range(3)]
        sm_t = [sb("sm%d" % i, [128, 128]) for i in range(16)]
        LGN = ("qkT", "kdc", "kbg", "vb", "u", "wT", "vn", "qg", "No")
        lg_t = {n: [sb("%s%d" % (n, i), [128, 128]) for i in range(2)] for n in LGN}
        S_t = [sb("S%d" % h, [128, 128]) for h in range(4)]
        Ss_t = sb("Ss", [128, NS * 128])
        cols_t = sb("cols", [128, 12, 32])
        scol_t = sb("scol", [128, 12, 16])
        modT_t2 = [sb("modT%d" % i, [128, 48, NS + 1]) for i in range(2)]
        gs_t2 = [sb("gs%d" % i, [128, 16, NS + 1]) for i in range(2)]
        pvn_t = sb("pvn", [128, 64])
        cT_t = sb("cT", [128, KD, NS + 1], BF16)
        pv_t = sb("pv", [128, PV_N])
        negA_t = sb("negA", [128, 32])
        cst_t = sb("cst", [128, 6, 128])
        upre_t = sb("upre", [128, 4, 30])
        qpre_t = sb("qpre", [128, 12, 3])
        sstT_t = sb("sstT", [128, 12, 48])
        pc_t = [sb("pc%d" % i, [NS, 128]) for i in range(4)]
        qkb_t = sb("qkb", [128, 2, 1040], BF16)
        bm_t = sb("bmask_sb", [128, 3, 128], BF16)
        ps_t = [es.enter_context(nc.psum_tensor("ps%d" % i, [128, 512], F32)) for i in range(8)]
        print("SBUF bytes remaining per partition:", nc.sbuf_bytes_remaining)

        sem_names = ["pe", "act", "dve", "pool"] + ["w%d" % i for i in range(WStream.NSLOT)] + \
            ["ld0", "ld1", "ldc", "ldp", "ldpn", "ldbm", "ldcs", "ldss", "ldSs", "o_cpo", "o_cso", "o_pc0", "o_pc1", "o_pc2", "o_pc3",
             "o_S0", "o_S1", "o_S2", "o_S3", "o_Ss", "o_y0", "o_y1", "o_d2d"]

        ws = WStream()

        def build(P):
            ws.begin(P, [w[:] for w in wslots])
            MM, TR, ACT, TT, TS, STT, CP, RED, RECIP, MEMSET, DMA = (P.MM, P.TR, P.ACT, P.TT, P.TS, P.STT, P.CP, P.RED,
                                                                     P.RECIP, P.MEMSET, P.DMA)
            GB = [(0, 512), (512, 512), (1024, 512), (1536, 512), (2048, NS)]
            xT = [[T(xT_t[:, k, b0:b0 + bw]) for (b0, bw) in GB] for k in range(KD)]
            LB = [(0, 512), (512, 512), (1024, NS)]
            hT = [[T(hT_t[:, k, b0:b0 + bw]) for (b0, bw) in LB] for k in range(KD)]
            ar = arena
            upad = T(ar[:, 0:1070])
            ca = [T(ar[:, A_CA + j * 1040:A_CA + (j + 1) * 1040]) for j in range(4)]
            pad = T(ar[:, 0:1043])
            qkv = [T(ar[:, A_QKV + c * 1040:A_QKV + (c + 1) * 1040]) for c in range(3)]
            zs = T(ar[:, A_ZS:A_ZS + 1040])
            merged_v = ar[:, 0:4160].bitcast(BF16).rearrange("p (k n) -> p k n", k=8)
            merged = [T(merged_v[:, c, :]) for c in range(8)]
            caT_v = ar[:, A_CAT:A_CAT + 2080].bitcast(BF16).rearrange("p (k n) -> p k n", k=4)
            caT = [T(caT_v[:, j, :]) for j in range(4)]
            ogT_v = ar[:, A_OGT:A_OGT + 2080].bitcast(BF16).rearrange("p (k n) -> p k n", k=4)
            ogT = [T(ogT_v[:, h, :]) for h in range(4)]
            act_v = ar[:, 0:11 * 520].bitcast(BF16).rearrange("p (k n) -> p k n", k=11)
            actT = [T(act_v[:, j, :]) for j in range(11)]
            cs_tok = T(ar[:120, A_OGT:A_OGT + 2048].rearrange("p (r c) -> p r c", r=4))
            PH1 = [upad] + ca + caT + [cs_tok]
            PH2 = [pad] + qkv + [zs] + ogT
            PH3 = merged
            PH4 = actT

            scr = Ring([T(t[:]) for t in scr_t])
            st1, st2, st3 = [T(t[:]) for t in st_t]
            cpo = T(st_t[0][:30, :], st1.bufs)
            cso = T(st_t[2][:NS, :], st3.bufs)
            cstT = T(st_t[1][:, 0:NS * 31].rearrange("p (s j) -> p s j", j=31), st2.bufs)
            sm_base = [T(t[:]) for t in sm_t]
            sm = Ring(sm_base)
            sm_extra = [T(st_t[i][:, k_ * 128:(k_ + 1) * 128]) for i in range(3) for k_ in range(4)]
            smx = Ring(sm_base + sm_extra)
            lgx_t = {n: list(v) + [Ss_t[:, j * 128:(j + 1) * 128]] for j, (n, v) in enumerate(lg_t.items())}
            lg = {n: Ring([T(t[:]) for t in v]) for n, v in lgx_t.items()}
            lgb = {n: Ring([T(t[:, 0:64].bitcast(BF16)) for t in v]) for n, v in lgx_t.items()}
            lgb["Xb"] = Ring([T(t[:, 64:128].bitcast(BF16)) for t in lgx_t["qkT"]])
            lgb["Sb"] = Ring([T(t[:, 64:128].bitcast(BF16)) for t in lgx_t["kdc"]])
            qkb = [T(qkb_t[:, i, :]) for i in range(2)]
            S = [T(t[:]) for t in S_t]
            Ss = T(Ss_t[:].rearrange("p (s v) -> p s v", v=128))
            ld = [T(Ss_t[:, i * 1024:(i + 1) * 1024], Ss.bufs) for i in range(2)]
            ss_tok = T(Ss_t[:48, 0:1536], Ss.bufs)
            big = Ring([T(ps_t[i][:], Buf("pb%d" % i, True)) for i in range(4)])
            sbank = [Buf("psb%d" % i, True) for i in range(4)]
            small = Ring([T(ps_t[4 + i % 4][:, (i // 4) * 128:(i // 4 + 1) * 128], sbank[i % 4]) for i in range(16)])
            colsv = [T(cols_t[:, i, :]) for i in range(12)]
            beta_c, g_c, gc_c, ngc_c, egc_c, bgc_c, nbeta_c, kdec_c, egl_c, tmp_c, tmp2_c, colr = colsv
            scol = Ring([T(scol_t[:, i, :]) for i in range(12)])
            modTs = [T(t[:]) for t in modT_t2]
            gss = [T(t[:]) for t in gs_t2]
            pvn = T(pvn_t[:])
            cur = {}
            cT = T(cT_t[:])
            pvl = T(pv_t[:])
            negA = T(negA_t[:])
            cst = T(cst_t[:, 0:5, :])
            ident = cst[:, 0, :]
            ones = cst[:, 1, :]
            tri = cst[:, 2, :]
            nm_strict = cst[:, 3, :]
            nm_inclT = cst[:, 4, :]
            eps6 = T(cst_t[:, 5, 0:1])
            eps5 = T(cst_t[:, 5, 1:2])
            onec = T(cst_t[:, 5, 2:3])
            zeroc = T(cst_t[:, 5, 3:4])
            P.zeroc = zeroc
            upre = T(upre_t[:])
            qpre = T(qpre_t[:])
            sstT = T(sstT_t[:])
            pcs = [T(t[:]) for t in pc_t]
            pc_i = [0]

            def piece_out(dram_ap, psrc, rows):
                i = pc_i[0] % 4
                pc_i[0] += 1
                CP(pcs[i][:rows, :], psrc)
                DMA("sp", dram_ap, pcs[i][:rows, :], sem="o_pc%d" % i)

            DMA("sp", cst, consts[:, :, :], sem="ldc")
            bm = T(bm_t[:])
            DMA("pool", bm, bmask[:, :, :], sem="ldbm")
            ones_b = bm[:, 2, :]

            def sqb_tile():
                t_ = scr.next()
                return T(t_.ap[:, 0:256].bitcast(BF16), t_.bufs)
            MEMSET(eps6, 1e-6)
            MEMSET(eps5, 1e-5)
            MEMSET(onec, 1.0)
            MEMSET(zeroc, 0.0)

            for t in range(SEQ // 128):
                l_ = ld[t % 2]
                DMA("sp", l_, x_p[t * 128:(t + 1) * 128, :], sem="ld%d" % (t % 2))
                gb = t // 4
                for k in range(KD):
                    p = small.next()
                    TR(p, l_[:, k * 128:(k + 1) * 128], ident)
                    dst = T(xT_t[:, k, t * 128:(t + 1) * 128], xT[k][gb].bufs)
                    if k % 2 == 0:
                        ACT(dst, p, AF.Copy)
                    else:
                        CP(dst, p)
            l_ = ld[0]
            DMA("sp", l_[:NS, :], x_s[:, :], sem="ld0")
            for k in range(KD):
                p = small.next()
                TR(p[:, :NS], l_[:NS, k * 128:(k + 1) * 128], ident[:NS, :NS])
                CP(xT[k][4], p[:, :NS])
            l_ = ld[1]
            DMA("sp", l_[:NS + 1, :], c_in[:, :], sem="ld1")
            ACT(l_[:NS + 1, :], l_[:NS + 1, :], AF.Silu)
            for k in range(KD):
                p = small.next()
                TR(p[:, :NS + 1], l_[:NS + 1, k * 128:(k + 1) * 128], ident[:NS + 1, :NS + 1])
                CP(cT[:, k, :], p[:, :NS + 1])

            ckpt('init')
            def rmsnorm_mod(blocks, gbs, gsoff, shoff):
                for b, (b0, bw) in enumerate(blocks):
                    gb = gbs[b]
                    ps = big.next()
                    for k in range(KD):
                        sq = sqb_tile()
                        ACT(sq[:, :bw], xT[k][gb], AF.Square)
                        MM(ps[:, :bw], ones_b, sq[:, :bw], start=(k == 0), stop=(k == KD - 1))
                    rs = st1
                    ACT(rs[:, :bw], ps[:, :bw], AF.Ln, scale=1.0 / D, bias=eps6)
                    ACT(rs[:, :bw], rs[:, :bw], AF.Exp, scale=-0.5)
                    for k in range(KD):
                        tmp = scr.next()
                        TT(tmp[:, :bw], xT[k][gb], rs[:, :bw], ALU.mult)
                        if bw == NS:
                            TT(tmp[:, :bw], tmp[:, :bw], gs[:, gsoff + k, 1:NS + 1], ALU.mult)
                            TT(hT[k][b], tmp[:, :bw], modT[:, shoff + k, 1:NS + 1], ALU.add)
                        else:
                            ACT(hT[k][b], tmp[:, :bw], AF.Identity, scale=gs[:, gsoff + k, 0:1], bias=modT[:, shoff + k, 0:1])

            def resid_update(c, ps, bw, gb, gtoff):
                if bw == NS:
                    tmp = scr.next()
                    TT(tmp[:, :bw], ps[:, :bw], modT[:, gtoff + c, 1:NS + 1], ALU.mult)
                    TT(xT[c][gb], xT[c][gb], tmp[:, :bw], ALU.add)
                else:
                    STT(xT[c][gb], ps[:, :bw], modT[:, gtoff + c, 0:1], xT[c][gb], ALU.mult, ALU.add)

            def wv(ap2d):
                return ap2d.rearrange("(k p) n -> p k n", p=128)

            def mod_piece(ll, n2):
                mT = modTs[ll % 2]
                w = ws.get(wv(w_ada[ll, :, n2 * 256:(n2 + 1) * 256]), KD, 256)
                for half in range(2):
                    n = n2 * 2 + half
                    p = small.next()
                    for k in range(KD):
                        MM(p[:, :NS + 1], w[:, k, half * 128:(half + 1) * 128], cT[:, k, :], start=(k == 0), stop=(k == KD - 1))
                    ACT(mT[:, n, :], p[:, :NS + 1], AF.Identity, bias=pvn[:, PV_BADA + n:PV_BADA + n + 1])

            def mod_finish(ll):
                mT, g_ = modTs[ll % 2], gss[ll % 2]
                for k in range(KD):
                    TS(g_[:, k, :], mT[:, 8 + k, :], 1.0, ALU.add, pvn[:, PV_G1 + k:PV_G1 + k + 1], ALU.mult)
                    TS(g_[:, 8 + k, :], mT[:, 32 + k, :], 1.0, ALU.add, pvn[:, PV_G2 + k:PV_G2 + k + 1], ALU.mult)

            for l in range(n_layers):
                DMA("sp", pvl, pvec[l], sem="ldp")
                modT, gs = modTs[l % 2], gss[l % 2]
                cur["modT"], cur["gs"] = modT, gs
                if l == 0:
                    DMA("sp", pvn, pvec[0, :, 0:64], sem="ldpn")
                    for n2 in range(24):
                        mod_piece(0, n2)
                    mod_finish(0)
                ACT(negA, pvl[:, PV_ALOG:PV_ALOG + 32], AF.Exp)
                ACT(negA, negA, AF.Identity, scale=-1.0)
                ckpt('mod')
                DMA("sp", o_conf_s[l, :, 0:29, :], st_conf[l, :, 1:30, :], sem="o_d2d")
                DMA("sp", o_short_s[l, :, 0:2, :], st_short[l, :, 1:3, :], sem="o_d2d")
                DMA("sp", ss_tok, st_short[l].rearrange("s j c -> (s j) c"), sem="ldss")
                for gch in range(12):
                    p = small.next()
                    TR(p[:, :48], ss_tok[:48, gch * 128:(gch + 1) * 128], ident[:48, :48])
                    CP(sstT[:, gch, :], p[:, :48])

                ckpt('lstart')
                for pi in range(2):
                    blocks = LB[:2] if pi == 0 else LB
                    gbs = [0, 1] if pi == 0 else [2, 3, 4]
                    has_s = pi == 1
                    W = 1024 + (NS if has_s else 0)
                    NCH = 8

                    rmsnorm_mod(blocks, gbs, 0, 0)

                    ckpt('norm1')
                    alias_barrier(PH2 + PH3 + PH4, PH1)
                    if has_s:
                        DMA("sp", cs_tok, st_conf[l].rearrange("(r s) j c -> (s j) r c", r=4), sem="ldcs")
                    else:
                        MEMSET(upad[:, 0:30], 0.0)
                    for j in range(4):
                        if has_s:
                            CP(upad[:, 0:30], upre[:, j, :])
                        wa = ws.get(wv(w_in[l, :, j * 128:(j + 1) * 128]), KD, 128)
                        wb = ws.get(wv(w_in[l, :, O_GLU_B + j * 128:O_GLU_B + (j + 1) * 128]), KD, 128)
                        for b, (b0, bw) in enumerate(blocks):
                            pa = big.next()
                            pb = big.next()
                            for k in range(KD):
                                MM(pa[:, :bw], wa[:, k, :], hT[k][b], start=(k == 0), stop=(k == KD - 1))
                            for k in range(KD):
                                MM(pb[:, :bw], wb[:, k, :], hT[k][b], start=(k == 0), stop=(k == KD - 1))
                            sg = scr.next()
                            ACT(sg[:, :bw], pb[:, :bw], AF.Sigmoid)
                            TT(upad[:, 30 + b0:30 + b0 + bw], pa[:, :bw], sg[:, :bw], ALU.mult)
                        cw0 = PV_CW + j * 31
                        caj = ca[j][:, 0:1024]
                        TS(caj, upad[:, 0:1024], pvl[:, cw0:cw0 + 1], ALU.mult, pvl[:, PV_CB + j:PV_CB + j + 1], ALU.add)
                        for i in range(1, CONF_KW):
                            STT(caj, upad[:, i:i + 1024], pvl[:, cw0 + i:cw0 + i + 1], caj, ALU.mult, ALU.add)
                        if not has_s:
                            ACT(upre[:, j, :], upad[:, 1024:1054], AF.Copy)
                        else:
                            p = small.next()
                            TR(p[:30, :], upad[:, 1024:1054], ident)
                            CP(cpo[:30, j * 128:(j + 1) * 128], p[:30, :])
                            us_ = upad[:, 30 + 1024:30 + 1024 + NS]
                            for rt in range(4):
                                p = small.next()
                                TR(p[:, :120], cs_tok[:120, rt, j * 128:(j + 1) * 128], ident[:120, :120])
                                CP(cstT[:, 4 * rt:4 * rt + 4, 0:30], p[:, :120].v(lambda a: a.rearrange("p (s j) -> p s j", j=30)))
                            CP(cstT[:, :, 30], us_)
                            prod = scr.next()[:, 0:NS * 31].v(lambda a: a.rearrange("p (s j) -> p s j", j=31))
                            TT(prod, cstT, pvl[:, cw0:cw0 + 31].v(lambda a: a.unsqueeze(1).to_broadcast([128, NS, 31])), ALU.mult)
                            RED(ca[j][:, 1024:1024 + NS], prod)
                            TS(ca[j][:, 1024:1024 + NS], ca[j][:, 1024:1024 + NS], pvl[:, PV_CB + j:PV_CB + j + 1], ALU.add)
                            p = small.next()
                            TR(p[:NS, :], us_, ident)
                            CP(cso[:NS, j * 128:(j + 1) * 128], p[:NS, :])
                    if has_s:
                        DMA("sp", o_conf_p[l], cpo, sem="o_cpo")
                        DMA("sp", o_conf_s[l, :, 29, :], cso, sem="o_cso")
                    ckpt('conf')
                    for b, (b0, bw) in enumerate(blocks):
                        p1 = big.next()
                        p2 = big.next()
                        for j in range(4):
                            cb_ = sqb_tile()
                            ACT(cb_[:, :bw], ca[j][:, b0:b0 + bw], AF.Copy)
                            MM(p1[:, :bw], ones_b, cb_[:, :bw], start=(j == 0), stop=(j == 3))
                        for j in range(4):
                            sq = sqb_tile()
                            ACT(sq[:, :bw], ca[j][:, b0:b0 + bw], AF.Square)
                            MM(p2[:, :bw], ones_b, sq[:, :bw], start=(j == 0), stop=(j == 3))
                        mean, msq, var = st1, st2, st3
                        ACT(mean[:, :bw], p1[:, :bw], AF.Identity, scale=1.0 / CONF_CH)
                        TT(msq[:, :bw], mean[:, :bw], mean[:, :bw], ALU.mult)
                        STT(var[:, :bw], p2[:, :bw], 1.0 / CONF_CH, msq[:, :bw], ALU.mult, ALU.subtract)
                        ACT(var[:, :bw], var[:, :bw], AF.Ln, bias=eps5)
                        ACT(var[:, :bw], var[:, :bw], AF.Exp, scale=-0.5)
                        for j in range(4):
                            t = scr.next()
                            TT(t[:, :bw], ca[j][:, b0:b0 + bw], mean[:, :bw], ALU.subtract)
                            TT(t[:, :bw], t[:, :bw], var[:, :bw], ALU.mult)
                            ACT(caT[j][:, b0:b0 + bw], t[:, :bw], AF.Silu, scale=pvl[:, PV_LNG + j:PV_LNG + j + 1],
                                bias=pvl[:, PV_LNB + j:PV_LNB + j + 1])

                    ckpt('confln')
                    alias_barrier(PH1, PH2)
                    wbg = ws.get(wv(w_in[l, :, O_BETA:O_BETA + 8]), KD, 8)
                    pbg = small.next()
                    for c in range(NCH):
                        for k in range(KD):
                            MM(pbg[:, c * 8:(c + 1) * 8], hT[k][c // 4][:, (c % 4) * 128:(c % 4 + 1) * 128], wbg[:, k, :],
                               start=(k == 0), stop=(k == KD - 1))
                    pbg3 = pbg[:, 0:64].v(lambda a: a.rearrange("p (c e) -> p c e", e=8))

                    def v3(t):
                        return t.v(lambda a: a.rearrange("p (c e) -> p c e", e=4))
                    ACT(v3(beta_c), pbg3[:, :, 0:4], AF.Sigmoid)
                    TT(v3(tmp_c), pbg3[:, :, 4:8], v3(pvl[:, PV_DTB:PV_DTB + 32]), ALU.add)
                    ACT(tmp_c, tmp_c, AF.Exp)
                    ACT(tmp_c, tmp_c, AF.Ln, bias=onec)
                    TT(g_c, tmp_c, negA, ALU.mult)
                    pgc = small.next()
                    pgl = small.next()
                    for c in range(NCH):
                        MM(pgc[:, c * 4:(c + 1) * 4], tri, g_c[:, c * 4:(c + 1) * 4])
                    for c in range(NCH):
                        MM(pgl[:, c * 4:(c + 1) * 4], ones, g_c[:, c * 4:(c + 1) * 4])
                    CP(gc_c, pgc[:, 0:32])
                    ACT(ngc_c, pgc[:, 0:32], AF.Identity, scale=-1.0)
                    ACT(egc_c, pgc[:, 0:32], AF.Exp)
                    TT(bgc_c, beta_c, egc_c, ALU.mult)
                    ACT(nbeta_c, beta_c, AF.Identity, scale=-1.0)
                    TT(tmp2_c, pgl[:, 0:32], gc_c, ALU.subtract)
                    ACT(kdec_c, tmp2_c, AF.Exp)
                    ACT(egl_c, pgl[:, 0:32], AF.Exp)
                    if has_s:
                        pbs = small.next()
                        for k in range(KD):
                            MM(pbs[:NS, 0:8], hT[k][2], wbg[:, k, :], start=(k == 0), stop=(k == KD - 1))
                        beta_s = T(cols_t[:NS, 11, 0:4], colr.bufs)
                        eg_s = T(cols_t[:NS, 11, 4:8], colr.bufs)
                        tmp_s = T(cols_t[:NS, 11, 8:12], colr.bufs)
                        ACT(beta_s, pbs[:NS, 0:4], AF.Sigmoid)
                        TT(tmp_s, pbs[:NS, 4:8], pvl[:NS, PV_DTB:PV_DTB + 4], ALU.add)
                        ACT(tmp_s, tmp_s, AF.Exp)
                        ACT(tmp_s, tmp_s, AF.Ln, bias=onec[:NS, :])
                        TT(tmp_s, tmp_s, negA[:NS, 0:4], ALU.mult)
                        ACT(eg_s, tmp_s, AF.Exp)
                    else:
                        MEMSET(pad[:, 0:3], 0.0)

                    ckpt('dprol')
                    for h in range(4):
                        wq = [ws.get(wv(w_in[l, :, off + h * 128:off + (h + 1) * 128]), KD, 128) for off in (O_Q, O_K, O_V, O_Z)]
                        for ci in range(3):
                            gch = ci * 4 + h
                            if has_s:
                                CP(pad[:, 0:3], qpre[:, gch, :])
                            for b, (b0, bw) in enumerate(blocks):
                                ps = big.next()
                                for k in range(KD):
                                    MM(ps[:, :bw], wq[ci][:, k, :], hT[k][b], start=(k == 0), stop=(k == KD - 1))
                                ACT(pad[:, 3 + b0:3 + b0 + bw], ps[:, :bw], AF.Copy)
                            sw0 = PV_SW + gch * 4
                            qc = qkv[ci][:, 0:1024]
                            TS(qc, pad[:, 0:1024], pvl[:, sw0:sw0 + 1], ALU.mult)
                            for i in range(1, 4):
                                STT(qc, pad[:, i:i + 1024], pvl[:, sw0 + i:sw0 + i + 1], qc, ALU.mult, ALU.add)
                            if not has_s:
                                ACT(qpre[:, gch, :], pad[:, 1024:1027], AF.Copy)
                            else:
                                p = small.next()
                                TR(p[:3, :], pad[:, 1024:1027], ident)
                                piece_out(o_short_p[l, :, gch * 128:(gch + 1) * 128], p[:3, :], 3)
                                xs_ = pad[:, 1027:1027 + NS]
                                qs_ = qkv[ci][:, 1024:1024 + NS]
                                st3v = sstT[:, gch, :].v(lambda a: a.rearrange("p (s j) -> p s j", j=3))
                                TS(qs_, xs_, pvl[:, sw0 + 3:sw0 + 4], ALU.mult)
                                for i in range(3):
                                    STT(qs_, st3v[:, :, i], pvl[:, sw0 + i:sw0 + i + 1], qs_, ALU.mult, ALU.add)
                                p = small.next()
                                TR(p[:NS, :], xs_, ident)
                                piece_out(o_short_s[l, :, 2, gch * 128:(gch + 1) * 128], p[:NS, :], NS)
                            ACT(qkv[ci][:, 0:W], qkv[ci][:, 0:W], AF.Silu)
                        for b, (b0, bw) in enumerate(blocks):
                            ps = big.next()
                            for k in range(KD):
                                MM(ps[:, :bw], wq[3][:, k, :], hT[k][b], start=(k == 0), stop=(k == KD - 1))
                            ACT(zs[:, b0:b0 + bw], ps[:, :bw], AF.Silu)
                        for ci in range(2):
                            for b, (b0, bw) in enumerate(blocks):
                                sq = sqb_tile()
                                ACT(sq[:, :bw], qkv[ci][:, b0:b0 + bw], AF.Square)
                                ps = big.next()
                                MM(ps[:, :bw], ones_b, sq[:, :bw])
                                rn = scr.next()
                                ACT(rn[:, :bw], ps[:, :bw], AF.Ln, bias=eps6)
                                ACT(rn[:, :bw], rn[:, :bw], AF.Exp, scale=-0.5)
                                if ci == 0:
                                    STT(qkv[ci][:, b0:b0 + bw], qkv[ci][:, b0:b0 + bw], 128.0 ** -0.5, rn[:, :bw], ALU.mult, ALU.mult)
                                else:
                                    TT(qkv[ci][:, b0:b0 + bw], qkv[ci][:, b0:b0 + bw], rn[:, :bw], ALU.mult)
                                if bw != NS:
                                    CP(qkb[ci][:, b0:b0 + bw], qkv[ci][:, b0:b0 + bw], eng="pool")
                        if not has_s:
                            MEMSET(S[h], 0.0)

                        ckpt('dhead')
                        def chunk_gen(c):
                            cs = slice(c * 128, (c + 1) * 128)
                            ix = c * 4 + h
                            col = slice(ix, ix + 1)
                            qTc, kTc, vTc = qkv[0][:, cs], qkv[1][:, cs], qkv[2][:, cs]
                            qbc, kbc = qkb[0][:, cs], qkb[1][:, cs]
                            grep = smx.next()
                            TS(grep, ones, g_c[:, col], ALU.mult)
                            pg = small.next()
                            MM(pg, grep, tri)
                            yield
                            pre1 = smx.next()
                            STT(pre1, pg, -1.0, nm_strict, ALU.mult, ALU.add)
                            pre2 = smx.next()
                            TT(pre2, pg, nm_inclT, ALU.add)
                            egr = smx.next()
                            ACT(egr, pg, AF.Exp)
                            yield
                            Ds = smx.next()
                            ACT(Ds, pre1, AF.Exp, bias=gc_c[:, col])
                            DT = smx.next()
                            ACT(DT, pre2, AF.Exp, bias=ngc_c[:, col])
                            qgc = lgb["qg"].next()
                            TT(qgc, qTc, egr, ALU.mult)
                            yield
                            pk = small.next()
                            MM(pk, kbc, kbc)
                            Nm = smx.next()
                            STT(Nm, pk, nbeta_c[:, col], Ds, ALU.mult, ALU.mult)
                            yield
                            pq = small.next()
                            MM(pq, kbc, qbc)
                            qkT = lgb["qkT"].next()
                            TT(qkT, pq, DT, ALU.mult)
                            yield
                            Nbd = smx.next()
                            TT(Nbd, Nm, bm[:, 0, :], ALU.mult)
                            No = lg["No"].next()
                            TT(No, Nm, bm[:, 1, :], ALU.mult, eng="pool")
                            pt = small.next()
                            TR(pt, Nbd, ident)
                            Mm = smx.next()
                            ACT(Mm, pt, AF.Copy)
                            X = smx.next()
                            TT(X, Mm, ident, ALU.add)
                            yield
                            ptk = small.next()
                            TR(ptk, kTc, ident)
                            kbg = lgb["kbg"].next()
                            ACT(kbg, ptk, AF.Identity, scale=bgc_c[:, col])
                            kdc = lgb["kdc"].next()
                            ACT(kdc, ptk, AF.Identity, scale=kdec_c[:, col])
                            yield
                            ptv = small.next()
                            TR(ptv, vTc, ident)
                            vb = lgb["vb"].next()
                            ACT(vb, ptv, AF.Identity, scale=beta_c[:, col])
                            Pm, Pn = Mm, Nbd
                            for lev in range(1, 6):
                                yield
                                pp2 = small.next()
                                MM(pp2, Pm, Pn)
                                Pn2 = smx.next()
                                ACT(Pn2, pp2, AF.Copy)
                                if lev < 5:
                                    pp1 = small.next()
                                    MM(pp1, Pn, Pm)
                                    Pm2 = smx.next()
                                    CP(Pm2, pp1)
                                yield
                                px = small.next()
                                MM(px, Pn2, X)
                                X2 = smx.next()
                                TT(X2, px, X, ALU.add)
                                X, Pn = X2, Pn2
                                if lev < 5:
                                    Pm = Pm2
                            yield
                            pxt = small.next()
                            TR(pxt, X, ident)
                            Yt = smx.next()
                            ACT(Yt, pxt, AF.Copy)
                            pt1 = small.next()
                            MM(pt1, No, X)
                            T1 = smx.next()
                            CP(T1, pt1)
                            yield
                            pxx = small.next()
                            MM(pxx, Yt, T1)
                            X2 = smx.next()
                            TT(X2, pxx, X, ALU.add)
                            X = X2
                            yield
                            Xb = lgb["Xb"].next()
                            CP(Xb, X, eng="pool")
                            yield
                            pu = small.next()
                            MM(pu, Xb, vb)
                            u = lg["u"].next()
                            ACT(u, pu, AF.Copy)
                            yield
                            pw = small.next()
                            MM(pw, kbg, Xb)
                            wT = lgb["wT"].next()
                            CP(wT, pw)
                            yield
                            Sb = lgb["Sb"].next()
                            CP(Sb, S[h], eng="pool")
                            pvn = small.next()
                            MM(pvn, wT, Sb)
                            vn = lgb["vn"].next()
                            TT(vn, u, pvn, ALU.subtract)
                            yield
                            po = small.next()
                            MM(po, qgc, Sb, start=True, stop=False)
                            MM(po, qkT, vn, start=False, stop=True)
                            yield
                            pS = small.next()
                            MM(pS, kdc, vn)
                            STT(S[h], S[h], egl_c[:, col], pS, ALU.mult, ALU.add)
                            yield
                            osq = smx.next()
                            ssq = scol.next()
                            ACT(osq, po, AF.Square, accum=ssq[:, 0:1])
                            ACT(ssq[:, 0:1], ssq[:, 0:1], AF.Ln, scale=1.0 / 128, bias=eps6)
                            ACT(ssq[:, 0:1], ssq[:, 0:1], AF.Exp, scale=-0.5)
                            yield
                            on = smx.next()
                            ACT(on, po, AF.Identity, scale=ssq[:, 0:1])
                            yield
                            pT = small.next()
                            TR(pT, on, ident)
                            STT(ogT[h][:, cs], pT, pvl[:, PV_DNG:PV_DNG + 1], zs[:, cs], ALU.mult, ALU.mult)


                        xtiles = sm_extra + [lg[n_].tiles[2] for n_ in lg] + [lgb[n_].tiles[2] for n_ in lgb]
                        alias_barrier([st1, st2, st3, Ss], xtiles)
                        pending = list(range(NCH))
                        active = []
                        rnd = 0
                        while pending or active:
                            if pending and len(active) < GCH and (rnd % STAG == 0 or not active):
                                active.append(chunk_gen(pending.pop(0)))
                            for g_ in list(active):
                                try:
                                    next(g_)
                                except StopIteration:
                                    active.remove(g_)
                            rnd += 1
                        alias_barrier(xtiles, [st1, st2, st3, Ss])
                        ckpt('dchunks')
                        if pi == 0 and l + 1 < n_layers:
                            if h == 0:
                                DMA("sp", pvn, pvec[l + 1, :, 0:64], sem="ldpn")
                            for n2 in range(h * 6, h * 6 + 6):
                                mod_piece(l + 1, n2)
                            if h == 3:
                                mod_finish(l + 1)
                        if has_s:
                            DMA("sp", o_delta_p[l, h], S[h], sem="o_S%d" % h)
                            sc = slice(1024, 1024 + NS)
                            qs, ks, vs = qkv[0][:, sc], qkv[1][:, sc], qkv[2][:, sc]
                            DMA("sp", Ss, st_delta[l, :, h].rearrange("s p v -> p s v"), sem="ldSs")

                            def bcast16(colap):
                                dg = sm.next()
                                TS(dg[:NS, :NS], ident[:NS, :NS], colap, ALU.mult)
                                pB = small.next()
                                MM(pB[:, :NS], ones[:NS, :], dg[:NS, :NS])
                                o_ = scol.next()
                                CP(o_, pB[:, :NS])
                                return o_
                            bbc = bcast16(beta_s[:, h:h + 1])
                            ebc = bcast16(eg_s[:, h:h + 1])
                            t0 = scol.next()
                            TT(t0, qs, ks, ALU.mult)
                            pqk = small.next()
                            MM(pqk[:, :NS], ones, t0)
                            qkbc = scol.next()
                            CP(qkbc, pqk[:, :NS])
                            pkS = small.next()
                            pqS = small.next()
                            for s in range(NS):
                                MM(pkS[:, s:s + 1], Ss[:, s, :], ks[:, s:s + 1])
                            for s in range(NS):
                                MM(pqS[:, s:s + 1], Ss[:, s, :], qs[:, s:s + 1])
                            t1 = scol.next()
                            TT(t1, pkS[:, :NS], ebc, ALU.mult)
                            TT(t1, vs, t1, ALU.subtract)
                            vnT = scol.next()
                            TT(vnT, t1, bbc, ALU.mult)
                            oT = scol.next()
                            TT(oT, pqS[:, :NS], ebc, ALU.mult)
                            t2 = scol.next()
                            TT(t2, qkbc, vnT, ALU.mult)
                            TT(oT, oT, t2, ALU.add)
                            t3 = scol.next()
                            TT(t3, oT, oT, ALU.mult)
                            pss = small.next()
                            MM(pss[:, :NS], ones, t3)
                            rs_ = scol.next()
                            ACT(rs_, pss[:, :NS], AF.Ln, scale=1.0 / 128, bias=eps6)
                            ACT(rs_, rs_, AF.Exp, scale=-0.5)
                            TT(oT, oT, rs_, ALU.mult)
                            STT(ogT[h][:, sc], oT, pvl[:, PV_DNG:PV_DNG + 1], zs[:, sc], ALU.mult, ALU.mult)
                            pkt = small.next()
                            TR(pkt[:NS, :], ks, ident)
                            k_tok = scr.next()[:, 0:128]
                            CP(k_tok[:NS, :], pkt[:NS, :])
                            pvt = small.next()
                            TR(pvt[:NS, :], vnT, ident)
                            vn_tok = scr.next()[:, 0:128]
                            CP(vn_tok[:NS, :], pvt[:NS, :])
                            for s in range(NS):
                                rhs_s = sm.next()
                                TS(rhs_s[:NS, :], vn_tok[:NS, :], ident[:NS, s:s + 1], ALU.mult)
                                po_ = small.next()
                                MM(po_, k_tok[:NS, :], rhs_s[:NS, :])
                                STT(Ss[:, s, :], Ss[:, s, :], ebc[:, s:s + 1], po_, ALU.mult, ALU.add)
                            DMA("sp", o_delta_s[l, :, h].rearrange("s p v -> p s v"), Ss, sem="o_Ss")

                    ckpt('delta')
                    alias_barrier(PH2, PH3)
                    for c2 in range(4):
                        wco = ws.get(wv(w_conf_out[l, :, c2 * 256:(c2 + 1) * 256]), 4, 256)
                        wdo = ws.get(wv(w_delta_out[l, :, c2 * 256:(c2 + 1) * 256]), 4, 256)
                        wma = ws.get(wv(w_in[l, :, O_MA + c2 * 256:O_MA + (c2 + 1) * 256]), KD, 256)
                        wmb = ws.get(wv(w_in[l, :, O_MB + c2 * 256:O_MB + (c2 + 1) * 256]), KD, 256)
                        for c1 in range(2):
                            c = c2 * 2 + c1
                            w0 = c1 * 128
                            for b, (b0, bw) in enumerate(blocks):
                                pya = big.next()
                                pma = big.next()
                                for k in range(4):
                                    MM(pya[:, :bw], wco[:, k, w0:w0 + 128], caT[k][:, b0:b0 + bw], start=(k == 0), stop=(k == 3))
                                for k in range(KD):
                                    MM(pma[:, :bw], wma[:, k, w0:w0 + 128], hT[k][b], start=(k == 0), stop=(k == KD - 1))
                                sa = scr.next()
                                ACT(sa[:, :bw], pma[:, :bw], AF.Sigmoid)
                                TT(sa[:, :bw], pya[:, :bw], sa[:, :bw], ALU.mult)
                                pyb = big.next()
                                pmb = big.next()
                                for k in range(4):
                                    MM(pyb[:, :bw], wdo[:, k, w0:w0 + 128], ogT[k][:, b0:b0 + bw], start=(k == 0), stop=(k == 3))
                                for k in range(KD):
                                    MM(pmb[:, :bw], wmb[:, k, w0:w0 + 128], hT[k][b], start=(k == 0), stop=(k == KD - 1))
                                sb_ = scr.next()
                                ACT(sb_[:, :bw], pmb[:, :bw], AF.Sigmoid)
                                TT(sb_[:, :bw], pyb[:, :bw], sb_[:, :bw], ALU.mult)
                                TT(merged[c][:, b0:b0 + bw], sa[:, :bw], sb_[:, :bw], ALU.add)
                    ckpt('merge')
                    for c2 in range(4):
                        wmo = ws.get(wv(w_merge_out[l, :, c2 * 256:(c2 + 1) * 256]), KD, 256)
                        for c1 in range(2):
                            c = c2 * 2 + c1
                            for b, (b0, bw) in enumerate(blocks):
                                ps = big.next()
                                for k in range(KD):
                                    MM(ps[:, :bw], wmo[:, k, c1 * 128:(c1 + 1) * 128], merged[k][:, b0:b0 + bw], start=(k == 0), stop=(k == KD - 1))
                                resid_update(c, ps, bw, gbs[b], 16)

                    ckpt('mergeout')
                    rmsnorm_mod(blocks, gbs, 8, 24)
                    for fh in range(2):
                        alias_barrier(PH1 + PH2 + PH3 + PH4, PH4)
                        for jj2 in range(6):
                            j0 = fh * 11 + jj2 * 2
                            nj = min(2, fh * 11 + 11 - j0)
                            wg = ws.get(wv(w_ffn_in[l, :, j0 * 128:(j0 + nj) * 128]), KD, nj * 128)
                            wu = ws.get(wv(w_ffn_in[l, :, D_FF + j0 * 128:D_FF + (j0 + nj) * 128]), KD, nj * 128)
                            for j1 in range(nj):
                                jl = jj2 * 2 + j1
                                for b, (b0, bw) in enumerate(blocks):
                                    pgt = big.next()
                                    pup = big.next()
                                    for k in range(KD):
                                        MM(pgt[:, :bw], wg[:, k, j1 * 128:(j1 + 1) * 128], hT[k][b], start=(k == 0), stop=(k == KD - 1))
                                    for k in range(KD):
                                        MM(pup[:, :bw], wu[:, k, j1 * 128:(j1 + 1) * 128], hT[k][b], start=(k == 0), stop=(k == KD - 1))
                                    sg = scr.next()
                                    ACT(sg[:, :bw], pgt[:, :bw], AF.Silu)
                                    TT(actT[jl][:, b0:b0 + bw], sg[:, :bw], pup[:, :bw], ALU.mult)
                        for c in range(KD):
                            wo = ws.get(wv(w_ffn_out[l, fh * 1408:(fh + 1) * 1408, c * 128:(c + 1) * 128]), 11, 128)
                            for b, (b0, bw) in enumerate(blocks):
                                ps = big.next()
                                for j in range(11):
                                    MM(ps[:, :bw], wo[:, j, :], actT[j][:, b0:b0 + bw], start=(j == 0), stop=(j == 10))
                                resid_update(c, ps, bw, gbs[b], 40)

            for gb, (g0, gw) in enumerate(GB):
                ps = big.next()
                for k in range(KD):
                    sq = sqb_tile()
                    ACT(sq[:, :gw], xT[k][gb], AF.Square)
                    MM(ps[:, :gw], ones_b, sq[:, :gw], start=(k == 0), stop=(k == KD - 1))
                rs = st1
                ACT(rs[:, :gw], ps[:, :gw], AF.Ln, scale=1.0 / D, bias=eps6)
                ACT(rs[:, :gw], rs[:, :gw], AF.Exp, scale=-0.5)
                nt = max(1, gw // 128)
                tw = min(gw, 128)
                for t in range(nt):
                    o_ = ld[t % 2]
                    for k in range(KD):
                        tmp = sm.next()
                        STT(tmp[:, :tw], xT[k][gb][:, t * 128:t * 128 + tw], pvl[:, PV_FG + k:PV_FG + k + 1], rs[:, t * 128:t * 128 + tw],
                            ALU.mult, ALU.mult)
                        p = small.next()
                        TR(p[:tw, :], tmp[:, :tw], ident)
                        if k % 2 == 0:
                            ACT(o_[:tw, k * 128:(k + 1) * 128], p[:tw, :], AF.Copy)
                        else:
                            CP(o_[:tw, k * 128:(k + 1) * 128], p[:tw, :])
                    if gb < 4:
                        r0 = g0 + t * 128
                        DMA("sp", y_p[r0:r0 + 128, :], o_, sem="o_y%d" % (t % 2))
                    else:
                        DMA("sp", y_s[:, :], o_[:NS, :], sem="o_y%d" % (t % 2))

        def build_w(Pq):
            try:
                build(Pq)
            except StopBuild:
                pass
            for sname in sem_names:
                if (sname.startswith("o_") or sname.startswith("w") or sname.startswith("ld")) and Pq.cnt.get(sname, 0) > 0:
                    Pq.q["sp"].append(("wait", sname, Pq.cnt[sname]))
        P0 = Prog()
        build_w(P0)
        ws.recording = False
        P = Prog()
        build_w(P)
        print("ops:", P.nops, {k: len(v) for k, v in P.q.items()}, "weights:", len(ws.specs))

        semh = {n: es.enter_context(nc.semaphore(n)) for n in sem_names}
        block = es.enter_context(nc.Block())

        def run(eng, items):
            for it in items:
                if it[0] == "wait":
                    eng.wait_ge(semh[it[1]], it[2])
                else:
                    it[1](eng).then_inc(semh[it[2]], it[3])

        @block.tensor
        def _(eng):
            run(eng, P.q["pe"])

        @block.scalar
        def _(eng):
            run(eng, P.q["act"])

        @block.vector
        def _(eng):
            run(eng, P.q["dve"])

        @block.gpsimd
        def _(eng):
            run(eng, P.q["pool"])

        @block.sync
        def _(eng):
            run(eng, P.q["sp"])
    return nc


def make_consts():
    c = np.zeros((128, 5, 128), np.float32)
    i = np.arange(128)
    c[:, 0, :] = np.eye(128, dtype=np.float32)
    c[:, 1, :] = 1.0
    c[:, 2, :] = (i[:, None] <= i[None, :]).astype(np.float32)
    c[:, 3, :] = np.where(i[:, None] > i[None, :], 0.0, NEG).astype(np.float32)
    c[:, 4, :] = np.where(i[None, :] >= i[:, None], 0.0, NEG).astype(np.float32)
    return c


def make_bmask():
    i = np.arange(128)
    b64 = (i[:, None] // 64 == i[None, :] // 64)
    m = np.zeros((128, 3, 128), np.float32)
    m[:, 0, :] = b64
    m[:, 1, :] = ~b64
    m[:, 2, :] = 1.0
    return m


def make_pvec(inp):
    pv = np.zeros((DEPTH, 128, PV_N), np.float32)
    for l in range(DEPTH):
        pv[l, :, PV_G1:PV_G1 + 8] = inp["norm1_g"][l].reshape(8, 128).T
        pv[l, :, PV_G2:PV_G2 + 8] = inp["norm2_g"][l].reshape(8, 128).T
        pv[l, :, PV_BADA:PV_BADA + 48] = inp["b_ada"][l].reshape(48, 128).T
        cw = inp["conf_dw_w"][l]
        for j in range(4):
            pv[l, :, PV_CW + j * 31:PV_CW + (j + 1) * 31] = cw[:, j * 128:(j + 1) * 128].T
        pv[l, :, PV_CB:PV_CB + 4] = inp["conf_dw_b"][l].reshape(4, 128).T
        pv[l, :, PV_LNG:PV_LNG + 4] = inp["conf_ln_g"][l].reshape(4, 128).T
        pv[l, :, PV_LNB:PV_LNB + 4] = inp["conf_ln_b"][l].reshape(4, 128).T
        sw = inp["short_conv_w"][l]
        for g in range(12):
            pv[l, :, PV_SW + g * 4:PV_SW + (g + 1) * 4] = sw[:, g * 128:(g + 1) * 128].T
        pv[l, :, PV_DNG] = inp["delta_norm_g"][l]
        pv[l, :, PV_ALOG:PV_ALOG + 32] = np.tile(inp["a_log"][l], 8)[None, :]
        pv[l, :, PV_DTB:PV_DTB + 32] = np.tile(inp["dt_bias"][l], 8)[None, :]
        pv[l, :, PV_FG:PV_FG + 8] = inp["final_norm_g"].reshape(8, 128).T
    return pv


_NC_CACHE = {}


def kernel(**inp):
    inp = {k: np.asarray(v) for k, v in inp.items()}
    n = 8
    if "nc" not in _NC_CACHE:
        _NC_CACHE["nc"] = build_nc()
    nc = _NC_CACHE["nc"]
    consts = make_consts()
    bmask_np = make_bmask()
    pv = make_pvec(inp)
    shared = {k: np.ascontiguousarray(inp[k], dtype=np.float32) for k in
              ("w_ada", "w_in", "w_conf_out", "w_delta_out", "w_merge_out", "w_ffn_in", "w_ffn_out")}
    in_maps = []
    for i in range(n):
        s0 = i * NS
        m = dict(shared)
        m["x_p"] = np.ascontiguousarray(inp["x_prompt"][i])
        m["x_s"] = np.ascontiguousarray(inp["x_sample"][s0:s0 + NS, 0, :])
        m["c_in"] = np.ascontiguousarray(np.concatenate([inp["c_prompt"][i:i + 1], inp["c_sample"][s0:s0 + NS]], axis=0))
        m["st_conf"] = np.ascontiguousarray(inp["state_conformer_conv"][:, s0:s0 + NS])
        m["st_short"] = np.ascontiguousarray(inp["state_short_conv"][:, s0:s0 + NS])
        m["st_delta"] = np.ascontiguousarray(inp["state_delta"][:, s0:s0 + NS])
        m["pvec"] = pv
        m["consts"] = consts
        m["bmask"] = bmask_np
        in_maps.append(m)
    res = run_bass_kernel_spmd(nc, in_maps, core_ids=list(range(n)))
    R = res.results
    y_p = np.stack([R[i]["y_p"] for i in range(n)], axis=0)
    y_s = np.concatenate([R[i]["y_s"] for i in range(n)], axis=0)[:, None, :]
    conf_p = np.stack([R[i]["o_conf_p"] for i in range(n)], axis=1)
    conf_s = np.concatenate([R[i]["o_conf_s"] for i in range(n)], axis=1)
    short_p = np.stack([R[i]["o_short_p"] for i in range(n)], axis=1)
    short_s = np.concatenate([R[i]["o_short_s"] for i in range(n)], axis=1)
    delta_p = np.stack([R[i]["o_delta_p"] for i in range(n)], axis=1)
    delta_s = np.concatenate([R[i]["o_delta_s"] for i in range(n)], axis=1)
    return tuple(np.ascontiguousarray(a, dtype=np.float32) for a in
                 (y_p, y_s, conf_p, conf_s, short_p, short_s, delta_p, delta_s))
```

```python
import numpy as np
from contextlib import ExitStack
import concourse.bass as bass
import concourse.mybir as mybir
from concourse.bass_utils import run_bass_kernel_spmd

F32 = mybir.dt.float32
BF16 = mybir.dt.bfloat16
AF = mybir.ActivationFunctionType
ALU = mybir.AluOpType
AX = mybir.AxisListType

DEPTH = 4
D = 1024
KD = 8
SEQ = 2048
NS = 16
NTOK = SEQ + NS
CONF_CH = 512
CONF_KW = 31
D_FF = 2816
KF = 22
O_GLU_B = 512
O_Q = 1024
O_K = 1536
O_V = 2048
O_Z = 2560
O_BETA = 3072
O_MA = 3080
O_MB = 4104
IN_DIM = 5128
NEG = -1.0e6

SAME_ENGINE_SYNC = True
GCH = 3
STAG = 10
N_LAYERS = DEPTH

PV_G1 = 0
PV_G2 = 8
PV_BADA = 16
PV_CW = 64
PV_CB = 188
PV_LNG = 192
PV_LNB = 196
PV_SW = 200
PV_DNG = 248
PV_ALOG = 249
PV_DTB = 281
PV_FG = 313
PV_N = 321


class Buf:
    __slots__ = ("name", "w", "r", "psum")

    def __init__(self, name="", psum=False):
        self.name = name
        self.w = None
        self.r = {}
        self.psum = psum


class T:
    __slots__ = ("ap", "bufs")

    def __init__(self, ap, bufs=None):
        self.ap = ap
        if bufs is None:
            bufs = [Buf()]
        elif isinstance(bufs, Buf):
            bufs = [bufs]
        self.bufs = bufs

    def __getitem__(self, idx):
        return T(self.ap[idx], self.bufs)

    def v(self, fn):
        return T(fn(self.ap), self.bufs)


def _ap(x):
    return x.ap if isinstance(x, T) else x


def _bufs(*xs):
    out = []
    for x in xs:
        if isinstance(x, T):
            out.extend(x.bufs)
    return out


class Ring:
    def __init__(self, tiles):
        self.tiles = tiles
        self.i = 0

    def next(self):
        t = self.tiles[self.i % len(self.tiles)]
        self.i += 1
        return t


class Prog:
    ENGS = ("pe", "act", "dve", "pool", "sp")

    def __init__(self):
        self.q = {n: [] for n in self.ENGS}
        self.cnt = {}
        self.seen = {n: {} for n in self.ENGS}
        self.nops = 0

    def op(self, e, fn, reads=(), writes=(), sem=None, inc=1):
        deps = {}
        for b in reads:
            if b.w is not None:
                if deps.get(b.w[0], 0) < b.w[1]:
                    deps[b.w[0]] = b.w[1]
            if b.psum:
                for s, v in b.r.items():
                    if s != e and deps.get(s, 0) < v:
                        deps[s] = v
        for b in writes:
            if b.w is not None:
                if deps.get(b.w[0], 0) < b.w[1]:
                    deps[b.w[0]] = b.w[1]
            for s, v in b.r.items():
                if deps.get(s, 0) < v:
                    deps[s] = v
        seen = self.seen[e]
        q = self.q[e]
        for s, v in deps.items():
            if s == e and (e == "pe" or not SAME_ENGINE_SYNC):
                continue
            if seen.get(s, 0) >= v:
                continue
            seen[s] = v
            q.append(("wait", s, v))
        sname = sem or e
        val = self.cnt.get(sname, 0) + inc
        self.cnt[sname] = val
        q.append(("op", fn, sname, inc))
        for b in reads:
            if b.r.get(sname, 0) < val:
                b.r[sname] = val
        for b in writes:
            b.w = (sname, val)
            b.r = {}
        self.nops += 1
        return val

    def MM(self, out, lhsT, rhs, start=True, stop=True):
        o, l, r = _ap(out), _ap(lhsT), _ap(rhs)
        self.op("pe", lambda e: e.matmul(o, lhsT=l, rhs=r, start=start, stop=stop), _bufs(lhsT, rhs), _bufs(out))

    def TR(self, out, in_, ident):
        o, i, d = _ap(out), _ap(in_), _ap(ident)
        self.op("pe", lambda e: e.transpose(o, i, d), _bufs(in_, ident), _bufs(out))

    def ACT(self, out, in_, func, bias=None, scale=None, accum=None):
        o, i = _ap(out), _ap(in_)
        kw = {}
        if bias is None and isinstance(scale, T):
            bias = self.zeroc[0:o.shape[0], :] if o.shape[0] != 128 else self.zeroc
        if bias is not None:
            kw["bias"] = _ap(bias)
        if scale is not None:
            kw["scale"] = _ap(scale)
        if accum is not None:
            kw["accum_out"] = _ap(accum)
        self.op("act", lambda e: e.activation(out=o, in_=i, func=func, **kw), _bufs(in_, bias, scale), _bufs(out, accum))

    def TT(self, out, in0, in1, op, eng="dve"):
        o, a, b = _ap(out), _ap(in0), _ap(in1)
        self.op(eng, lambda e: e.tensor_tensor(out=o, in0=a, in1=b, op=op), _bufs(in0, in1), _bufs(out))

    def TS(self, out, in0, s1, op0, s2=None, op1=None, eng="dve"):
        o, a, x1, x2 = _ap(out), _ap(in0), _ap(s1), _ap(s2)
        if op1 is None:
            self.op(eng, lambda e: e.tensor_scalar(out=o, in0=a, scalar1=x1, scalar2=None, op0=op0), _bufs(in0, s1), _bufs(out))
        else:
            self.op(eng, lambda e: e.tensor_scalar(out=o, in0=a, scalar1=x1, scalar2=x2, op0=op0, op1=op1), _bufs(in0, s1, s2), _bufs(out))

    def STT(self, out, in0, scalar, in1, op0, op1, eng="dve"):
        o, a, s, b = _ap(out), _ap(in0), _ap(scalar), _ap(in1)
        self.op(eng, lambda e: e.scalar_tensor_tensor(out=o, in0=a, scalar=s, in1=b, op0=op0, op1=op1), _bufs(in0, scalar, in1), _bufs(out))

    def CP(self, out, in_, eng="dve"):
        o, i = _ap(out), _ap(in_)
        self.op(eng, lambda e: e.tensor_copy(out=o, in_=i), _bufs(in_), _bufs(out))

    def RED(self, out, in_, op=ALU.add, eng="dve"):
        o, i = _ap(out), _ap(in_)
        self.op(eng, lambda e: e.tensor_reduce(out=o, in_=i, axis=AX.X, op=op), _bufs(in_), _bufs(out))

    def RECIP(self, out, in_):
        o, i = _ap(out), _ap(in_)
        self.op("dve", lambda e: e.reciprocal(out=o, in_=i), _bufs(in_), _bufs(out))

    def MEMSET(self, out, val, eng="dve"):
        o = _ap(out)
        self.op(eng, lambda e: e.memset(o, val), (), _bufs(out))

    def DMA(self, q, out, in_, sem, **kw):
        o, i = _ap(out), _ap(in_)
        self.op(q, lambda e: e.dma_start(out=o, in_=i, **kw), _bufs(in_), _bufs(out), sem=sem, inc=16)


class WStream:
    NSLOT = 6
    LAG = 4
    ELEMS = 2048

    def __init__(self):
        self.specs = []
        self.recording = True

    def begin(self, P, slot_aps):
        self.P = P
        self.i = 0
        self.loaded = 0
        self.slots = [T(a) for a in slot_aps]

    def _view(self, j):
        _, kc, n = self.specs[j]
        t = self.slots[j % self.NSLOT]
        return T(t.ap[:, 0:kc * n].rearrange("p (k n) -> p k n", k=kc), t.bufs)

    def get(self, dram_ap, kc, n):
        assert kc * n <= self.ELEMS
        i = self.i
        self.i += 1
        if self.recording:
            self.specs.append((dram_ap, kc, n))
            return self._view(i)
        while self.loaded < len(self.specs) and (self.loaded < self.NSLOT or self.loaded <= i - self.LAG + self.NSLOT):
            j = self.loaded
            self.P.DMA("pool", self._view(j), self.specs[j][0], sem="w%d" % (j % self.NSLOT))
            self.loaded += 1
        assert self.loaded > i
        return self._view(i)


class StopBuild(Exception):
    pass


STOP_AT = [None]


def ckpt(name):
    if STOP_AT[0] == name:
        raise StopBuild(name)


def alias_barrier(olds, news):
    acc = {}
    for t in olds:
        for b in t.bufs:
            if b.w is not None and acc.get(b.w[0], 0) < b.w[1]:
                acc[b.w[0]] = b.w[1]
            for s, v in b.r.items():
                if acc.get(s, 0) < v:
                    acc[s] = v
    for t in news:
        for b in t.bufs:
            for s, v in acc.items():
                if b.r.get(s, 0) < v:
                    b.r[s] = v


def build_nc(n_layers=N_LAYERS):
    nc = bass.Bass("TRN2", target_bir_lowering=False)

    def din(name, shape):
        return nc.dram_tensor(name, list(shape), F32, kind="ExternalInput").ap()

    def dout(name, shape):
        return nc.dram_tensor(name, list(shape), F32, kind="ExternalOutput").ap()

    x_p = din("x_p", [SEQ, D])
    x_s = din("x_s", [NS, D])
    c_in = din("c_in", [NS + 1, D])
    st_conf = din("st_conf", [DEPTH, NS, 30, CONF_CH])
    st_short = din("st_short", [DEPTH, NS, 3, 1536])
    st_delta = din("st_delta", [DEPTH, NS, 4, 128, 128])
    w_ada = din("w_ada", [DEPTH, D, 6 * D])
    w_in = din("w_in", [DEPTH, D, IN_DIM])
    w_conf_out = din("w_conf_out", [DEPTH, CONF_CH, D])
    w_delta_out = din("w_delta_out", [DEPTH, 512, D])
    w_merge_out = din("w_merge_out", [DEPTH, D, D])
    w_ffn_in = din("w_ffn_in", [DEPTH, D, 2 * D_FF])
    w_ffn_out = din("w_ffn_out", [DEPTH, D_FF, D])
    pvec = din("pvec", [DEPTH, 128, PV_N])
    consts = din("consts", [128, 5, 128])
    bmask = din("bmask", [128, 3, 128])

    y_p = dout("y_p", [SEQ, D])
    y_s = dout("y_s", [NS, D])
    o_conf_p = dout("o_conf_p", [DEPTH, 30, CONF_CH])
    o_conf_s = dout("o_conf_s", [DEPTH, NS, 30, CONF_CH])
    o_short_p = dout("o_short_p", [DEPTH, 3, 1536])
    o_short_s = dout("o_short_s", [DEPTH, NS, 3, 1536])
    o_delta_p = dout("o_delta_p", [DEPTH, 4, 128, 128])
    o_delta_s = dout("o_delta_s", [DEPTH, NS, 4, 128, 128])

    with ExitStack() as es:
        def sb(name, shape, dt=F32):
            return es.enter_context(nc.sbuf_tensor(name, list(shape), dt))

        xT_t = sb("xT", [128, KD, NTOK])
        hT_t = sb("hT", [128, KD, 1040], BF16)
        A_CA = 1070
        A_QKV = 1043
        A_ZS = A_QKV + 3 * 1040
        R_WORDS = 1070 + 4 * 1040
        A_CAT = R_WORDS
        A_OGT = A_CAT + 2080
        ARENA_WORDS = A_OGT + 2080
        assert A_ZS + 1040 <= R_WORDS and 11 * 520 <= A_OGT
        arena = sb("arena", [128, ARENA_WORDS])
        wslots = [sb("wslot%d" % i, [128, WStream.ELEMS], BF16) for i in range(WStream.NSLOT)]
        scr_t = [sb("scr%d" % i, [128, 512]) for i in range(4)]
        st_t = [sb("st%d" % i, [128, 512]) for i in range(3)]
        sm_t = [sb("sm%d" % i, [128, 128]) for i in range(16)]
        LGN = ("qkT", "kdc", "kbg", "vb", "u", "wT", "vn", "qg", "No")
        lg_t = {n: [sb("%s%d" % (n, i), [128, 128]) for i in range(2)] for n in LGN}
        S_t = [sb("S%d" % h, [128, 128]) for h in range(4)]
        Ss_t = sb("Ss", [128, NS * 128])
        cols_t = sb("cols", [128, 12, 32])
        scol_t = sb("scol", [128, 12, 16])
        modT_t2 = [sb("modT%d" % i, [128, 48, NS + 1]) for i in range(2)]
        gs_t2 = [sb("gs%d" % i, [128, 16, NS + 1]) for i in range(2)]
        pvn_t = sb("pvn", [128, 64])
        cT_t = sb("cT", [128, KD, NS + 1], BF16)
        pv_t = sb("pv", [128, PV_N])
        negA_t = sb("negA", [128, 32])
        cst_t = sb("cst", [128, 6, 128])
        upre_t = sb("upre", [128, 4, 30])
        qpre_t = sb("qpre", [128, 12, 3])
        sstT_t = sb("sstT", [128, 12, 48])
        pc_t = [sb("pc%d" % i, [NS, 128]) for i in range(4)]
        qkb_t = sb("qkb", [128, 2, 1040], BF16)
        bm_t = sb("bmask_sb", [128, 3, 128], BF16)
        ps_t = [es.enter_context(nc.psum_tensor("ps%d" % i, [128, 512], F32)) for i in range(8)]
        print("SBUF bytes remaining per partition:", nc.sbuf_bytes_remaining)

        sem_names = ["pe", "act", "dve", "pool"] + ["w%d" % i for i in range(WStream.NSLOT)] + \
            ["ld0", "ld1", "ldc", "ldp", "ldpn", "ldbm", "ldcs", "ldss", "ldSs", "o_cpo", "o_cso", "o_pc0", "o_pc1", "o_pc2", "o_pc3",
             "o_S0", "o_S1", "o_S2", "o_S3", "o_Ss", "o_y0", "o_y1", "o_d2d"]

        ws = WStream()

        def build(P):
            ws.begin(P, [w[:] for w in wslots])
            MM, TR, ACT, TT, TS, STT, CP, RED, RECIP, MEMSET, DMA = (P.MM, P.TR, P.ACT, P.TT, P.TS, P.STT, P.CP, P.RED,
                                                                     P.RECIP, P.MEMSET, P.DMA)
            GB = [(0, 512), (512, 512), (1024, 512), (1536, 512), (2048, NS)]
            xT = [[T(xT_t[:, k, b0:b0 + bw]) for (b0, bw) in GB] for k in range(KD)]
            LB = [(0, 512), (512, 512), (1024, NS)]
            hT = [[T(hT_t[:, k, b0:b0 + bw]) for (b0, bw) in LB] for k in range(KD)]
            ar = arena
            upad = T(ar[:, 0:1070])
            ca = [T(ar[:, A_CA + j * 1040:A_CA + (j + 1) * 1040]) for j in range(4)]
            pad = T(ar[:, 0:1043])
            qkv = [T(ar[:, A_QKV + c * 1040:A_QKV + (c + 1) * 1040]) for c in range(3)]
            zs = T(ar[:, A_ZS:A_ZS + 1040])
            merged_v = ar[:, 0:4160].bitcast(BF16).rearrange("p (k n) -> p k n", k=8)
            merged = [T(merged_v[:, c, :]) for c in range(8)]
            caT_v = ar[:, A_CAT:A_CAT + 2080].bitcast(BF16).rearrange("p (k n) -> p k n", k=4)
            caT = [T(caT_v[:, j, :]) for j in range(4)]
            ogT_v = ar[:, A_OGT:A_OGT + 2080].bitcast(BF16).rearrange("p (k n) -> p k n", k=4)
            ogT = [T(ogT_v[:, h, :]) for h in range(4)]
            act_v = ar[:, 0:11 * 520].bitcast(BF16).rearrange("p (k n) -> p k n", k=11)
            actT = [T(act_v[:, j, :]) for j in range(11)]
            cs_tok = T(ar[:120, A_OGT:A_OGT + 2048].rearrange("p (r c) -> p r c", r=4))
            PH1 = [upad] + ca + caT + [cs_tok]
            PH2 = [pad] + qkv + [zs] + ogT
            PH3 = merged
            PH4 = actT

            scr = Ring([T(t[:]) for t in scr_t])
            st1, st2, st3 = [T(t[:]) for t in st_t]
            cpo = T(st_t[0][:30, :], st1.bufs)
            cso = T(st_t[2][:NS, :], st3.bufs)
            cstT = T(st_t[1][:, 0:NS * 31].rearrange("p (s j) -> p s j", j=31), st2.bufs)
            sm_base = [T(t[:]) for t in sm_t]
            sm = Ring(sm_base)
            sm_extra = [T(st_t[i][:, k_ * 128:(k_ + 1) * 128]) for i in range(3) for k_ in range(4)]
            smx = Ring(sm_base + sm_extra)
            lgx_t = {n: list(v) + [Ss_t[:, j * 128:(j + 1) * 128]] for j, (n, v) in enumerate(lg_t.items())}
            lg = {n: Ring([T(t[:]) for t in v]) for n, v in lgx_t.items()}
            lgb = {n: Ring([T(t[:, 0:64].bitcast(BF16)) for t in v]) for n, v in lgx_t.items()}
            lgb["Xb"] = Ring([T(t[:, 64:128].bitcast(BF16)) for t in lgx_t["qkT"]])
            lgb["Sb"] = Ring([T(t[:, 64:128].bitcast(BF16)) for t in lgx_t["kdc"]])
            qkb = [T(qkb_t[:, i, :]) for i in range(2)]
            S = [T(t[:]) for t in S_t]
            Ss = T(Ss_t[:].rearrange("p (s v) -> p s v", v=128))
            ld = [T(Ss_t[:, i * 1024:(i + 1) * 1024], Ss.bufs) for i in range(2)]
            ss_tok = T(Ss_t[:48, 0:1536], Ss.bufs)
            big = Ring([T(ps_t[i][:], Buf("pb%d" % i, True)) for i in range(4)])
            sbank = [Buf("psb%d" % i, True) for i in range(4)]
            small = Ring([T(ps_t[4 + i % 4][:, (i // 4) * 128:(i // 4 + 1) * 128], sbank[i % 4]) for i in range(16)])
            colsv = [T(cols_t[:, i, :]) for i in range(12)]
            beta_c, g_c, gc_c, ngc_c, egc_c, bgc_c, nbeta_c, kdec_c, egl_c, tmp_c, tmp2_c, colr = colsv
            scol = Ring([T(scol_t[:, i, :]) for i in range(12)])
            modTs = [T(t[:]) for t in modT_t2]
            gss = [T(t[:]) for t in gs_t2]
            pvn = T(pvn_t[:])
            cur = {}
            cT = T(cT_t[:])
            pvl = T(pv_t[:])
            negA = T(negA_t[:])
            cst = T(cst_t[:, 0:5, :])
            ident = cst[:, 0, :]
            ones = cst[:, 1, :]
            tri = cst[:, 2, :]
            nm_strict = cst[:, 3, :]
            nm_inclT = cst[:, 4, :]
            eps6 = T(cst_t[:, 5, 0:1])
            eps5 = T(cst_t[:, 5, 1:2])
            onec = T(cst_t[:, 5, 2:3])
            zeroc = T(cst_t[:, 5, 3:4])
            P.zeroc = zeroc
            upre = T(upre_t[:])
            qpre = T(qpre_t[:])
            sstT = T(sstT_t[:])
            pcs = [T(t[:]) for t in pc_t]
            pc_i = [0]

            def piece_out(dram_ap, psrc, rows):
                i = pc_i[0] % 4
                pc_i[0] += 1
                CP(pcs[i][:rows, :], psrc)
                DMA("sp", dram_ap, pcs[i][:rows, :], sem="o_pc%d" % i)

            DMA("sp", cst, consts[:, :, :], sem="ldc")
            bm = T(bm_t[:])
            DMA("pool", bm, bmask[:, :, :], sem="ldbm")
            ones_b = bm[:, 2, :]

            def sqb_tile():
                t_ = scr.next()
                return T(t_.ap[:, 0:256].bitcast(BF16), t_.bufs)
            MEMSET(eps6, 1e-6)
            MEMSET(eps5, 1e-5)
            MEMSET(onec, 1.0)
            MEMSET(zeroc, 0.0)

            for t in range(SEQ // 128):
                l_ = ld[t % 2]
                DMA("sp", l_, x_p[t * 128:(t + 1) * 128, :], sem="ld%d" % (t % 2))
                gb = t // 4
                for k in range(KD):
                    p = small.next()
                    TR(p, l_[:, k * 128:(k + 1) * 128], ident)
                    dst = T(xT_t[:, k, t * 128:(t + 1) * 128], xT[k][gb].bufs)
                    if k % 2 == 0:
                        ACT(dst, p, AF.Copy)
                    else:
                        CP(dst, p)
            l_ = ld[0]
            DMA("sp", l_[:NS, :], x_s[:, :], sem="ld0")
            for k in range(KD):
                p = small.next()
                TR(p[:, :NS], l_[:NS, k * 128:(k + 1) * 128], ident[:NS, :NS])
                CP(xT[k][4], p[:, :NS])
            l_ = ld[1]
            DMA("sp", l_[:NS + 1, :], c_in[:, :], sem="ld1")
            ACT(l_[:NS + 1, :], l_[:NS + 1, :], AF.Silu)
            for k in range(KD):
                p = small.next()
                TR(p[:, :NS + 1], l_[:NS + 1, k * 128:(k + 1) * 128], ident[:NS + 1, :NS + 1])
                CP(cT[:, k, :], p[:, :NS + 1])

            ckpt('init')
            def rmsnorm_mod(blocks, gbs, gsoff, shoff):
                for b, (b0, bw) in enumerate(blocks):
                    gb = gbs[b]
                    ps = big.next()
                    for k in range(KD):
                        sq = sqb_tile()
                        ACT(sq[:, :bw], xT[k][gb], AF.Square)
                        MM(ps[:, :bw], ones_b, sq[:, :bw], start=(k == 0), stop=(k == KD - 1))
                    rs = st1
                    ACT(rs[:, :bw], ps[:, :bw], AF.Ln, scale=1.0 / D, bias=eps6)
                    ACT(rs[:, :bw], rs[:, :bw], AF.Exp, scale=-0.5)
                    for k in range(KD):
                        tmp = scr.next()
                        TT(tmp[:, :bw], xT[k][gb], rs[:, :bw], ALU.mult)
                        if bw == NS:
                            TT(tmp[:, :bw], tmp[:, :bw], gs[:, gsoff + k, 1:NS + 1], ALU.mult)
                            TT(hT[k][b], tmp[:, :bw], modT[:, shoff + k, 1:NS + 1], ALU.add)
                        else:
                            ACT(hT[k][b], tmp[:, :bw], AF.Identity, scale=gs[:, gsoff + k, 0:1], bias=modT[:, shoff + k, 0:1])

            def resid_update(c, ps, bw, gb, gtoff):
                if bw == NS:
                    tmp = scr.next()
                    TT(tmp[:, :bw], ps[:, :bw], modT[:, gtoff + c, 1:NS + 1], ALU.mult)
                    TT(xT[c][gb], xT[c][gb], tmp[:, :bw], ALU.add)
                else:
                    STT(xT[c][gb], ps[:, :bw], modT[:, gtoff + c, 0:1], xT[c][gb], ALU.mult, ALU.add)

            def wv(ap2d):
                return ap2d.rearrange("(k p) n -> p k n", p=128)

            def mod_piece(ll, n2):
                mT = modTs[ll % 2]
                w = ws.get(wv(w_ada[ll, :, n2 * 256:(n2 + 1) * 256]), KD, 256)
                for half in range(2):
                    n = n2 * 2 + half
                    p = small.next()
                    for k in range(KD):
                        MM(p[:, :NS + 1], w[:, k, half * 128:(half + 1) * 128], cT[:, k, :], start=(k == 0), stop=(k == KD - 1))
                    ACT(mT[:, n, :], p[:, :NS + 1], AF.Identity, bias=pvn[:, PV_BADA + n:PV_BADA + n + 1])

            def mod_finish(ll):
                mT, g_ = modTs[ll % 2], gss[ll % 2]
                for k in range(KD):
                    TS(g_[:, k, :], mT[:, 8 + k, :], 1.0, ALU.add, pvn[:, PV_G1 + k:PV_G1 + k + 1], ALU.mult)
                    TS(g_[:, 8 + k, :], mT[:, 32 + k, :], 1.0, ALU.add, pvn[:, PV_G2 + k:PV_G2 + k + 1], ALU.mult)

            for l in range(n_layers):
                DMA("sp", pvl, pvec[l], sem="ldp")
                modT, gs = modTs[l % 2], gss[l % 2]
                cur["modT"], cur["gs"] = modT, gs
                if l == 0:
                    DMA("sp", pvn, pvec[0, :, 0:64], sem="ldpn")
                    for n2 in range(24):
                        mod_piece(0, n2)
                    mod_finish(0)
                ACT(negA, pvl[:, PV_ALOG:PV_ALOG + 32], AF.Exp)
                ACT(negA, negA, AF.Identity, scale=-1.0)
                ckpt('mod')
                DMA("sp", o_conf_s[l, :, 0:29, :], st_conf[l, :, 1:30, :], sem="o_d2d")
                DMA("sp", o_short_s[l, :, 0:2, :], st_short[l, :, 1:3, :], sem="o_d2d")
                DMA("sp", ss_tok, st_short[l].rearrange("s j c -> (s j) c"), sem="ldss")
                for gch in range(12):
                    p = small.next()
                    TR(p[:, :48], ss_tok[:48, gch * 128:(gch + 1) * 128], ident[:48, :48])
                    CP(sstT[:, gch, :], p[:, :48])

                ckpt('lstart')
                for pi in range(2):
                    blocks = LB[:2] if pi == 0 else LB
                    gbs = [0, 1] if pi == 0 else [2, 3, 4]
                    has_s = pi == 1
                    W = 1024 + (NS if has_s else 0)
                    NCH = 8

                    rmsnorm_mod(blocks, gbs, 0, 0)

                    ckpt('norm1')
                    alias_barrier(PH2 + PH3 + PH4, PH1)
                    if has_s:
                        DMA("sp", cs_tok, st_conf[l].rearrange("(r s) j c -> (s j) r c", r=4), sem="ldcs")
                    else:
                        MEMSET(upad[:, 0:30], 0.0)
                    for j in range(4):
                        if has_s:
                            CP(upad[:, 0:30], upre[:, j, :])
                        wa = ws.get(wv(w_in[l, :, j * 128:(j + 1) * 128]), KD, 128)
                        wb = ws.get(wv(w_in[l, :, O_GLU_B + j * 128:O_GLU_B + (j + 1) * 128]), KD, 128)
                        for b, (b0, bw) in enumerate(blocks):
                            pa = big.next()
                            pb = big.next()
                            for k in range(KD):
                                MM(pa[:, :bw], wa[:, k, :], hT[k][b], start=(k == 0), stop=(k == KD - 1))
                            for k in range(KD):
                                MM(pb[:, :bw], wb[:, k, :], hT[k][b], start=(k == 0), stop=(k == KD - 1))
                            sg = scr.next()
                            ACT(sg[:, :bw], pb[:, :bw], AF.Sigmoid)
                            TT(upad[:, 30 + b0:30 + b0 + bw], pa[:, :bw], sg[:, :bw], ALU.mult)
                        cw0 = PV_CW + j * 31
                        caj = ca[j][:, 0:1024]
                        TS(caj, upad[:, 0:1024], pvl[:, cw0:cw0 + 1], ALU.mult, pvl[:, PV_CB + j:PV_CB + j + 1], ALU.add)
                        for i in range(1, CONF_KW):
                            STT(caj, upad[:, i:i + 1024], pvl[:, cw0 + i:cw0 + i + 1], caj, ALU.mult, ALU.add)
                        if not has_s:
                            ACT(upre[:, j, :], upad[:, 1024:1054], AF.Copy)
                        else:
                            p = small.next()
                            TR(p[:30, :], upad[:, 1024:1054], ident)
                            CP(cpo[:30, j * 128:(j + 1) * 128], p[:30, :])
                            us_ = upad[:, 30 + 1024:30 + 1024 + NS]
                            for rt in range(4):
                                p = small.next()
                                TR(p[:, :120], cs_tok[:120, rt, j * 128:(j + 1) * 128], ident[:120, :120])
                                CP(cstT[:, 4 * rt:4 * rt + 4, 0:30], p[:, :120].v(lambda a: a.rearrange("p (s j) -> p s j", j=30)))
                            CP(cstT[:, :, 30], us_)
                            prod = scr.next()[:, 0:NS * 31].v(lambda a: a.rearrange("p (s j) -> p s j", j=31))
                            TT(prod, cstT, pvl[:, cw0:cw0 + 31].v(lambda a: a.unsqueeze(1).to_broadcast([128, NS, 31])), ALU.mult)
                            RED(ca[j][:, 1024:1024 + NS], prod)
                            TS(ca[j][:, 1024:1024 + NS], ca[j][:, 1024:1024 + NS], pvl[:, PV_CB + j:PV_CB + j + 1], ALU.add)
                            p = small.next()
                            TR(p[:NS, :], us_, ident)
                            CP(cso[:NS, j * 128:(j + 1) * 128], p[:NS, :])
                    if has_s:
                        DMA("sp", o_conf_p[l], cpo, sem="o_cpo")
                        DMA("sp", o_conf_s[l, :, 29, :], cso, sem="o_cso")
                    ckpt('conf')
                    for b, (b0, bw) in enumerate(blocks):
                        p1 = big.next()
                        p2 = big.next()
                        for j in range(4):
                            cb_ = sqb_tile()
                            ACT(cb_[:, :bw], ca[j][:, b0:b0 + bw], AF.Copy)
                            MM(p1[:, :bw], ones_b, cb_[:, :bw], start=(j == 0), stop=(j == 3))
                        for j in range(4):
                            sq = sqb_tile()
                            ACT(sq[:, :bw], ca[j][:, b0:b0 + bw], AF.Square)
                            MM(p2[:, :bw], ones_b, sq[:, :bw], start=(j == 0), stop=(j == 3))
                        mean, msq, var = st1, st2, st3
                        ACT(mean[:, :bw], p1[:, :bw], AF.Identity, scale=1.0 / CONF_CH)
                        TT(msq[:, :bw], mean[:, :bw], mean[:, :bw], ALU.mult)
                        STT(var[:, :bw], p2[:, :bw], 1.0 / CONF_CH, msq[:, :bw], ALU.mult, ALU.subtract)
                        ACT(var[:, :bw], var[:, :bw], AF.Ln, bias=eps5)
                        ACT(var[:, :bw], var[:, :bw], AF.Exp, scale=-0.5)
                        for j in range(4):
                            t = scr.next()
                            TT(t[:, :bw], ca[j][:, b0:b0 + bw], mean[:, :bw], ALU.subtract)
                            TT(t[:, :bw], t[:, :bw], var[:, :bw], ALU.mult)
                            ACT(caT[j][:, b0:b0 + bw], t[:, :bw], AF.Silu, scale=pvl[:, PV_LNG + j:PV_LNG + j + 1],
                                bias=pvl[:, PV_LNB + j:PV_LNB + j + 1])

                    ckpt('confln')
                    alias_barrier(PH1, PH2)
                    wbg = ws.get(wv(w_in[l, :, O_BETA:O_BETA + 8]), KD, 8)
                    pbg = small.next()
                    for c in range(NCH):
                        for k in range(KD):
                            MM(pbg[:, c * 8:(c + 1) * 8], hT[k][c // 4][:, (c % 4) * 128:(c % 4 + 1) * 128], wbg[:, k, :],
                               start=(k == 0), stop=(k == KD - 1))
                    pbg3 = pbg[:, 0:64].v(lambda a: a.rearrange("p (c e) -> p c e", e=8))

                    def v3(t):
                        return t.v(lambda a: a.rearrange("p (c e) -> p c e", e=4))
                    ACT(v3(beta_c), pbg3[:, :, 0:4], AF.Sigmoid)
                    TT(v3(tmp_c), pbg3[:, :, 4:8], v3(pvl[:, PV_DTB:PV_DTB + 32]), ALU.add)
                    ACT(tmp_c, tmp_c, AF.Exp)
                    ACT(tmp_c, tmp_c, AF.Ln, bias=onec)
                    TT(g_c, tmp_c, negA, ALU.mult)
                    pgc = small.next()
                    pgl = small.next()
                    for c in range(NCH):
                        MM(pgc[:, c * 4:(c + 1) * 4], tri, g_c[:, c * 4:(c + 1) * 4])
                    for c in range(NCH):
                        MM(pgl[:, c * 4:(c + 1) * 4], ones, g_c[:, c * 4:(c + 1) * 4])
                    CP(gc_c, pgc[:, 0:32])
                    ACT(ngc_c, pgc[:, 0:32], AF.Identity, scale=-1.0)
                    ACT(egc_c, pgc[:, 0:32], AF.Exp)
                    TT(bgc_c, beta_c, egc_c, ALU.mult)
                    ACT(nbeta_c, beta_c, AF.Identity, scale=-1.0)
                    TT(tmp2_c, pgl[:, 0:32], gc_c, ALU.subtract)
                    ACT(kdec_c, tmp2_c, AF.Exp)
                    ACT(egl_c, pgl[:, 0:32], AF.Exp)
                    if has_s:
                        pbs = small.next()
                        for k in range(KD):
                            MM(pbs[:NS, 0:8], hT[k][2], wbg[:, k, :], start=(k == 0), stop=(k == KD - 1))
                        beta_s = T(cols_t[:NS, 11, 0:4], colr.bufs)
                        eg_s = T(cols_t[:NS, 11, 4:8], colr.bufs)
                        tmp_s = T(cols_t[:NS, 11, 8:12], colr.bufs)
                        ACT(beta_s, pbs[:NS, 0:4], AF.Sigmoid)
                        TT(tmp_s, pbs[:NS, 4:8], pvl[:NS, PV_DTB:PV_DTB + 4], ALU.add)
                        ACT(tmp_s, tmp_s, AF.Exp)
                        ACT(tmp_s, tmp_s, AF.Ln, bias=onec[:NS, :])
                        TT(tmp_s, tmp_s, negA[:NS, 0:4], ALU.mult)
                        ACT(eg_s, tmp_s, AF.Exp)
                    else:
                        MEMSET(pad[:, 0:3], 0.0)

                    ckpt('dprol')
                    for h in range(4):
                        wq = [ws.get(wv(w_in[l, :, off + h * 128:off + (h + 1) * 128]), KD, 128) for off in (O_Q, O_K, O_V, O_Z)]
                        for ci in range(3):
                            gch = ci * 4 + h
                            if has_s:
                                CP(pad[:, 0:3], qpre[:, gch, :])
                            for b, (b0, bw) in enumerate(blocks):
                                ps = big.next()
                                for k in range(KD):
                                    MM(ps[:, :bw], wq[ci][:, k, :], hT[k][b], start=(k == 0), stop=(k == KD - 1))
                                ACT(pad[:, 3 + b0:3 + b0 + bw], ps[:, :bw], AF.Copy)
                            sw0 = PV_SW + gch * 4
                            qc = qkv[ci][:, 0:1024]
                            TS(qc, pad[:, 0:1024], pvl[:, sw0:sw0 + 1], ALU.mult)
                            for i in range(1, 4):
                                STT(qc, pad[:, i:i + 1024], pvl[:, sw0 + i:sw0 + i + 1], qc, ALU.mult, ALU.add)
                            if not has_s:
                                ACT(qpre[:, gch, :], pad[:, 1024:1027], AF.Copy)
                            else:
                                p = small.next()
                                TR(p[:3, :], pad[:, 1024:1027], ident)
                                piece_out(o_short_p[l, :, gch * 128:(gch + 1) * 128], p[:3, :], 3)
                                xs_ = pad[:, 1027:1027 + NS]
                                qs_ = qkv[ci][:, 1024:1024 + NS]
                                st3v = sstT[:, gch, :].v(lambda a: a.rearrange("p (s j) -> p s j", j=3))
                                TS(qs_, xs_, pvl[:, sw0 + 3:sw0 + 4], ALU.mult)
                                for i in range(3):
                                    STT(qs_, st3v[:, :, i], pvl[:, sw0 + i:sw0 + i + 1], qs_, ALU.mult, ALU.add)
                                p = small.next()
                                TR(p[:NS, :], xs_, ident)
                                piece_out(o_short_s[l, :, 2, gch * 128:(gch + 1) * 128], p[:NS, :], NS)
                            ACT(qkv[ci][:, 0:W], qkv[ci][:, 0:W], AF.Silu)
                        for b, (b0, bw) in enumerate(blocks):
                            ps = big.next()
                            for k in range(KD):
                                MM(ps[:, :bw], wq[3][:, k, :], hT[k][b], start=(k == 0), stop=(k == KD - 1))
                            ACT(zs[:, b0:b0 + bw], ps[:, :bw], AF.Silu)
                        for ci in range(2):
                            for b, (b0, bw) in enumerate(blocks):
                                sq = sqb_tile()
                                ACT(sq[:, :bw], qkv[ci][:, b0:b0 + bw], AF.Square)
                                ps = big.next()
                                MM(ps[:, :bw], ones_b, sq[:, :bw])
                                rn = scr.next()
                                ACT(rn[:, :bw], ps[:, :bw], AF.Ln, bias=eps6)
                                ACT(rn[:, :bw], rn[:, :bw], AF.Exp, scale=-0.5)
                                if ci == 0:
                                    STT(qkv[ci][:, b0:b0 + bw], qkv[ci][:, b0:b0 + bw], 128.0 ** -0.5, rn[:, :bw], ALU.mult, ALU.mult)
                                else:
                                    TT(qkv[ci][:, b0:b0 + bw], qkv[ci][:, b0:b0 + bw], rn[:, :bw], ALU.mult)
                                if bw != NS:
                                    CP(qkb[ci][:, b0:b0 + bw], qkv[ci][:, b0:b0 + bw], eng="pool")
                        if not has_s:
                            MEMSET(S[h], 0.0)

                        ckpt('dhead')
                        def chunk_gen(c):
                            cs = slice(c * 128, (c + 1) * 128)
                            ix = c * 4 + h
                            col = slice(ix, ix + 1)
                            qTc, kTc, vTc = qkv[0][:, cs], qkv[1][:, cs], qkv[2][:, cs]
                            qbc, kbc = qkb[0][:, cs], qkb[1][:, cs]
                            grep = smx.next()
                            TS(grep, ones, g_c[:, col], ALU.mult)
                            pg = small.next()
                            MM(pg, grep, tri)
                            yield
                            pre1 = smx.next()
                            STT(pre1, pg, -1.0, nm_strict, ALU.mult, ALU.add)
                            pre2 = smx.next()
                            TT(pre2, pg, nm_inclT, ALU.add)
                            egr = smx.next()
                            ACT(egr, pg, AF.Exp)
                            yield
                            Ds = smx.next()
                            ACT(Ds, pre1, AF.Exp, bias=gc_c[:, col])
                            DT = smx.next()
                            ACT(DT, pre2, AF.Exp, bias=ngc_c[:, col])
                            qgc = lgb["qg"].next()
                            TT(qgc, qTc, egr, ALU.mult)
                            yield
                            pk = small.next()
                            MM(pk, kbc, kbc)
                            Nm = smx.next()
                            STT(Nm, pk, nbeta_c[:, col], Ds, ALU.mult, ALU.mult)
                            yield
                            pq = small.next()
                            MM(pq, kbc, qbc)
                            qkT = lgb["qkT"].next()
                            TT(qkT, pq, DT, ALU.mult)
                            yield
                            Nbd = smx.next()
                            TT(Nbd, Nm, bm[:, 0, :], ALU.mult)
                            No = lg["No"].next()
                            TT(No, Nm, bm[:, 1, :], ALU.mult, eng="pool")
                            pt = small.next()
                            TR(pt, Nbd, ident)
                            Mm = smx.next()
                            ACT(Mm, pt, AF.Copy)
                            X = smx.next()
                            TT(X, Mm, ident, ALU.add)
                            yield
                            ptk = small.next()
                            TR(ptk, kTc, ident)
                            kbg = lgb["kbg"].next()
                            ACT(kbg, ptk, AF.Identity, scale=bgc_c[:, col])
                            kdc = lgb["kdc"].next()
                            ACT(kdc, ptk, AF.Identity, scale=kdec_c[:, col])
                            yield
                            ptv = small.next()
                            TR(ptv, vTc, ident)
                            vb = lgb["vb"].next()
                            ACT(vb, ptv, AF.Identity, scale=beta_c[:, col])
                            Pm, Pn = Mm, Nbd
                            for lev in range(1, 6):
                                yield
                                pp2 = small.next()
                                MM(pp2, Pm, Pn)
                                Pn2 = smx.next()
                                ACT(Pn2, pp2, AF.Copy)
                                if lev < 5:
                                    pp1 = small.next()
                                    MM(pp1, Pn, Pm)
                                    Pm2 = smx.next()
                                    CP(Pm2, pp1)
                                yield
                                px = small.next()
                                MM(px, Pn2, X)
                                X2 = smx.next()
                                TT(X2, px, X, ALU.add)
                                X, Pn = X2, Pn2
                                if lev < 5:
                                    Pm = Pm2
                            yield
                            pxt = small.next()
                            TR(pxt, X, ident)
                            Yt = smx.next()
                            ACT(Yt, pxt, AF.Copy)
                            pt1 = small.next()
                            MM(pt1, No, X)
                            T1 = smx.next()
                            CP(T1, pt1)
                            yield
                            pxx = small.next()
                            MM(pxx, Yt, T1)
                            X2 = smx.next()
                            TT(X2, pxx, X, ALU.add)
                            X = X2
                            yield
                            Xb = lgb["Xb"].next()
                            CP(Xb, X, eng="pool")
                            yield
                            pu = small.next()
                            MM(pu, Xb, vb)
                            u = lg["u"].next()
                            ACT(u, pu, AF.Copy)
                            yield
                            pw = small.next()
                            MM(pw, kbg, Xb)
                            wT = lgb["wT"].next()
                            CP(wT, pw)
                            yield
                            Sb = lgb["Sb"].next()
                            CP(Sb, S[h], eng="pool")
                            pvn = small.next()
                            MM(pvn, wT, Sb)
                            vn = lgb["vn"].next()
                            TT(vn, u, pvn, ALU.subtract)
                            yield
                            po = small.next()
                            MM(po, qgc, Sb, start=True, stop=False)
                            MM(po, qkT, vn, start=False, stop=True)
                            yield
                            pS = small.next()
                            MM(pS, kdc, vn)
                            STT(S[h], S[h], egl_c[:, col], pS, ALU.mult, ALU.add)
                            yield
                            osq = smx.next()
                            ssq = scol.next()
                            ACT(osq, po, AF.Square, accum=ssq[:, 0:1])
                            ACT(ssq[:, 0:1], ssq[:, 0:1], AF.Ln, scale=1.0 / 128, bias=eps6)
                            ACT(ssq[:, 0:1], ssq[:, 0:1], AF.Exp, scale=-0.5)
                            yield
                            on = smx.next()
                            ACT(on, po, AF.Identity, scale=ssq[:, 0:1])
                            yield
                            pT = small.next()
                            TR(pT, on, ident)
                            STT(ogT[h][:, cs], pT, pvl[:, PV_DNG:PV_DNG + 1], zs[:, cs], ALU.mult, ALU.mult)


                        xtiles = sm_extra + [lg[n_].tiles[2] for n_ in lg] + [lgb[n_].tiles[2] for n_ in lgb]
                        alias_barrier([st1, st2, st3, Ss], xtiles)
                        mod_q = []
                        if pi == 0 and l + 1 < n_layers:
                            if h == 0:
                                DMA("sp", pvn, pvec[l + 1, :, 0:64], sem="ldpn")
                            mod_q = list(range(h * 6, h * 6 + 6))
                        pending = list(range(NCH))
                        active = []
                        rnd = 0
                        while pending or active:
                            if mod_q and rnd % 14 == 7:
                                mod_piece(l + 1, mod_q.pop(0))
                            if pending and len(active) < GCH and (rnd % STAG == 0 or not active):
                                active.append(chunk_gen(pending.pop(0)))
                            for g_ in list(active):
                                try:
                                    next(g_)
                                except StopIteration:
                                    active.remove(g_)
                            rnd += 1
                        alias_barrier(xtiles, [st1, st2, st3, Ss])
                        ckpt('dchunks')
                        for n2 in mod_q:
                            mod_piece(l + 1, n2)
                        if pi == 0 and l + 1 < n_layers and h == 3:
                            mod_finish(l + 1)
                        if has_s:
                            DMA("sp", o_delta_p[l, h], S[h], sem="o_S%d" % h)
                            sc = slice(1024, 1024 + NS)
                            qs, ks, vs = qkv[0][:, sc], qkv[1][:, sc], qkv[2][:, sc]
                            DMA("sp", Ss, st_delta[l, :, h].rearrange("s p v -> p s v"), sem="ldSs")

                            def bcast16(colap):
                                dg = sm.next()
                                TS(dg[:NS, :NS], ident[:NS, :NS], colap, ALU.mult)
                                pB = small.next()
                                MM(pB[:, :NS], ones[:NS, :], dg[:NS, :NS])
                                o_ = scol.next()
                                CP(o_, pB[:, :NS])
                                return o_
                            bbc = bcast16(beta_s[:, h:h + 1])
                            ebc = bcast16(eg_s[:, h:h + 1])
                            t0 = scol.next()
                            TT(t0, qs, ks, ALU.mult)
                            pqk = small.next()
                            MM(pqk[:, :NS], ones, t0)
                            qkbc = scol.next()
                            CP(qkbc, pqk[:, :NS])
                            pkS = small.next()
                            pqS = small.next()
                            for s in range(NS):
                                MM(pkS[:, s:s + 1], Ss[:, s, :], ks[:, s:s + 1])
                            for s in range(NS):
                                MM(pqS[:, s:s + 1], Ss[:, s, :], qs[:, s:s + 1])
                            t1 = scol.next()
                            TT(t1, pkS[:, :NS], ebc, ALU.mult)
                            TT(t1, vs, t1, ALU.subtract)
                            vnT = scol.next()
                            TT(vnT, t1, bbc, ALU.mult)
                            oT = scol.next()
                            TT(oT, pqS[:, :NS], ebc, ALU.mult)
                            t2 = scol.next()
                            TT(t2, qkbc, vnT, ALU.mult)
                            TT(oT, oT, t2, ALU.add)
                            t3 = scol.next()
                            TT(t3, oT, oT, ALU.mult)
                            pss = small.next()
                            MM(pss[:, :NS], ones, t3)
                            rs_ = scol.next()
                            ACT(rs_, pss[:, :NS], AF.Ln, scale=1.0 / 128, bias=eps6)
                            ACT(rs_, rs_, AF.Exp, scale=-0.5)
                            TT(oT, oT, rs_, ALU.mult)
                            STT(ogT[h][:, sc], oT, pvl[:, PV_DNG:PV_DNG + 1], zs[:, sc], ALU.mult, ALU.mult)
                            pkt = small.next()
                            TR(pkt[:NS, :], ks, ident)
                            k_tok = scr.next()[:, 0:128]
                            CP(k_tok[:NS, :], pkt[:NS, :])
                            pvt = small.next()
                            TR(pvt[:NS, :], vnT, ident)
                            vn_tok = scr.next()[:, 0:128]
                            CP(vn_tok[:NS, :], pvt[:NS, :])
                            for g0 in range(0, NS, 8):
                                rl_, pl_ = [], []
                                for s in range(g0, g0 + 8):
                                    rhs_s = sm.next()
                                    TS(rhs_s[:NS, :], vn_tok[:NS, :], ident[:NS, s:s + 1], ALU.mult)
                                    rl_.append(rhs_s)
                                for i_ in range(8):
                                    po_ = small.next()
                                    MM(po_, k_tok[:NS, :], rl_[i_][:NS, :])
                                    pl_.append(po_)
                                for i_, s in enumerate(range(g0, g0 + 8)):
                                    STT(Ss[:, s, :], Ss[:, s, :], ebc[:, s:s + 1], pl_[i_], ALU.mult, ALU.add)
                            DMA("sp", o_delta_s[l, :, h].rearrange("s p v -> p s v"), Ss, sem="o_Ss")

                    ckpt('delta')
                    alias_barrier(PH2, PH3)
                    for c2 in range(4):
                        wco = ws.get(wv(w_conf_out[l, :, c2 * 256:(c2 + 1) * 256]), 4, 256)
                        wdo = ws.get(wv(w_delta_out[l, :, c2 * 256:(c2 + 1) * 256]), 4, 256)
                        wma = ws.get(wv(w_in[l, :, O_MA + c2 * 256:O_MA + (c2 + 1) * 256]), KD, 256)
                        wmb = ws.get(wv(w_in[l, :, O_MB + c2 * 256:O_MB + (c2 + 1) * 256]), KD, 256)
                        for c1 in range(2):
                            c = c2 * 2 + c1
                            w0 = c1 * 128
                            for b, (b0, bw) in enumerate(blocks):
                                pya = big.next()
                                pma = big.next()
                                for k in range(4):
                                    MM(pya[:, :bw], wco[:, k, w0:w0 + 128], caT[k][:, b0:b0 + bw], start=(k == 0), stop=(k == 3))
                                for k in range(KD):
                                    MM(pma[:, :bw], wma[:, k, w0:w0 + 128], hT[k][b], start=(k == 0), stop=(k == KD - 1))
                                sa = scr.next()
                                ACT(sa[:, :bw], pma[:, :bw], AF.Sigmoid)
                                TT(sa[:, :bw], pya[:, :bw], sa[:, :bw], ALU.mult)
                                pyb = big.next()
                                pmb = big.next()
                                for k in range(4):
                                    MM(pyb[:, :bw], wdo[:, k, w0:w0 + 128], ogT[k][:, b0:b0 + bw], start=(k == 0), stop=(k == 3))
                                for k in range(KD):
                                    MM(pmb[:, :bw], wmb[:, k, w0:w0 + 128], hT[k][b], start=(k == 0), stop=(k == KD - 1))
                                sb_ = scr.next()
                                ACT(sb_[:, :bw], pmb[:, :bw], AF.Sigmoid)
                                TT(sb_[:, :bw], pyb[:, :bw], sb_[:, :bw], ALU.mult)
                                TT(merged[c][:, b0:b0 + bw], sa[:, :bw], sb_[:, :bw], ALU.add)
                    ckpt('merge')
                    for c2 in range(4):
                        wmo = ws.get(wv(w_merge_out[l, :, c2 * 256:(c2 + 1) * 256]), KD, 256)
                        for c1 in range(2):
                            c = c2 * 2 + c1
                            for b, (b0, bw) in enumerate(blocks):
                                ps = big.next()
                                for k in range(KD):
                                    MM(ps[:, :bw], wmo[:, k, c1 * 128:(c1 + 1) * 128], merged[k][:, b0:b0 + bw], start=(k == 0), stop=(k == KD - 1))
                                resid_update(c, ps, bw, gbs[b], 16)

                    ckpt('mergeout')
                    rmsnorm_mod(blocks, gbs, 8, 24)
                    for fh in range(2):
                        alias_barrier(PH1 + PH2 + PH3 + PH4, PH4)
                        for jj2 in range(6):
                            j0 = fh * 11 + jj2 * 2
                            nj = min(2, fh * 11 + 11 - j0)
                            wg = ws.get(wv(w_ffn_in[l, :, j0 * 128:(j0 + nj) * 128]), KD, nj * 128)
                            wu = ws.get(wv(w_ffn_in[l, :, D_FF + j0 * 128:D_FF + (j0 + nj) * 128]), KD, nj * 128)
                            for j1 in range(nj):
                                jl = jj2 * 2 + j1
                                for b, (b0, bw) in enumerate(blocks):
                                    pgt = big.next()
                                    pup = big.next()
                                    for k in range(KD):
                                        MM(pgt[:, :bw], wg[:, k, j1 * 128:(j1 + 1) * 128], hT[k][b], start=(k == 0), stop=(k == KD - 1))
                                    for k in range(KD):
                                        MM(pup[:, :bw], wu[:, k, j1 * 128:(j1 + 1) * 128], hT[k][b], start=(k == 0), stop=(k == KD - 1))
                                    sg = scr.next()
                                    ACT(sg[:, :bw], pgt[:, :bw], AF.Silu)
                                    TT(actT[jl][:, b0:b0 + bw], sg[:, :bw], pup[:, :bw], ALU.mult)
                        for c in range(KD):
                            wo = ws.get(wv(w_ffn_out[l, fh * 1408:(fh + 1) * 1408, c * 128:(c + 1) * 128]), 11, 128)
                            for b, (b0, bw) in enumerate(blocks):
                                ps = big.next()
                                for j in range(11):
                                    MM(ps[:, :bw], wo[:, j, :], actT[j][:, b0:b0 + bw], start=(j == 0), stop=(j == 10))
                                resid_update(c, ps, bw, gbs[b], 40)

            for gb, (g0, gw) in enumerate(GB):
                ps = big.next()
                for k in range(KD):
                    sq = sqb_tile()
                    ACT(sq[:, :gw], xT[k][gb], AF.Square)
                    MM(ps[:, :gw], ones_b, sq[:, :gw], start=(k == 0), stop=(k == KD - 1))
                rs = st1
                ACT(rs[:, :gw], ps[:, :gw], AF.Ln, scale=1.0 / D, bias=eps6)
                ACT(rs[:, :gw], rs[:, :gw], AF.Exp, scale=-0.5)
                nt = max(1, gw // 128)
                tw = min(gw, 128)
                for t in range(nt):
                    o_ = ld[t % 2]
                    for k in range(KD):
                        tmp = sm.next()
                        STT(tmp[:, :tw], xT[k][gb][:, t * 128:t * 128 + tw], pvl[:, PV_FG + k:PV_FG + k + 1], rs[:, t * 128:t * 128 + tw],
                            ALU.mult, ALU.mult)
                        p = small.next()
                        TR(p[:tw, :], tmp[:, :tw], ident)
                        if k % 2 == 0:
                            ACT(o_[:tw, k * 128:(k + 1) * 128], p[:tw, :], AF.Copy)
                        else:
                            CP(o_[:tw, k * 128:(k + 1) * 128], p[:tw, :])
                    if gb < 4:
                        r0 = g0 + t * 128
                        DMA("sp", y_p[r0:r0 + 128, :], o_, sem="o_y%d" % (t % 2))
                    else:
                        DMA("sp", y_s[:, :], o_[:NS, :], sem="o_y%d" % (t % 2))

        def build_w(Pq):
            try:
                build(Pq)
            except StopBuild:
                pass
            for sname in sem_names:
                if (sname.startswith("o_") or sname.startswith("w") or sname.startswith("ld")) and Pq.cnt.get(sname, 0) > 0:
                    Pq.q["sp"].append(("wait", sname, Pq.cnt[sname]))
        P0 = Prog()
        build_w(P0)
        ws.recording = False
        P = Prog()
        build_w(P)
        print("ops:", P.nops, {k: len(v) for k, v in P.q.items()}, "weights:", len(ws.specs))

        semh = {n: es.enter_context(nc.semaphore(n)) for n in sem_names}
        block = es.enter_context(nc.Block())

        def run(eng, items):
            for it in items:
                if it[0] == "wait":
                    eng.wait_ge(semh[it[1]], it[2])
                else:
                    it[1](eng).then_inc(semh[it[2]], it[3])

        @block.tensor
        def _(eng):
            run(eng, P.q["pe"])

        @block.scalar
        def _(eng):
            run(eng, P.q["act"])

        @block.vector
        def _(eng):
            run(eng, P.q["dve"])

        @block.gpsimd
        def _(eng):
            run(eng, P.q["pool"])

        @block.sync
        def _(eng):
            run(eng, P.q["sp"])
    return nc


def make_consts():
    c = np.zeros((128, 5, 128), np.float32)
    i = np.arange(128)
    c[:, 0, :] = np.eye(128, dtype=np.float32)
    c[:, 1, :] = 1.0
    c[:, 2, :] = (i[:, None] <= i[None, :]).astype(np.float32)
    c[:, 3, :] = np.where(i[:, None] > i[None, :], 0.0, NEG).astype(np.float32)
    c[:, 4, :] = np.where(i[None, :] >= i[:, None], 0.0, NEG).astype(np.float32)
    return c


def make_bmask():
    i = np.arange(128)
    b64 = (i[:, None] // 64 == i[None, :] // 64)
    m = np.zeros((128, 3, 128), np.float32)
    m[:, 0, :] = b64
    m[:, 1, :] = ~b64
    m[:, 2, :] = 1.0
    return m


def make_pvec(inp):
    pv = np.zeros((DEPTH, 128, PV_N), np.float32)
    for l in range(DEPTH):
        pv[l, :, PV_G1:PV_G1 + 8] = inp["norm1_g"][l].reshape(8, 128).T
        pv[l, :, PV_G2:PV_G2 + 8] = inp["norm2_g"][l].reshape(8, 128).T
        pv[l, :, PV_BADA:PV_BADA + 48] = inp["b_ada"][l].reshape(48, 128).T
        cw = inp["conf_dw_w"][l]
        for j in range(4):
            pv[l, :, PV_CW + j * 31:PV_CW + (j + 1) * 31] = cw[:, j * 128:(j + 1) * 128].T
        pv[l, :, PV_CB:PV_CB + 4] = inp["conf_dw_b"][l].reshape(4, 128).T
        pv[l, :, PV_LNG:PV_LNG + 4] = inp["conf_ln_g"][l].reshape(4, 128).T
        pv[l, :, PV_LNB:PV_LNB + 4] = inp["conf_ln_b"][l].reshape(4, 128).T
        sw = inp["short_conv_w"][l]
        for g in range(12):
            pv[l, :, PV_SW + g * 4:PV_SW + (g + 1) * 4] = sw[:, g * 128:(g + 1) * 128].T
        pv[l, :, PV_DNG] = inp["delta_norm_g"][l]
        pv[l, :, PV_ALOG:PV_ALOG + 32] = np.tile(inp["a_log"][l], 8)[None, :]
        pv[l, :, PV_DTB:PV_DTB + 32] = np.tile(inp["dt_bias"][l], 8)[None, :]
        pv[l, :, PV_FG:PV_FG + 8] = inp["final_norm_g"].reshape(8, 128).T
    return pv


_NC_CACHE = {}


def kernel(**inp):
    inp = {k: np.asarray(v) for k, v in inp.items()}
    n = 8
    if "nc" not in _NC_CACHE:
        _NC_CACHE["nc"] = build_nc()
    nc = _NC_CACHE["nc"]
    consts = make_consts()
    bmask_np = make_bmask()
    pv = make_pvec(inp)
    shared = {k: np.ascontiguousarray(inp[k], dtype=np.float32) for k in
              ("w_ada", "w_in", "w_conf_out", "w_delta_out", "w_merge_out", "w_ffn_in", "w_ffn_out")}
    in_maps = []
    for i in range(n):
        s0 = i * NS
        m = dict(shared)
        m["x_p"] = np.ascontiguousarray(inp["x_prompt"][i])
        m["x_s"] = np.ascontiguousarray(inp["x_sample"][s0:s0 + NS, 0, :])
        m["c_in"] = np.ascontiguousarray(np.concatenate([inp["c_prompt"][i:i + 1], inp["c_sample"][s0:s0 + NS]], axis=0))
        m["st_conf"] = np.ascontiguousarray(inp["state_conformer_conv"][:, s0:s0 + NS])
        m["st_short"] = np.ascontiguousarray(inp["state_short_conv"][:, s0:s0 + NS])
        m["st_delta"] = np.ascontiguousarray(inp["state_delta"][:, s0:s0 + NS])
        m["pvec"] = pv
        m["consts"] = consts
        m["bmask"] = bmask_np
        in_maps.append(m)
    res = run_bass_kernel_spmd(nc, in_maps, core_ids=list(range(n)))
    R = res.results
    y_p = np.stack([R[i]["y_p"] for i in range(n)], axis=0)
    y_s = np.concatenate([R[i]["y_s"] for i in range(n)], axis=0)[:, None, :]
    conf_p = np.stack([R[i]["o_conf_p"] for i in range(n)], axis=1)
    conf_s = np.concatenate([R[i]["o_conf_s"] for i in range(n)], axis=1)
    short_p = np.stack([R[i]["o_short_p"] for i in range(n)], axis=1)
    short_s = np.concatenate([R[i]["o_short_s"] for i in range(n)], axis=1)
    delta_p = np.stack([R[i]["o_delta_p"] for i in range(n)], axis=1)
    delta_s = np.concatenate([R[i]["o_delta_s"] for i in range(n)], axis=1)
    return tuple(np.ascontiguousarray(a, dtype=np.float32) for a in
                 (y_p, y_s, conf_p, conf_s, short_p, short_s, delta_p, delta_s))
```
